# Optimizing a Trainium2 kernel written in Bass

```python
import jax, jax.numpy as jnp
from jax import lax
import numpy as np

D_MODEL = 4096
BATCH = 16
SEQ = 256
DEPTH = 2
DEC_BATCH = 4
DEC_SEQ = 1024
PAST_LEN = 512

GRID_W = 64
HEAD_DIM = 128
ATTN_HEADS = 16
ATTN_KV_HEADS = 4
ATTN_WIDTH = ATTN_HEADS * HEAD_DIM
KV_WIDTH = ATTN_KV_HEADS * HEAD_DIM
LRU_WIDTH = D_MODEL // 4
LRU_BLOCKS = 8
LRU_BLOCK = LRU_WIDTH // LRU_BLOCKS
LRU_CONV = 4
LRU_C = 8.0
RET_HEADS = 8
RET_DK = 128
RET_DV = 128
RET_WIDTH = RET_HEADS * RET_DV
RET_CHUNK = 128
Q_BLOCK = 128
FFN_DIM = 11008
ROPE_THETA = 10000.0
EPS = 1e-6
N_MOD = 9
MIX_WIDTH = ATTN_WIDTH + LRU_WIDTH + RET_WIDTH
IN_WIDTH = ATTN_WIDTH + 2 * KV_WIDTH + 2 * LRU_WIDTH + 4 * RET_WIDTH
IN_SPLITS = [ATTN_WIDTH,
             ATTN_WIDTH + KV_WIDTH,
             ATTN_WIDTH + 2 * KV_WIDTH,
             ATTN_WIDTH + 2 * KV_WIDTH + LRU_WIDTH,
             ATTN_WIDTH + 2 * KV_WIDTH + 2 * LRU_WIDTH,
             ATTN_WIDTH + 2 * KV_WIDTH + 2 * LRU_WIDTH + RET_WIDTH,
             ATTN_WIDTH + 2 * KV_WIDTH + 2 * LRU_WIDTH + 2 * RET_WIDTH,
             ATTN_WIDTH + 2 * KV_WIDTH + 2 * LRU_WIDTH + 3 * RET_WIDTH]

kernel_name = 'hybrid_flow_gqa_rglru_retention_step'


def rms_norm(x, g):
    xf = x.astype(jnp.float32)
    y = xf * lax.rsqrt(jnp.mean(xf * xf, axis=-1, keepdims=True) + EPS)
    return (y * g.astype(jnp.float32)).astype(x.dtype)


def axial_rope(x):
    t = x.shape[1]
    rows = t // GRID_W
    row = jnp.repeat(jnp.arange(rows), GRID_W)
    col = jnp.tile(jnp.arange(GRID_W), rows)
    half = HEAD_DIM // 2
    quarter = half // 2
    inv = ROPE_THETA ** (-jnp.arange(quarter, dtype=jnp.float32) / quarter)

    def rot(xa, pos):
        ang = pos.astype(jnp.float32)[:, None] * inv[None, :]
        cos = jnp.cos(ang)[None, :, None, :]
        sin = jnp.sin(ang)[None, :, None, :]
        x1, x2 = xa[..., :quarter], xa[..., quarter:]
        return jnp.concatenate([x1 * cos - x2 * sin, x2 * cos + x1 * sin], axis=-1)

    xf = x.astype(jnp.float32)
    out = jnp.concatenate([rot(xf[..., :half], row), rot(xf[..., half:], col)], axis=-1)
    return out.astype(x.dtype)


def block_attention(q, k, v):
    b, tq = q.shape[0], q.shape[1]
    nb = tq // Q_BLOCK
    groups = ATTN_HEADS // ATTN_KV_HEADS
    qb = q.reshape(b, nb, Q_BLOCK, ATTN_KV_HEADS, groups, HEAD_DIM).transpose(1, 0, 2, 3, 4, 5)
    scale = HEAD_DIM ** -0.5

    def one_block(qblk):
        s = jnp.einsum('bqkgd,bskd->bkgqs', qblk, k, preferred_element_type=jnp.float32) * scale
        p = jax.nn.softmax(s, axis=-1).astype(v.dtype)
        return jnp.einsum('bkgqs,bskd->bqkgd', p, v)

    o = lax.map(one_block, qb)
    return o.transpose(1, 0, 2, 3, 4, 5).reshape(b, tq, ATTN_WIDTH)


def centred_conv(x, w, bias):
    t = x.shape[1]
    left = LRU_CONV // 2
    right = LRU_CONV - 1 - left
    xp = jnp.pad(x, ((0, 0), (left, right), (0, 0)))
    out = bias
    for j in range(LRU_CONV):
        out = out + xp[:, j:j + t] * w[j]
    return out


def rglru_dir(xc, wa, ba, wi, bi, lam, h0, reverse):
    b, t, _ = xc.shape
    xr = xc.reshape(b, t, LRU_BLOCKS, LRU_BLOCK)
    r = jax.nn.sigmoid((jnp.einsum('btnk,nkj->btnj', xr, wa).reshape(b, t, LRU_WIDTH) + ba).astype(jnp.float32))
    i = jax.nn.sigmoid((jnp.einsum('btnk,nkj->btnj', xr, wi).reshape(b, t, LRU_WIDTH) + bi).astype(jnp.float32))
    log_a = -LRU_C * r * jax.nn.softplus(-lam.astype(jnp.float32))
    a = jnp.exp(log_a)
    u = jnp.sqrt(-jnp.expm1(2.0 * log_a)) * i * xc.astype(jnp.float32)
    edge = -1 if reverse else 0
    u = u.at[:, edge].add(a[:, edge] * h0.astype(jnp.float32))

    def combine(left, right):
        a1, b1 = left
        a2, b2 = right
        return a1 * a2, a2 * b1 + b2

    _, h = lax.associative_scan(combine, (a, u), reverse=reverse, axis=1)
    h_last = h[:, 0] if reverse else h[:, -1]
    return h, h_last


def retention_dir(q, k, v, log_g, s0):
    b, t = q.shape[0], q.shape[1]
    n = t // RET_CHUNK
    idx = jnp.arange(RET_CHUNK, dtype=jnp.float32)
    diff = idx[:, None] - idx[None, :]
    lower = diff >= 0
    dmask = jnp.where(lower[None], jnp.exp(jnp.where(lower, diff, 0.0)[None] * log_g[:, None, None]), 0.0)
    q_dec = jnp.exp((idx[:, None] + 1.0) * log_g[None, :])
    k_dec = jnp.exp((RET_CHUNK - 1.0 - idx)[:, None] * log_g[None, :])
    c_dec = jnp.exp(RET_CHUNK * log_g)

    def chunks(z):
        return z.reshape(b, n, RET_CHUNK, z.shape[2], z.shape[3]).swapaxes(0, 1)

    def step(s, qkv):
        qc, kc, vc = qkv
        inner = jnp.einsum('bihd,bjhd->bhij', qc, kc) * dmask[None]
        o = (jnp.einsum('bhij,bjhe->bihe', inner, vc)
             + jnp.einsum('bihd,bhde->bihe', qc, s) * q_dec[None, :, :, None])
        s = s * c_dec[None, :, None, None] + jnp.einsum('bjhd,bjhe->bhde', kc * k_dec[None, :, :, None], vc)
        return s, o

    s_fin, o = lax.scan(step, s0.astype(jnp.float32), (chunks(q), chunks(k), chunks(v)))
    return o.swapaxes(0, 1).reshape(b, t, RET_HEADS, RET_DV), s_fin


def swiglu(h, wg, wu, wd):
    return (jax.nn.silu(h @ wg) * (h @ wu)) @ wd


def token_mixer(h, p, ctx):
    b, t = h.shape[0], h.shape[1]
    f32 = jnp.float32
    u = h @ p['w_in']
    q, k, v, xb, yb, rq, rk, rv, rg = jnp.split(u, IN_SPLITS, axis=-1)
    q = rms_norm(q.reshape(b, t, ATTN_HEADS, HEAD_DIM), p['q_gain'])
    k = rms_norm(k.reshape(b, t, ATTN_KV_HEADS, HEAD_DIM), p['k_gain'])
    v = v.reshape(b, t, ATTN_KV_HEADS, HEAD_DIM)
    if ctx is None:
        attn = block_attention(q, k, v)
        lru_h0 = jnp.zeros((b, 2, LRU_WIDTH), f32)
        ret_s0 = jnp.zeros((b, 2, RET_HEADS, RET_DK, RET_DV), f32)
    else:
        ck, cv, lru_h0, ret_s0 = ctx
        keys = jnp.concatenate([axial_rope(k), ck.astype(k.dtype)], axis=1)
        vals = jnp.concatenate([v, cv.astype(v.dtype)], axis=1)
        attn = block_attention(axial_rope(q), keys, vals)
    xc = centred_conv(xb, p['conv_w'], p['conv_b'])
    hf, lf = rglru_dir(xc, p['lru_wa'][0], p['lru_ba'][0], p['lru_wi'][0], p['lru_bi'][0], p['lru_lambda'][0], lru_h0[:, 0], False)
    hb, lb = rglru_dir(xc, p['lru_wa'][1], p['lru_ba'][1], p['lru_wi'][1], p['lru_bi'][1], p['lru_lambda'][1], lru_h0[:, 1], True)
    lru = ((hf + hb) * jax.nn.gelu(yb.astype(f32))).astype(h.dtype)
    rq = rq.reshape(b, t, RET_HEADS, RET_DK).astype(f32)
    rk = rk.reshape(b, t, RET_HEADS, RET_DK).astype(f32) * (RET_DK ** -0.5)
    rv = rv.reshape(b, t, RET_HEADS, RET_DV).astype(f32)
    log_g = jax.nn.log_sigmoid(p['ret_logit'].astype(f32))
    of, sf = retention_dir(rq, rk, rv, log_g[0], ret_s0[:, 0])
    ob, sb = retention_dir(rq[:, ::-1], rk[:, ::-1], rv[:, ::-1], log_g[1], ret_s0[:, 1])
    ret = rms_norm(of + ob[:, ::-1], p['ret_g'].reshape(RET_HEADS, RET_DV)).reshape(b, t, RET_WIDTH)
    ret = (jax.nn.silu(rg.astype(f32)) * ret).astype(h.dtype)
    out = jnp.concatenate([attn, lru, ret], axis=-1) @ p['w_out']
    if ctx is None:
        state = (k, v, jnp.stack([lf, lb], axis=1).astype(h.dtype), jnp.stack([sf, sb], axis=1).astype(h.dtype))
    else:
        state = None
    return out, state


def trunk_layer(x, cond, p, ctx):
    mod = (jax.nn.silu(cond) @ p['w_mod'] + p['b_mod']).reshape(cond.shape[0], 1, N_MOD, D_MODEL)

    def modulated(z, i):
        return rms_norm(z, p['norm_g'][i]) * (1 + mod[:, :, 3 * i + 1]) + mod[:, :, 3 * i]

    x = x + 0.5 * mod[:, :, 2] * swiglu(modulated(x, 0), p['ffn_wg'][0], p['ffn_wu'][0], p['ffn_wd'][0])
    mix, state = token_mixer(modulated(x, 1), p, ctx)
    x = x + mod[:, :, 5] * mix
    x = x + 0.5 * mod[:, :, 8] * swiglu(modulated(x, 2), p['ffn_wg'][1], p['ffn_wu'][1], p['ffn_wd'][1])
    return x, state


def setup_inputs(seed: int = 0) -> dict:
    key = jax.random.key(seed)
    ks = jax.random.split(key, 32)
    nrm = jax.random.normal
    D = D_MODEL
    gam = 1.0 - 2.0 ** (-5.0 - jnp.arange(RET_HEADS, dtype=jnp.float32))
    ret_base = jnp.log(gam) - jnp.log1p(-gam)
    u = jax.random.uniform(ks[0], (DEPTH, 2, LRU_WIDTH), minval=0.9, maxval=0.999)
    a = u ** (1.0 / LRU_C)
    return {
        'x_prompt': nrm(ks[1], (BATCH, SEQ, D)),
        'x_sample': nrm(ks[2], (DEC_BATCH, DEC_SEQ, D)),
        'cache_k': nrm(ks[3], (DEC_BATCH, DEPTH, PAST_LEN, ATTN_KV_HEADS, HEAD_DIM)),
        'cache_v': nrm(ks[4], (DEC_BATCH, DEPTH, PAST_LEN, ATTN_KV_HEADS, HEAD_DIM)),
        'state_lru': nrm(ks[5], (DEC_BATCH, DEPTH, 2, LRU_WIDTH)),
        'state_ret': 0.5 * nrm(ks[6], (DEC_BATCH, DEPTH, 2, RET_HEADS, RET_DK, RET_DV)),
        'c': nrm(ks[7], (DEC_BATCH, D)),
        'c_ctx': nrm(ks[8], (D,)),
        'norm_g': 1.0 + 0.02 * nrm(ks[9], (DEPTH, 3, D)),
        'w_mod': nrm(ks[10], (DEPTH, D, N_MOD * D)) * D ** -0.5,
        'b_mod': 0.02 * nrm(ks[11], (DEPTH, N_MOD * D)),
        'ffn_wg': nrm(ks[12], (DEPTH, 2, D, FFN_DIM)) * D ** -0.5,
        'ffn_wu': nrm(ks[13], (DEPTH, 2, D, FFN_DIM)) * D ** -0.5,
        'ffn_wd': nrm(ks[14], (DEPTH, 2, FFN_DIM, D)) * FFN_DIM ** -0.5,
        'w_in': nrm(ks[15], (DEPTH, D, IN_WIDTH)) * D ** -0.5,
        'q_gain': 1.0 + 0.02 * nrm(ks[16], (DEPTH, HEAD_DIM)),
        'k_gain': 1.0 + 0.02 * nrm(ks[17], (DEPTH, HEAD_DIM)),
        'lru_conv_w': nrm(ks[18], (DEPTH, LRU_CONV, LRU_WIDTH)) * LRU_CONV ** -0.5,
        'lru_conv_b': 0.02 * nrm(ks[19], (DEPTH, LRU_WIDTH)),
        'lru_wa': nrm(ks[20], (DEPTH, 2, LRU_BLOCKS, LRU_BLOCK, LRU_BLOCK)) * LRU_BLOCK ** -0.5,
        'lru_ba': 0.02 * nrm(ks[21], (DEPTH, 2, LRU_WIDTH)),
        'lru_wi': nrm(ks[22], (DEPTH, 2, LRU_BLOCKS, LRU_BLOCK, LRU_BLOCK)) * LRU_BLOCK ** -0.5,
        'lru_bi': 0.02 * nrm(ks[23], (DEPTH, 2, LRU_WIDTH)),
        'lru_lambda': jnp.log(a) - jnp.log1p(-a),
        'ret_logit': ret_base + 0.02 * nrm(ks[24], (DEPTH, 2, RET_HEADS)),
        'ret_g': 1.0 + 0.02 * nrm(ks[25], (DEPTH, RET_WIDTH)),
        'w_out': nrm(ks[26], (DEPTH, MIX_WIDTH, D)) * MIX_WIDTH ** -0.5,
        'final_g': 1.0 + 0.02 * nrm(ks[27], (D,)),
    }


def reference(x_prompt, x_sample, cache_k, cache_v, state_lru, state_ret, c, c_ctx,
              norm_g, w_mod, b_mod, ffn_wg, ffn_wu, ffn_wd, w_in, q_gain, k_gain,
              lru_conv_w, lru_conv_b, lru_wa, lru_ba, lru_wi, lru_bi, lru_lambda,
              ret_logit, ret_g, w_out, final_g):
    ctx_cond = c_ctx[None, :]
    yp = x_prompt
    ys = x_sample
    new_k, new_v, new_lru, new_ret = [], [], [], []
    for l in range(DEPTH):
        p = {'norm_g': norm_g[l], 'w_mod': w_mod[l], 'b_mod': b_mod[l],
             'ffn_wg': ffn_wg[l], 'ffn_wu': ffn_wu[l], 'ffn_wd': ffn_wd[l],
             'w_in': w_in[l], 'q_gain': q_gain[l], 'k_gain': k_gain[l],
             'conv_w': lru_conv_w[l], 'conv_b': lru_conv_b[l],
             'lru_wa': lru_wa[l], 'lru_ba': lru_ba[l], 'lru_wi': lru_wi[l], 'lru_bi': lru_bi[l],
             'lru_lambda': lru_lambda[l], 'ret_logit': ret_logit[l], 'ret_g': ret_g[l],
             'w_out': w_out[l]}
        yp, st = trunk_layer(yp, ctx_cond, p, None)
        new_k.append(st[0])
        new_v.append(st[1])
        new_lru.append(st[2])
        new_ret.append(st[3])
        ys, _ = trunk_layer(ys, c, p, (cache_k[:, l], cache_v[:, l], state_lru[:, l], state_ret[:, l]))
    y_prompt = rms_norm(yp, final_g)
    y_sample = rms_norm(ys, final_g)
    return (y_prompt, y_sample, jnp.stack(new_k, axis=1), jnp.stack(new_v, axis=1),
            jnp.stack(new_lru, axis=1), jnp.stack(new_ret, axis=1))
```

```python
import numpy as np
from contextlib import ExitStack
import concourse.bass as bass
import concourse.mybir as mybir
from concourse.bass_utils import run_bass_kernel_spmd

F32 = mybir.dt.float32
BF16 = mybir.dt.bfloat16
AF = mybir.ActivationFunctionType
ALU = mybir.AluOpType
AX = mybir.AxisListType

EPS = 1e-6
NEG = -30000.0


class Cfg:
    def __init__(self, D=4096, F=11008, H=16, KVH=4, NB=8, RH=8, NT=1024, PAST=512, DEPTH=2,
                 GRID_W=64, NPC=4, NSC=4, FG=4, stop=0, CC=False):
        self.stop = stop
        self.CC = CC
        self.NCORES = NPC + NSC
        self.NCOND = 1 + NSC
        self.CS = 9 * D // (NPC + NSC)
        self.D, self.F, self.H, self.KVH, self.NB, self.RH = D, F, H, KVH, NB, RH
        self.NT, self.PAST, self.DEPTH, self.GRID_W = NT, PAST, DEPTH, GRID_W
        self.NPC, self.NSC, self.FG = NPC, NSC, FG
        self.DC = D // 128
        self.AW, self.KVW, self.LW, self.RW = H * 128, KVH * 128, NB * 128, RH * 128
        self.INW = self.AW + 2 * self.KVW + 2 * self.LW + 4 * self.RW
        self.FC = F // 128
        self.NSEG = NT // 256
        self.NCH = NT // 128
        self.NH = NT // 512
        self.PCH = PAST // 128
        self.G = H // KVH
        self.oQ = 0
        self.oK = self.AW
        self.oV = self.oK + self.KVW
        self.oXB = self.oV + self.KVW
        self.oYB = self.oXB + self.LW
        self.oRQ = self.oYB + self.LW
        self.oRK = self.oRQ + self.RW
        self.oRV = self.oRK + self.RW
        self.oRG = self.oRV + self.RW
        assert self.AW + self.LW + self.RW == D


class StopBuild(Exception):
    pass


class Buf:
    __slots__ = ("w", "r", "name", "persistent")

    def __init__(self, name, persistent=False):
        self.w = None
        self.r = {}
        self.name = name
        self.persistent = persistent


class Sched:
    ENG = ("pe", "act", "dve", "pool", "sp")

    def __init__(self, nc, es):
        self.nc, self.es = nc, es
        self.prog = {e: [] for e in self.ENG}
        self.S = {e: es.enter_context(nc.semaphore("S_" + e)) for e in ("pe", "act", "dve", "pool")}
        self.cnt = {e: 0 for e in self.S}
        self.waited = {e: {} for e in self.ENG}
        self.semmap = {}
        self.dcnt = {}
        self.dsem = {}
        self.bar = []
        self.ninst = 0
        self.dead = False

    def _sem(self, key):
        if key not in self.semmap:
            s = self.es.enter_context(self.nc.semaphore("D%d" % len(self.semmap)))
            self.semmap[key] = s
            self.dcnt[id(s)] = 0
            self.dsem[id(s)] = s
        return self.semmap[key]

    def _deps(self, rd, wr):
        toks = []
        for b in rd:
            if b.w is not None:
                toks.append(b.w)
        for b in wr:
            if b.w is not None:
                toks.append(b.w)
            toks.extend(b.r.values())
        return toks

    def _wait(self, e, toks):
        p, wd = self.prog[e], self.waited[e]
        for (sem, val) in toks:
            if e == "pe" and sem is self.S["pe"]:
                continue
            k = id(sem)
            if wd.get(k, 0) < val:
                wd[k] = val
                p.append(lambda eng, sem=sem, val=val: eng.wait_ge(sem, val))
                self.ninst += 1

    def _mark(self, tok, rd, wr):
        k = id(tok[0])
        for b in rd:
            if k not in b.r or b.r[k][1] < tok[1]:
                b.r[k] = tok
        for b in wr:
            b.w = tok
            b.r = {}

    def op(self, e, fn, rd=(), wr=(), sig=True):
        if self.dead:
            return
        self._wait(e, self._deps(rd, wr))
        self.ninst += 1
        if sig:
            self.cnt[e] += 1
            sem = self.S[e]
            self.prog[e].append(lambda eng, fn=fn, sem=sem: fn(eng).then_inc(sem, 1))
            tok = (sem, self.cnt[e])
        else:
            self.prog[e].append(lambda eng, fn=fn: fn(eng))
            tok = (self.S[e], self.cnt[e] + 1)
        self._mark(tok, rd, wr)

    def dma(self, q, out, in_, rd=(), wr=(), key=None, fn=None):
        if self.dead:
            return
        sem = self._sem(key)
        k = id(sem)
        toks = self._deps(rd, wr)
        if self.dcnt[k] > 0:
            toks.append((sem, self.dcnt[k]))
        if q == "pool" and not all(b.persistent for b in list(rd) + list(wr)):
            toks = toks + self.bar
        self._wait(q, toks)
        self.dcnt[k] += 16
        if fn is not None:
            self.prog[q].append(lambda eng, fn=fn, sem=sem: fn(eng).then_inc(sem, 16))
        else:
            self.prog[q].append(lambda eng, out=out, in_=in_, sem=sem: eng.dma_start(out=out, in_=in_).then_inc(sem, 16))
        self.ninst += 1
        self._mark((sem, self.dcnt[k]), rd, wr)

    def barrier(self, engines=("pe", "act", "dve", "sp")):
        if self.dead:
            return
        toks = [(self.S[e], self.cnt[e]) for e in self.S if self.cnt[e] > 0]
        toks += [(self.dsem[k], v) for k, v in self.dcnt.items() if v > 0]
        self.bar = toks
        for e in engines:
            self._wait(e, toks)

    def final_wait(self):
        toks = [(self.dsem[k], v) for k, v in self.dcnt.items() if v > 0]
        toks += [(self.S[e], self.cnt[e]) for e in self.S if self.cnt[e] > 0]
        self._wait("sp", toks)


class TB:
    __slots__ = ("ap", "b")

    def __init__(self, ap, b):
        self.ap, self.b = ap, b


class Arena:
    def __init__(self, tensor, n, tag):
        self.t, self.n, self.tag, self.off = tensor, n, tag, 0
        self.live = []

    def reset(self):
        self.off = 0
        self.live = []

    def alloc(self, shape, name):
        size = int(np.prod(shape))
        size_al = (size + 15) // 16 * 16
        assert self.off + size_al <= self.n, "arena %s overflow: need %d have %d (%s)" % (self.tag, self.off + size_al, self.n, name)
        st, en = self.off, self.off + size_al
        self.off = en
        tb = None
        for (s0, e0, tb0, shp0) in self.live:
            if s0 == st and e0 == en and tb0.b.name == name and shp0 == tuple(shape):
                tb = tb0
        if tb is None:
            ap = self.t[:, st:st + size]
            if len(shape) == 2:
                ap = ap.rearrange("p (a b) -> p a b", a=shape[0])
            elif len(shape) == 3:
                ap = ap.rearrange("p (a b c) -> p a b c", a=shape[0], b=shape[1])
            tb = TB(ap, Buf(name))
            self.live.append((st, en, tb, tuple(shape)))
        nb = tb.b
        for (s0, e0, tb0, shp0) in self.live:
            if tb0 is not tb and s0 < en and st < e0:
                ob = tb0.b
                toks = list(ob.r.values()) + ([ob.w] if ob.w is not None else [])
                for tok in toks:
                    k = id(tok[0])
                    if k not in nb.r or nb.r[k][1] < tok[1]:
                        nb.r[k] = tok
        return tb


def build_nc(cfg):
    c = cfg
    D, DC, NT, NCH, NH, NSEG, PAST, PCH = c.D, c.DC, c.NT, c.NCH, c.NH, c.NSEG, c.PAST, c.PCH
    DEPTH, F, FC, INW, KVW, LW, RW, NB, RH, G = c.DEPTH, c.F, c.FC, c.INW, c.KVW, c.LW, c.RW, c.NB, c.RH, c.G
    NKEY = NT + PAST
    KCH = NCH + PCH
    nc = bass.Bass("TRN2", target_bir_lowering=False, num_devices=c.NPC + c.NSC)

    def din(name, shape):
        return nc.dram_tensor(name, list(shape), F32, kind="ExternalInput").ap()

    def dout(name, shape):
        return nc.dram_tensor(name, list(shape), F32, kind="ExternalOutput").ap()

    x_in = din("x", [NT, D])
    cond_in = din("cond", [1, D]) if not c.CC else None
    ck_in = din("ck", [DEPTH, PAST, KVW])
    cv_in = din("cv", [DEPTH, PAST, KVW])
    slru_in = din("slru", [DEPTH, 2 * NB, 128])
    sret_in = din("sret", [DEPTH, 2, RH, 128, 128])
    flag_in = din("flag", [128, 2])
    mb_in = din("mb", [128, (NCH + PCH) * NSEG])
    ropc_in = din("ropc", [NT, 128])
    rops_in = din("rops", [NT, 128])
    cst_in = din("cst", [128, 7 * 128 + 4])
    norm_g = din("norm_g", [DEPTH, 3 * DC, 128])
    NCORES, NCOND, CS = c.NCORES, c.NCOND, c.CS
    NRG = NCORES * NCOND
    if c.CC:
        assert NRG <= 128
        w_mod = din("w_mod", [DEPTH, D, CS])
        condall_in = din("condall", [NCOND, D])
        selj_in = din("selj", [NRG, NCORES])
        ag_in = nc.dram_tensor("ag_in", [NCOND, DEPTH * CS], F32, kind="Internal").ap()
        ag_out = nc.dram_tensor("ag_out", [NRG, DEPTH * CS], F32, kind="Internal").ap()
        agin_b, agout_b = Buf("agin", True), Buf("agout", True)
    else:
        w_mod = din("w_mod", [DEPTH, D, 9 * D])
    b_mod = din("b_mod", [DEPTH, 9 * DC, 128])
    ffn_wg = din("ffn_wg", [DEPTH, 2, D, F])
    ffn_wu = din("ffn_wu", [DEPTH, 2, D, F])
    ffn_wd = din("ffn_wd", [DEPTH, 2, F, D])
    w_in = din("w_in", [DEPTH, D, INW])
    q_gain = din("q_gain", [DEPTH, 128])
    k_gain = din("k_gain", [DEPTH, 128])
    conv_w = din("conv_w", [DEPTH, 4 * NB, 128])
    conv_b = din("conv_b", [DEPTH, NB, 128])
    lru_wa = din("lru_wa", [DEPTH, 2, NB, 128, 128])
    lru_ba = din("lru_ba", [DEPTH, 2 * NB, 128])
    lru_wi = din("lru_wi", [DEPTH, 2, NB, 128, 128])
    lru_bi = din("lru_bi", [DEPTH, 2 * NB, 128])
    lru_lam = din("lru_lam", [DEPTH, 2 * NB, 128])
    ret_logit = din("ret_logit", [DEPTH, 2 * RH])
    ret_g = din("ret_g", [DEPTH, RH, 128])
    w_out = din("w_out", [DEPTH, D, D])
    final_g = din("final_g", [DC, 128])

    y_out = dout("y", [NT, D])
    nk_out = dout("nk", [DEPTH, NT, KVW])
    nv_out = dout("nv", [DEPTH, NT, KVW])
    nlru_out = dout("nlru", [DEPTH, NSEG * 2 * NB, 128])
    nret_out = dout("nret", [DEPTH, NSEG, 2, RH, 128, 128])

    xT_s = nc.dram_tensor("xT_s", [DC, 128, NT], F32, kind="Internal").ap()
    a_s = nc.dram_tensor("a_s", [FC, 128, NT], BF16, kind="Internal").ap()
    mt_s = nc.dram_tensor("mt_s", [DC, 128, NT], BF16, kind="Internal").ap()
    modr_s = nc.dram_tensor("modr_s", [DEPTH, 9 * DC, 128], F32, kind="Internal").ap()
    xT_b = [Buf("xTs%d" % i, True) for i in range(DC)]
    a_b = [Buf("as%d" % i, True) for i in range(FC)]
    mt_b = [Buf("mts%d" % i, True) for i in range(DC)]
    modr_b = Buf("modrs", True)

    es = ExitStack()
    sch = Sched(nc, es)

    def sb(name, shape, dt):
        return es.enter_context(nc.sbuf_tensor(name, list(shape), dt))

    R1 = sb("R1", [128, DC * NT // 2], F32)
    HT = R1[:].bitcast(BF16).rearrange("p (c t) -> p c t", c=DC)
    ACC = R1[:].rearrange("p (c t) -> p c t", c=DC // 2)
    HT_b = Buf("HT", True)
    ACC_b = [Buf("ACC%d" % i, True) for i in range(DC // 2)]
    NSLOT = 4
    WPE = DC * 256
    R2 = sb("R2", [128, NSLOT * WPE], BF16)
    WS_b = [Buf("WS%d" % i, True) for i in range(NSLOT)]

    def wslot(i):
        return R2[:, i * WPE:(i + 1) * WPE]

    ARF_N = 8192
    ARB_N = 14336
    arf = Arena(sb("ARF", [128, ARF_N], F32), ARF_N, "f")
    arb = Arena(sb("ARB", [128, ARB_N], BF16), ARB_N, "b")
    PS = es.enter_context(nc.psum_tensor("PS", [128, 8, 512], F32))
    PS_b = [Buf("PS%d" % i, True) for i in range(8)]

    CST = sb("CST", [128, 7 * 128 + 4], F32)
    CST_b = Buf("CST", True)
    IDN = CST[:, 0:128]
    DPOS, DNEG, LOWM, UPM = (CST[:, 128 * i:128 * (i + 1)] for i in range(1, 5))
    IROW1, IROW2 = CST[:, 640:768], CST[:, 768:896]
    IVEC = CST[:, 896:900]
    ONES = sb("ONES", [128, 128], F32)
    ONESB = sb("ONESB", [128, 128], BF16)
    SMALL = sb("SMALL", [128, 16], F32)
    EPSC, ONEC = SMALL[:, 0:1], SMALL[:, 1:2]
    FLAG = SMALL[:, 2:3]
    CONST_b = Buf("CONST", True)
    MODT = sb("MODT", [128, DEPTH, 9 * DC], F32)
    NGT = sb("NGT", [128, DEPTH, 3 * DC], F32)
    GS = sb("GS", [128, DEPTH, 3 * DC], F32)
    GATE = sb("GATE", [128, DEPTH, 3 * DC], F32)
    FGT = sb("FGT", [128, DC], F32)
    ZSH = sb("ZSH", [128, DC], F32)
    MOD_b = Buf("MOD", True)
    ST = sb("ST", [128, DC, c.NCOND if c.CC else 1], BF16)
    SELJ = sb("SELJ", [128, c.NCORES], F32)
    MB = sb("MB", [128, NCH + PCH, NSEG], F32)
    TAB_b = Buf("TAB", True)
    NLV = 4 * NB + NB + 2 * NB * 4 + RH
    LV = sb("LV", [128, NLV], F32)
    LV_b = Buf("LV", True)
    oCW, oCB = 0, 4 * NB
    oBA, oBI, oLAM, oSL, oRG_ = 5 * NB, 7 * NB, 9 * NB, 11 * NB, 13 * NB
    SC8 = sb("SC8", [128, 2 * NB], F32)
    GAINQ = sb("GAINQ", [128, 128], F32)
    GAINK = sb("GAINK", [128, 128], F32)
    LG = sb("LG", [128, 2 * RH], F32)
    KDEC = sb("KDEC", [128, RH, 2], F32)
    CDEC = sb("CDEC", [128, RH, 2], F32)
    LST = sb("LST", [128, NSEG * 2 * NB], F32)
    LST_b = Buf("LST", True)
    RET_b = Buf("RETTAB", True)

    def mm(out, lhsT, rhs, start, stop, rd, wr):
        sch.op("pe", lambda e: e.matmul(out, lhsT, rhs, start=start, stop=stop), rd, wr, sig=stop)

    def tr(out, in_, idn, rd, wr):
        sch.op("pe", lambda e: e.transpose(out, in_, idn), rd, wr)

    def act(out, in_, func, rd, wr, bias=None, scale=None):
        kw = {}
        if bias is not None:
            kw["bias"] = bias
        if scale is not None:
            kw["scale"] = scale
        sch.op("act", lambda e: e.activation(out=out, in_=in_, func=func, **kw), rd, wr)

    def tt(out, in0, in1, op, rd, wr, eng="dve"):
        sch.op(eng, lambda e: e.tensor_tensor(out=out, in0=in0, in1=in1, op=op), rd, wr)

    def ts(out, in0, s1, op0, rd, wr, s2=None, op1=None, eng="dve"):
        if op1 is None:
            sch.op(eng, lambda e: e.tensor_scalar(out=out, in0=in0, scalar1=s1, scalar2=None, op0=op0), rd, wr)
        else:
            sch.op(eng, lambda e: e.tensor_scalar(out=out, in0=in0, scalar1=s1, scalar2=s2, op0=op0, op1=op1), rd, wr)

    def stt(out, in0, scalar, in1, op0, op1, rd, wr, eng="dve"):
        sch.op(eng, lambda e: e.scalar_tensor_tensor(out=out, in0=in0, scalar=scalar, in1=in1, op0=op0, op1=op1), rd, wr)

    def cpv(out, in_, rd, wr):
        sch.op("dve", lambda e: e.tensor_copy(out=out, in_=in_), rd, wr)

    def cpa(out, in_, rd, wr):
        act(out, in_, AF.Identity, rd, wr)

    def recip(out, in_, rd, wr):
        sch.op("dve", lambda e: e.reciprocal(out=out, in_=in_), rd, wr)

    def mset(ap, val, wr, eng="dve"):
        sch.op(eng, lambda e: e.memset(ap, val), (), wr)

    def ld(out, in_, rd, wr, key):
        sch.dma("sp", out, in_, rd, wr, key)

    def ldc(out, in_, rd, wr, key):
        sch.dma("pool", out, in_, rd, wr, key)

    ckpt = [0]

    def checkpoint(name=""):
        ckpt[0] += 1
        if c.stop:
            print("ckpt", ckpt[0], name)
        if c.stop and ckpt[0] >= c.stop:
            sch.dead = True

    def phase():
        checkpoint()
        sch.barrier()
        arf.reset()
        arb.reset()

    def bank(i, lo=0, hi=512):
        return PS[:, i, lo:hi]

    ld(CST[:], cst_in, (), [CST_b], "cst")
    mset(ONES[:], 1.0, [CONST_b])
    mset(ONESB[:], 1.0, [CONST_b])
    mset(SMALL[:, 0:1], EPS, [CONST_b])
    mset(SMALL[:, 1:2], 1.0, [CONST_b])
    mset(ZSH[:], 0.0, [CONST_b])
    ld(SMALL[:, 2:4], flag_in, (), [CONST_b], "flag")
    ld(MB[:].rearrange("p k s -> p (k s)"), mb_in, (), [TAB_b], "mb")

    def rows_to_cols(dst_ap, rows_dram, R, rd_b, wr_b, add_dram=None, add_b=None, psb=7):
        t = arf.alloc([128], "rtc_a")
        ld(t.ap[0:R, :], rows_dram, rd_b, [t.b], "rtc_a")
        if add_dram is not None:
            t2 = arf.alloc([128], "rtc_b")
            ld(t2.ap[0:R, :], add_dram, add_b, [t2.b], "rtc_b")
            tt(t.ap[0:R, :], t.ap[0:R, :], t2.ap[0:R, :], ALU.add, [t.b, t2.b], [t.b])
        tr(bank(psb, 0, R), t.ap[0:R, :], IDN[0:R, 0:R], [t.b, CST_b], [PS_b[psb]])
        cpv(dst_ap, bank(psb, 0, R), [PS_b[psb]], wr_b)

    if c.CC:
        phase()
        for r in range(NCOND):
            cr = arf.alloc([128], "cr%d" % r)
            ld(cr.ap[0:DC, :], condall_in[r:r + 1, :].rearrange("o (c p) -> (o c) p", p=128), (), [cr.b], "cr%d" % (r % 2))
            act(cr.ap[0:DC, :], cr.ap[0:DC, :], AF.Silu, [cr.b], [cr.b])
            tr(bank(6 + r % 2, 0, DC), cr.ap[0:DC, :], IDN[0:DC, 0:DC], [cr.b, CST_b], [PS_b[6 + r % 2]])
            cpv(ST[:, :, r], bank(6 + r % 2, 0, DC), [PS_b[6 + r % 2]], [MOD_b])
        ld(SELJ[0:NRG, :], selj_in, (), [MOD_b], "selj")
        stg_l = [arf.alloc([512], "modstg%d" % k) for k in range(4)]
        PW = 512 if CS % 512 == 0 else 384
        assert CS % PW == 0
        npp = CS // PW
        k = 0
        for l in range(DEPTH):
            for p in range(npp):
                s0 = (k % 2) * 2
                pb = k % 2
                stg = stg_l[k % 4]
                k += 1
                wv = R2[:, s0 * WPE:s0 * WPE + DC * PW].rearrange("p (c f) -> p c f", c=DC)
                ldc(wv, w_mod[l, :, p * PW:(p + 1) * PW].rearrange("(c p) f -> p c f", p=128), (), [WS_b[s0], WS_b[s0 + 1]], "ws%d" % s0)
                for dc in range(DC):
                    mm(PS[0:NCOND, pb, 0:PW], ST[:, dc, :], wv[:, dc, :], dc == 0, dc == DC - 1,
                       [MOD_b, WS_b[s0], WS_b[s0 + 1]], [PS_b[pb]])
                cpa(stg.ap[0:NCOND, 0:PW], PS[0:NCOND, pb, 0:PW], [PS_b[pb]], [stg.b])
                ld(ag_in[:, l * CS + p * PW:l * CS + (p + 1) * PW], stg.ap[0:NCOND, 0:PW], [stg.b], [agin_b], stg.b.name)
        sch.dma("pool", None, None, [agin_b], [agout_b], "agcc",
                fn=lambda eng: eng.collective_compute("AllGather", op=ALU.bypass, replica_groups=[list(range(NCORES))],
                                                      ins=[ag_in], outs=[ag_out]))
        RPR = CS // 128
        gts = [arf.alloc([512], "agt%d" % k) for k in range(2)]
        k = 0
        for t in range(DEPTH * npp):
            l, p = t // npp, t % npp
            gt = gts[t % 2]
            ld(gt.ap[0:NRG, 0:PW], ag_out[:, t * PW:(t + 1) * PW], [agout_b], [gt.b], gt.b.name)
            for j in range(NCORES):
                pb = 2 + k % 4
                stg = stg_l[k % 4]
                k += 1
                mm(PS[0:1, pb, 0:PW], SELJ[0:NRG, j:j + 1], gt.ap[0:NRG, 0:PW], True, True, [MOD_b, gt.b], [PS_b[pb]])
                cpa(stg.ap[0:1, 0:PW], PS[0:1, pb, 0:PW], [PS_b[pb]], [stg.b])
                ld(modr_s[l, j * RPR + p * (PW // 128):j * RPR + (p + 1) * (PW // 128), :].rearrange("(o r) f -> o (r f)", o=1), stg.ap[0:1, 0:PW],
                   [stg.b], [modr_b], stg.b.name)
    else:
        phase()
        cr = arf.alloc([128], "cr")
        ld(cr.ap[0:DC, :], cond_in.rearrange("o (c p) -> (o c) p", p=128), (), [cr.b], "cr")
        act(cr.ap[0:DC, :], cr.ap[0:DC, :], AF.Silu, [cr.b], [cr.b])
        tr(bank(7, 0, DC), cr.ap[0:DC, :], IDN[0:DC, 0:DC], [cr.b, CST_b], [PS_b[7]])
        cpv(ST[:, :, 0], bank(7, 0, DC), [PS_b[7]], [MOD_b])

        pass
    nmp = 9 * D // 512
    MSTG = sb("MSTG", [128, 2, 512], F32)
    MSTG_b = [Buf("MSTG0", True), Buf("MSTG1", True)]
    mctr = [0]

    def mod_panel(l, p, pair=None, pb=7):
        k = mctr[0]
        mctr[0] += 1
        s0 = (k % 2) * 2 if pair is None else pair
        wv = R2[:, s0 * WPE:(s0 + 2) * WPE].rearrange("p (c f) -> p c f", c=DC)
        ldc(wv, w_mod[l, :, p * 512:(p + 1) * 512].rearrange("(c p) f -> p c f", p=128), (), [WS_b[s0], WS_b[s0 + 1]], "ws%d" % s0)
        for dc in range(DC):
            mm(PS[0:1, pb, :], ST[:, dc, :], wv[:, dc, :], dc == 0, dc == DC - 1,
               [MOD_b, WS_b[s0], WS_b[s0 + 1]], [PS_b[pb]])
        cpa(MSTG[0:1, k % 2, :], PS[0:1, pb, :], [PS_b[pb]], [MSTG_b[k % 2]])
        ld(modr_s[l, p * 4:(p + 1) * 4, :].rearrange("(o r) f -> o (r f)", o=1), MSTG[0:1, k % 2, :], [MSTG_b[k % 2]], [modr_b], "mstg%d" % (k % 2))

    def mod_finalize(l, groups, with_ng):
        phase()
        for j3 in groups:
            R = 3 * DC
            rows_to_cols(MODT[:, l, j3 * R:(j3 + 1) * R], modr_s[l, j3 * R:(j3 + 1) * R, :], R, [modr_b], [MOD_b],
                         add_dram=b_mod[l, j3 * R:(j3 + 1) * R, :], add_b=())
        if with_ng:
            rows_to_cols(NGT[:, l, :], norm_g[l], 3 * DC, (), [MOD_b])
        for i in groups:
            stt(GS[:, l, i * DC:(i + 1) * DC], MODT[:, l, (3 * i + 1) * DC:(3 * i + 2) * DC], 1.0, NGT[:, l, i * DC:(i + 1) * DC],
                ALU.add, ALU.mult, [MOD_b], [MOD_b])
            gsc = 1.0 if i == 1 else 0.5
            ts(GATE[:, l, i * DC:(i + 1) * DC], MODT[:, l, (3 * i + 2) * DC:(3 * i + 3) * DC], gsc, ALU.mult, [MOD_b], [MOD_b])

    npg = nmp // 3
    if not c.CC:
        for p in range(npg):
            mod_panel(0, p)
        mod_queue = [(0, p) for p in range(npg, nmp)] + [(l, p) for l in range(1, DEPTH) for p in range(nmp)]
        mod_finalize(0, [0], True)
    else:
        mod_queue = []
        for l in range(DEPTH):
            mod_finalize(l, [0, 1, 2], True)
    rows_to_cols(FGT[:], final_g, DC, (), [MOD_b])

    def mod_pump(n, pair=None, pb=7):
        for _ in range(n):
            if mod_queue:
                l, p = mod_queue.pop(0)
                mod_panel(l, p, pair, pb)

    def mod_need(l, g):
        while mod_queue and (mod_queue[0][0], mod_queue[0][1] // npg) <= (l, g):
            l_, p_ = mod_queue.pop(0)
            mod_panel(l_, p_)

    phase()
    for tc in range(NCH):
        xin = arf.alloc([D], "xin")
        ld(xin.ap, x_in[tc * 128:(tc + 1) * 128, :], (), [xin.b], "xin")
        for c4 in range(DC // 4):
            pb = c4 % 2
            for k in range(4):
                cc = c4 * 4 + k
                tr(bank(pb, k * 128, (k + 1) * 128), xin.ap[:, cc * 128:(cc + 1) * 128], IDN, [xin.b, CST_b], [PS_b[pb]])
            stg = arf.alloc([4, 128], "xstg%d" % pb)
            cpv(stg.ap, bank(pb).rearrange("p (k t) -> p k t", k=4), [PS_b[pb]], [stg.b])
            ld(xT_s[c4 * 4:(c4 + 1) * 4, :, tc * 128:(tc + 1) * 128].rearrange("c p t -> p c t"), stg.ap,
               [stg.b], xT_b[c4 * 4:(c4 + 1) * 4], "xstg%d" % pb)
        arf.off = 0

    HTw_b = [Buf("HTw%d" % i, True) for i in range(DC)]

    def sumsq_rstd(getx, ntok, dview):
        accs = [arf.alloc([ntok], "ssacc%d" % k) for k in range(4)]
        sqs = [arf.alloc([ntok], "sq%d" % k) for k in range(4)]
        for cc in range(DC):
            xap, xb = getx(cc)
            k = cc % 4
            if cc < 4:
                act(accs[k].ap, xap, AF.Square, [xb], [accs[k].b])
            else:
                act(sqs[k].ap, xap, AF.Square, [xb], [sqs[k].b])
                tt(accs[k].ap, accs[k].ap, sqs[k].ap, ALU.add, [accs[k].b, sqs[k].b], [accs[k].b])
        tt(accs[0].ap, accs[0].ap, accs[1].ap, ALU.add, [accs[0].b, accs[1].b], [accs[0].b])
        tt(accs[2].ap, accs[2].ap, accs[3].ap, ALU.add, [accs[2].b, accs[3].b], [accs[2].b])
        tt(accs[0].ap, accs[0].ap, accs[2].ap, ALU.add, [accs[0].b, accs[2].b], [accs[0].b])
        mm(bank(6, 0, ntok), ONES[:], accs[0].ap, True, True, [accs[0].b, CONST_b], [PS_b[6]])
        rs = arf.alloc([ntok], "rstd")
        act(rs.ap, bank(6, 0, ntok), AF.Sqrt, [PS_b[6], CONST_b], [rs.b], bias=EPSC, scale=1.0 / dview)
        recip(rs.ap, rs.ap, [rs.b], [rs.b])
        return rs

    XHV = R2[:].bitcast(F32).rearrange("p (c t) -> p c t", c=DC)
    QC = DC // 4

    def norm_phase(gs_ap, sh_ap):
        phase()
        nth = NT // 512
        for th in range(nth):
            arf.off = 0
            for q4 in range(4):
                ld(XHV[:, q4 * QC:(q4 + 1) * QC, :], xT_s[q4 * QC:(q4 + 1) * QC, :, th * 512:(th + 1) * 512].rearrange("c p t -> p c t"),
                   xT_b[q4 * QC:(q4 + 1) * QC], [WS_b[q4]], "xhq%d" % q4)
            getx = lambda cc: (XHV[:, cc, :], WS_b[cc // QC])
            rs = sumsq_rstd(getx, 512, D)
            tmps = [arf.alloc([512], "ntmp%d" % k) for k in range(4)]
            for cc in range(DC):
                tmp = tmps[cc % 4]
                xap, xb = getx(cc)
                stt(tmp.ap, xap, gs_ap[:, cc:cc + 1], rs.ap, ALU.mult, ALU.mult, [xb, rs.b, MOD_b], [tmp.b])
                last = (th == nth - 1 and cc == DC - 1)
                act(HT[:, cc, th * 512:(th + 1) * 512], tmp.ap, AF.Identity, [tmp.b, MOD_b] + (HTw_b if last else []),
                    [HT_b] if last else [HTw_b[cc]], bias=sh_ap[:, cc:cc + 1])

    wctr = [0]

    def load_panel(w_dram_rows_cols):
        s = wctr[0] % NSLOT
        wctr[0] += 1
        v = wslot(s).rearrange("p (c f) -> p c f", c=DC)
        ldc(v, w_dram_rows_cols.rearrange("(c p) f -> p c f", p=128), (), [WS_b[s]], "ws%d" % s)
        return v, WS_b[s]

    def gemm_fm(wv, wb, col0, banks):
        for h in range(NH):
            for dc in range(DC):
                mm(bank(banks[h]), wv[:, dc, col0:col0 + 128], HT[:, dc, h * 512:(h + 1) * 512], dc == 0, dc == DC - 1,
                   [wb, HT_b], [PS_b[banks[h]]])

    def gemm_tm(wv, wb, tc, pb):
        for dc in range(DC):
            mm(bank(pb, 0, 256), HT[:, dc, tc * 128:(tc + 1) * 128], wv[:, dc, :], dc == 0, dc == DC - 1,
               [wb, HT_b], [PS_b[pb]])

    def ffn(l, i):
        gate_ap = GATE[:, l, i * DC:(i + 1) * DC]
        norm_phase(GS[:, l, i * DC:(i + 1) * DC], MODT[:, l, 3 * i * DC:(3 * i + 1) * DC])
        phase()
        wg, wu, wd = ffn_wg[l, i // 2], ffn_wu[l, i // 2], ffn_wd[l, i // 2]
        sgs = [arf.alloc([512], "sg%d" % k) for k in range(2)]
        asts = [arb.alloc([NT], "ast%d" % k) for k in range(2)]
        k = 0
        for p in range(F // 256):
            gv, gb = load_panel(wg[:, p * 256:(p + 1) * 256])
            uv, ub = load_panel(wu[:, p * 256:(p + 1) * 256])
            for fl in range(2):
                fc = p * 2 + fl
                base = (fc % 2) * 4
                gemm_fm(gv, gb, fl * 128, [base + h for h in range(NH)])
                gemm_fm(uv, ub, fl * 128, [base + 2 + h for h in range(NH)])
                ast = asts[fc % 2]
                for h in range(NH):
                    sg = sgs[k % 2]
                    k += 1
                    act(sg.ap, bank(base + h), AF.Silu, [PS_b[base + h]], [sg.b])
                    tt(ast.ap[:, h * 512:(h + 1) * 512], sg.ap, bank(base + 2 + h), ALU.mult, [sg.b, PS_b[base + 2 + h]], [ast.b])
                ld(a_s[fc], ast.ap, [ast.b], [a_b[fc]], "ast%d" % (fc % 2))
        phase()
        FG = c.FG
        ngr = (FC + FG - 1) // FG
        xts = [arf.alloc([NT], "xt%d" % k) for k in range(2)]
        ags = [arb.alloc([FG, NT], "ag%d" % k) for k in range(2)]
        HD = DC // 2
        HW = HD * 128
        assert FG * HW <= WPE
        gi = 0
        pumping = bool(mod_queue)
        for dh in range(2):
            for g in range(ngr):
                f0 = g * FG
                nf = min(FG, FC - f0)
                ag = ags[gi % 2]
                s0 = gi % (2 if pumping else NSLOT)
                gi += 1
                ld(ag.ap[:, 0:nf, :], a_s[f0:f0 + nf].rearrange("f p t -> p f t"), a_b[f0:f0 + nf], [ag.b], ag.b.name)
                wv = R2[:, s0 * WPE:s0 * WPE + FG * HW].rearrange("p (f d) -> p f d", f=FG)
                ldc(wv[:, 0:nf, :], wd[f0 * 128:(f0 + nf) * 128, dh * HW:(dh + 1) * HW].rearrange("(f p) d -> p f d", p=128),
                    (), [WS_b[s0]], "ws%d" % s0)
                k = 0
                for dc in range(HD):
                    for h in range(NH):
                        pb = k % (6 if pumping else 8)
                        k += 1
                        for fl in range(nf):
                            mm(bank(pb), wv[:, fl, dc * 128:(dc + 1) * 128], ag.ap[:, fl, h * 512:(h + 1) * 512], fl == 0, fl == nf - 1,
                               [WS_b[s0], ag.b], [PS_b[pb]])
                        dst = ACC[:, dc, h * 512:(h + 1) * 512]
                        if g == 0:
                            cpa(dst, bank(pb), [PS_b[pb]], [ACC_b[dc]])
                        else:
                            tt(dst, dst, bank(pb), ALU.add, [PS_b[pb], ACC_b[dc]], [ACC_b[dc]])
                if pumping:
                    mod_pump(1, pair=2, pb=6 + gi % 2)
            for dc in range(HD):
                cc = dh * HD + dc
                xt = xts[cc % 2]
                ld(xt.ap, xT_s[cc], [xT_b[cc]], [xt.b], xt.b.name)
                stt(xt.ap, ACC[:, dc, :], gate_ap[:, cc:cc + 1], xt.ap, ALU.mult, ALU.add, [ACC_b[dc], xt.b, MOD_b], [xt.b])
                ld(xT_s[cc], xt.ap, [xt.b], [xT_b[cc]], xt.b.name + "s")

    def layer_tables(l):
        phase()
        o = 0
        for (src, R) in ((conv_w[l], 4 * NB), (conv_b[l], NB), (lru_ba[l], 2 * NB), (lru_bi[l], 2 * NB),
                         (lru_lam[l], 2 * NB), (slru_in[l], 2 * NB), (ret_g[l], RH)):
            rows_to_cols(LV[:, o:o + R], src, R, (), [LV_b])
            o += R
        t = arf.alloc([2 * NB], "sc8t")
        act(t.ap, LV[:, oLAM:oLAM + 2 * NB], AF.Exp, [LV_b], [t.b], scale=-1.0)
        act(t.ap, t.ap, AF.Ln, [t.b, CONST_b], [t.b], bias=ONEC)
        ts(SC8[:], t.ap, -8.0, ALU.mult, [t.b], [LV_b])
        ld(GAINQ[:], q_gain[l:l + 1, :].partition_broadcast(128), (), [LV_b], "gq")
        ld(GAINK[:], k_gain[l:l + 1, :].partition_broadcast(128), (), [LV_b], "gk")
        ld(LG[:], ret_logit[l:l + 1, :].partition_broadcast(128), (), [RET_b], "lg")
        act(LG[:], LG[:], AF.Exp, [RET_b], [RET_b], scale=-1.0)
        act(LG[:], LG[:], AF.Ln, [RET_b, CONST_b], [RET_b], bias=ONEC)
        ts(LG[:], LG[:], -1.0, ALU.mult, [RET_b], [RET_b])
        for h in range(RH):
            lgf, lgb = LG[:, h:h + 1], LG[:, RH + h:RH + h + 1]
            act(KDEC[:, h, 0:1], IVEC[:, 0:1], AF.Exp, [CST_b, RET_b], [RET_b], scale=lgf)
            act(KDEC[:, h, 1:2], IVEC[:, 1:2], AF.Exp, [CST_b, RET_b], [RET_b], scale=lgb)
            act(CDEC[:, h, 0:1], IVEC[:, 2:3], AF.Exp, [CST_b, RET_b], [RET_b], scale=lgf)
            act(CDEC[:, h, 1:2], IVEC[:, 2:3], AF.Exp, [CST_b, RET_b], [RET_b], scale=lgb)
        ts(KDEC[:], KDEC[:], 128.0 ** -0.5, ALU.mult, [RET_b], [RET_b])

    def qk_evac(ps_ap, psb, gain_ap, ropc, rops, tc, store_dram=None, tag="q"):
        sq = arf.alloc([256], tag + "sq")
        act(sq.ap, ps_ap, AF.Square, [psb], [sq.b])
        ss = arf.alloc([2], tag + "ss")
        sch.op("dve", lambda e: e.tensor_reduce(out=ss.ap, in_=sq.ap.rearrange("p (h d) -> p h d", h=2), axis=AX.X, op=ALU.add),
               [sq.b], [ss.b])
        act(ss.ap, ss.ap, AF.Sqrt, [ss.b, CONST_b], [ss.b], bias=EPSC, scale=1.0 / 128)
        recip(ss.ap, ss.ap, [ss.b], [ss.b])
        kn = arf.alloc([256], tag + "kn")
        for h in range(2):
            stt(kn.ap[:, h * 128:(h + 1) * 128], ps_ap[:, h * 128:(h + 1) * 128], ss.ap[:, h:h + 1], gain_ap, ALU.mult, ALU.mult,
                [psb, ss.b, LV_b], [kn.b])
        if store_dram is not None:
            ld(store_dram, kn.ap, [kn.b], (), tag + "kns")
        kr = arf.alloc([256], tag + "kr")
        t2 = arf.alloc([256], tag + "t2")
        for h in range(2):
            xv = kn.ap[:, h * 128:(h + 1) * 128]
            tt(kr.ap[:, h * 128:(h + 1) * 128], xv, ropc.ap[:, tc, :], ALU.mult, [kn.b, ropc.b], [kr.b])
            x4 = xv.rearrange("p (a b d) -> p a b d", a=2, b=2)
            s4 = rops.ap[:, tc, :].rearrange("p (a b d) -> p a b d", a=2, b=2)
            o4 = t2.ap[:, h * 128:(h + 1) * 128].rearrange("p (a b d) -> p a b d", a=2, b=2)
            tt(o4[:, :, 0, :], x4[:, :, 1, :], s4[:, :, 0, :], ALU.mult, [kn.b, rops.b], [t2.b])
            tt(o4[:, :, 1, :], x4[:, :, 0, :], s4[:, :, 1, :], ALU.mult, [kn.b, rops.b], [t2.b])
        tt(kr.ap, kr.ap, t2.ap, ALU.add, [kr.b, t2.b], [kr.b])
        return kr

    def mixer(l):
        norm_phase(GS[:, l, DC:2 * DC], MODT[:, l, 3 * DC:4 * DC])
        layer_tables(l)
        win = w_in[l]
        scale = 128.0 ** -0.5
        for kp in range(KVW // 256):
            phase()
            ropc = arf.alloc([NCH, 128], "ropc")
            rops = arf.alloc([NCH, 128], "rops")
            ld(ropc.ap, ropc_in.rearrange("(c p) f -> p c f", p=128), (), [ropc.b], "ropc")
            ld(rops.ap, rops_in.rearrange("(c p) f -> p c f", p=128), (), [rops.b], "rops")
            KT = arb.alloc([2, NKEY], "KT")
            V = arb.alloc([KCH, 256], "V")
            QT = arb.alloc([G, NT], "QT")
            atts = [arb.alloc([NT], "att%d" % k) for k in range(2)]
            pts = [arb.alloc([512], "pt%d" % k) for k in range(4)]
            ldc(V.ap[:, NCH:KCH, :], cv_in[l, :, kp * 256:(kp + 1) * 256].rearrange("(c p) f -> p c f", p=128), (), [V.b], "Vc")
            for pc in range(PCH):
                ckt = arf.alloc([256], "ckt%d" % pc)
                ld(ckt.ap, ck_in[l, pc * 128:(pc + 1) * 128, kp * 256:(kp + 1) * 256], (), [ckt.b], "ckt%d" % (pc % 2))
                for h in range(2):
                    tr(bank(2 + h, 0, 128), ckt.ap[:, h * 128:(h + 1) * 128], IDN, [ckt.b, CST_b], [PS_b[2 + h]])
                    cpa(KT.ap[:, h, NT + pc * 128:NT + (pc + 1) * 128], bank(2 + h, 0, 128), [PS_b[2 + h]], [KT.b])
            mark = arf.off
            checkpoint("att: after cacheK")
            kv_, kb_ = load_panel(win[:, c.oK + kp * 256:c.oK + (kp + 1) * 256])
            for tc in range(NCH):
                pb = tc % 2
                arf.off = mark
                gemm_tm(kv_, kb_, tc, pb)
                kr = qk_evac(bank(pb, 0, 256), PS_b[pb], GAINK[:], ropc, rops, tc,
                             store_dram=nk_out[l, tc * 128:(tc + 1) * 128, kp * 256:(kp + 1) * 256], tag="k")
                for h in range(2):
                    tr(bank(2 + h, 0, 128), kr.ap[:, h * 128:(h + 1) * 128], IDN, [kr.b, CST_b], [PS_b[2 + h]])
                    cpa(KT.ap[:, h, tc * 128:(tc + 1) * 128], bank(2 + h, 0, 128), [PS_b[2 + h]], [KT.b])
            checkpoint("att: after K panel")
            mod_pump(3)
            vv_, vb_ = load_panel(win[:, c.oV + kp * 256:c.oV + (kp + 1) * 256])
            for tc in range(NCH):
                pb = tc % 2
                arf.off = mark
                gemm_tm(vv_, vb_, tc, pb)
                vf = arf.alloc([256], "vf")
                cpa(vf.ap, bank(pb, 0, 256), [PS_b[pb]], [vf.b])
                ld(nv_out[l, tc * 128:(tc + 1) * 128, kp * 256:(kp + 1) * 256], vf.ap, [vf.b], (), "vfs")
                cpv(V.ap[:, tc, :], vf.ap, [vf.b], [V.b])
            checkpoint("att: after V panel")
            mod_pump(3)
            for gl in range(2):
                g = kp * 2 + gl
                for qp in range(G // 2):
                    qv_, qb_ = load_panel(win[:, c.oQ + (g * G + qp * 2) * 128:c.oQ + (g * G + qp * 2 + 2) * 128])
                    for tc in range(NCH):
                        pb = tc % 2
                        arf.off = mark
                        gemm_tm(qv_, qb_, tc, pb)
                        qr = qk_evac(bank(pb, 0, 256), PS_b[pb], GAINQ[:], ropc, rops, tc, tag="q")
                        for h in range(2):
                            tr(bank(2 + h, 0, 128), qr.ap[:, h * 128:(h + 1) * 128], IDN, [qr.b, CST_b], [PS_b[2 + h]])
                            cpa(QT.ap[:, qp * 2 + h, tc * 128:(tc + 1) * 128], bank(2 + h, 0, 128), [PS_b[2 + h]], [QT.b])
                    mod_pump(3)
                arf.off = mark
                checkpoint("att: after Q panels")
                rcs = [arf.alloc([512], "rc0"), arf.alloc([512], "rc1")]
                for hq0 in range(0, G, 2):
                    for hh in range(NH):
                        def score(ci, kc, hh=hh, hq0=hq0):
                            cb = 4 + 2 * ci + (kc % 2)
                            mm(bank(cb), KT.ap[:, gl, kc * 128:(kc + 1) * 128], QT.ap[:, hq0 + ci, hh * 512:(hh + 1) * 512], True, True,
                               [KT.b, QT.b], [PS_b[cb]])
                        for ci in range(2):
                            score(ci, 0)
                        for kc in range(KCH):
                            if kc + 1 < KCH:
                                for ci in range(2):
                                    score(ci, kc + 1)
                            for ci in range(2):
                                cb = 4 + 2 * ci + (kc % 2)
                                pt = pts[ci * 2 + kc % 2]
                                for q2 in range(2):
                                    act(pt.ap[:, q2 * 256:(q2 + 1) * 256], bank(cb, q2 * 256, (q2 + 1) * 256), AF.Exp, [PS_b[cb], TAB_b], [pt.b],
                                        scale=scale, bias=MB[:, kc, hh * 2 + q2:hh * 2 + q2 + 1])
                            for ci in range(2):
                                pt = pts[ci * 2 + kc % 2]
                                mm(bank(2 * ci), V.ap[:, kc, gl * 128:(gl + 1) * 128], pt.ap, kc == 0, kc == KCH - 1, [V.b, pt.b], [PS_b[2 * ci]])
                                mm(bank(2 * ci + 1), ONESB[:], pt.ap, kc == 0, kc == KCH - 1, [CONST_b, pt.b], [PS_b[2 * ci + 1]])
                        for ci in range(2):
                            recip(rcs[ci].ap, bank(2 * ci + 1), [PS_b[2 * ci + 1]], [rcs[ci].b])
                            tt(atts[ci].ap[:, hh * 512:(hh + 1) * 512], bank(2 * ci), rcs[ci].ap, ALU.mult, [PS_b[2 * ci], rcs[ci].b], [atts[ci].b])
                    for ci in range(2):
                        ld(mt_s[g * G + hq0 + ci], atts[ci].ap, [atts[ci].b], [mt_b[g * G + hq0 + ci]], "atts%d" % ci)
        for p in range(LW // 256):
            phase()
            WAI = arb.alloc([8, 128], "WAI")
            for r in range(2):
                ldc(WAI.ap[:, r * 2:(r + 1) * 2, :], lru_wa[l, r, p * 2:(p + 1) * 2].rearrange("n k j -> k n j"), (), [WAI.b], "wa%d" % r)
                ldc(WAI.ap[:, 4 + r * 2:4 + (r + 1) * 2, :], lru_wi[l, r, p * 2:(p + 1) * 2].rearrange("n k j -> k n j"), (), [WAI.b], "wi%d" % r)
            xv_, xb_ = load_panel(win[:, c.oXB + p * 256:c.oXB + (p + 1) * 256])
            yv_, yb_ = load_panel(win[:, c.oYB + p * 256:c.oYB + (p + 1) * 256])
            for nl in range(2):
                n = p * 2 + nl
                arf.off = 0
                arb.off = 8 * 128
                gemm_fm(xv_, xb_, nl * 128, [0, 1][:NH])
                HS = arf.alloc([NSEG, 256], "HS")
                XBP = arf.alloc([NSEG, 259], "XBP")
                for h in range(NH):
                    cpa(XBP.ap[:, 2 * h:2 * h + 2, 2:258], bank(h).rearrange("p (s t) -> p s t", s=2), [PS_b[h]], [XBP.b])
                mset(XBP.ap[:, 0, 0:2], 0.0, [XBP.b])
                mset(XBP.ap[:, NSEG - 1, 258:259], 0.0, [XBP.b])
                ts(XBP.ap[:, 1:NSEG, 0:2], XBP.ap[:, 0:NSEG - 1, 256:258], FLAG, ALU.mult, [XBP.b, CONST_b], [XBP.b])
                ts(XBP.ap[:, 0:NSEG - 1, 258:259], XBP.ap[:, 1:NSEG, 2:3], FLAG, ALU.mult, [XBP.b, CONST_b], [XBP.b])
                XC = arf.alloc([NSEG, 256], "XC")
                cw = lambda j: LV[:, oCW + j * NB + n:oCW + j * NB + n + 1]
                act(XC.ap, XBP.ap[:, :, 2:258], AF.Identity, [XBP.b, LV_b], [XC.b], bias=LV[:, oCB + n:oCB + n + 1], scale=cw(2))
                for (j, o0) in ((0, 0), (1, 1), (3, 3)):
                    stt(XC.ap, XBP.ap[:, :, o0:o0 + 256], cw(j), XC.ap, ALU.mult, ALU.add, [XBP.b, XC.b, LV_b], [XC.b])
                XCB = arb.alloc([NT], "XCB")
                xcf = XC.ap.rearrange("p s t -> p (s t)")
                cpa(XCB.ap, xcf, [XC.b], [XCB.b])
                A = arf.alloc([NSEG, 256], "A")
                U = arf.alloc([NSEG, 256], "U")
                Hb = arf.alloc([NSEG, 256], "Hb")
                MU = arf.alloc([NT], "MU")
                cr_ = arf.alloc([1], "carry")
                Af = A.ap.rearrange("p s t -> p (s t)")
                Uf = U.ap.rearrange("p s t -> p (s t)")
                for r in range(2):
                    ia, ii = r * 2 + nl, 4 + r * 2 + nl
                    for h in range(NH):
                        mm(bank(2 + h), WAI.ap[:, ia, :], XCB.ap[:, h * 512:(h + 1) * 512], True, True, [WAI.b, XCB.b], [PS_b[2 + h]])
                        mm(bank(4 + h), WAI.ap[:, ii, :], XCB.ap[:, h * 512:(h + 1) * 512], True, True, [WAI.b, XCB.b], [PS_b[4 + h]])
                    Hd = HS if r == 0 else Hb
                    for h in range(NH):
                        sl = slice(h * 512, (h + 1) * 512)
                        act(Af[:, sl], bank(2 + h), AF.Sigmoid, [PS_b[2 + h], LV_b], [A.b], bias=LV[:, oBA + r * NB + n:oBA + r * NB + n + 1])
                        act(Uf[:, sl], bank(4 + h), AF.Sigmoid, [PS_b[4 + h], LV_b], [U.b], bias=LV[:, oBI + r * NB + n:oBI + r * NB + n + 1])
                    act(Af, Af, AF.Exp, [A.b, LV_b], [A.b], scale=SC8[:, r * NB + n:r * NB + n + 1])
                    tt(Uf, Uf, xcf, ALU.mult, [U.b, XC.b], [U.b])
                    tt(MU.ap, Af, Af, ALU.mult, [A.b], [MU.b])
                    act(MU.ap, MU.ap, AF.Sqrt, [MU.b, CONST_b], [MU.b], bias=ONEC, scale=-1.0)
                    tt(Uf, Uf, MU.ap, ALU.mult, [U.b, MU.b], [U.b])
                    order = range(NSEG) if r == 0 else range(NSEG - 1, -1, -1)
                    for si, s in enumerate(order):
                        if si == 0:
                            init = LV[:, oSL + r * NB + n:oSL + r * NB + n + 1]
                            ird = [LV_b]
                        else:
                            init = cr_.ap
                            ird = [cr_.b]

                        def view(tb, s=s, r=r):
                            a = tb.ap[:, s, :]
                            if r == 0:
                                return a
                            return bass.AP(a.tensor, a.offset + 255, [[a.ap[0][0], 128], [-1, 256]])
                        o_, d0, d1 = view(Hd), view(A), view(U)
                        sch.op("dve", lambda e, o_=o_, d0=d0, d1=d1, init=init: e.tensor_tensor_scan(
                            out=o_, data0=d0, data1=d1, initial=init, op0=ALU.mult, op1=ALU.add), [A.b, U.b] + ird, [Hd.b])
                        last = Hd.ap[:, s, 255:256] if r == 0 else Hd.ap[:, s, 0:1]
                        col = (s * 2 + r) * NB + n
                        cpa(LST[:, col:col + 1], last, [Hd.b], [LST_b])
                        if si < NSEG - 1:
                            ts(cr_.ap, last, FLAG, ALU.mult, [Hd.b, CONST_b], [cr_.b])
                tt(HS.ap, HS.ap, Hb.ap, ALU.add, [HS.b, Hb.b], [HS.b])
                gemm_fm(yv_, yb_, nl * 128, [0, 1][:NH])
                lr = arb.alloc([NT], "lr")
                hsf = HS.ap.rearrange("p s t -> p (s t)")
                t = arf.alloc([512], "gl_t")
                for h in range(NH):
                    sl = slice(h * 512, (h + 1) * 512)
                    act(t.ap, bank(h), AF.Square, [PS_b[h]], [t.b])
                    ts(t.ap, t.ap, 0.044715, ALU.mult, [t.b], [t.b], s2=1.0, op1=ALU.add)
                    tt(t.ap, t.ap, bank(h), ALU.mult, [t.b, PS_b[h]], [t.b])
                    act(t.ap, t.ap, AF.Sigmoid, [t.b], [t.b], scale=1.5957691216057308)
                    tt(t.ap, t.ap, bank(h), ALU.mult, [t.b, PS_b[h]], [t.b])
                    tt(lr.ap[:, sl], t.ap, hsf[:, sl], ALU.mult, [t.b, HS.b], [lr.b])
                ld(mt_s[c.AW // 128 + n], lr.ap, [lr.b], [mt_b[c.AW // 128 + n]], "lrs")
        phase()
        NR = NSEG * 2 * NB
        tr(PS[0:NR, 7, 0:128], LST[:], IDN, [LST_b, CST_b], [PS_b[7]])
        lso = arf.alloc([128], "lso")
        cpv(lso.ap[0:NR, :], PS[0:NR, 7, 0:128], [PS_b[7]], [lso.b])
        ld(nlru_out[l], lso.ap[0:NR, :], [lso.b], (), "lsos")
        kscale = 128.0 ** -0.5
        for p in range(RW // 256):
            phase()
            qv_, qb_ = load_panel(win[:, c.oRQ + p * 256:c.oRQ + (p + 1) * 256])
            kv_, kb_ = load_panel(win[:, c.oRK + p * 256:c.oRK + (p + 1) * 256])
            vv_, vb_ = load_panel(win[:, c.oRV + p * 256:c.oRV + (p + 1) * 256])
            gv_, gb_ = load_panel(win[:, c.oRG + p * 256:c.oRG + (p + 1) * 256])
            for hl in range(2):
                h = p * 2 + hl
                arf.off = 0
                arb.off = 0
                hs = slice(hl * 128, (hl + 1) * 128)
                QR = arb.alloc([NT], "QR")
                QDF = arb.alloc([NT], "QDF")
                QDB = arb.alloc([NT], "QDB")
                KR = arb.alloc([NT], "KR")
                KDF = arb.alloc([NCH, 128], "KDF")
                KDB = arb.alloc([NCH, 128], "KDB")
                VR = arb.alloc([NCH, 128], "VR")
                SGR = arf.alloc([NT], "SGR")
                mcomb = arf.alloc([128], "mcomb")
                qdec = arf.alloc([2, 128], "qdec")
                t2 = arf.alloc([128], "mc2")
                lgf, lgb = LG[:, h:h + 1], LG[:, RH + h:RH + h + 1]
                act(mcomb.ap, DPOS, AF.Exp, [CST_b, RET_b], [mcomb.b], scale=lgf)
                tt(mcomb.ap, mcomb.ap, LOWM, ALU.mult, [mcomb.b, CST_b], [mcomb.b])
                act(t2.ap, DNEG, AF.Exp, [CST_b, RET_b], [t2.b], scale=lgb)
                tt(t2.ap, t2.ap, UPM, ALU.mult, [t2.b, CST_b], [t2.b])
                tt(mcomb.ap, mcomb.ap, t2.ap, ALU.add, [mcomb.b, t2.b], [mcomb.b])
                ts(mcomb.ap, mcomb.ap, kscale, ALU.mult, [mcomb.b], [mcomb.b])
                act(qdec.ap[:, 0, :], IROW1, AF.Exp, [CST_b, RET_b], [qdec.b], scale=lgf)
                act(qdec.ap[:, 1, :], IROW2, AF.Exp, [CST_b, RET_b], [qdec.b], scale=lgb)
                gemm_fm(qv_, qb_, hl * 128, [0, 1][:NH])
                for hh in range(NH):
                    sl = slice(hh * 512, (hh + 1) * 512)
                    cpa(QR.ap[:, sl], bank(hh), [PS_b[hh]], [QR.b])
                    b3 = bank(hh).rearrange("p (c i) -> p c i", c=4)
                    tt(QDF.ap[:, sl].rearrange("p (c i) -> p c i", c=4), b3, qdec.ap[:, 0:1, :].broadcast_to([128, 4, 128]), ALU.mult,
                       [PS_b[hh], qdec.b, QR.b], [QDF.b])
                    tt(QDB.ap[:, sl].rearrange("p (c i) -> p c i", c=4), b3, qdec.ap[:, 1:2, :].broadcast_to([128, 4, 128]), ALU.mult,
                       [PS_b[hh], qdec.b, QR.b], [QDB.b])
                gemm_fm(kv_, kb_, hl * 128, [2, 3][:NH])
                for hh in range(NH):
                    cpa(KR.ap[:, hh * 512:(hh + 1) * 512], bank(2 + hh), [PS_b[2 + hh]], [KR.b])
                gemm_fm(gv_, gb_, hl * 128, [0, 1][:NH])
                for hh in range(NH):
                    act(SGR.ap[:, hh * 512:(hh + 1) * 512], bank(hh), AF.Silu, [PS_b[hh]], [SGR.b])
                for t4 in range(NCH // 4):
                    for (wv_, wb_, pb) in ((kv_, kb_, 2 + t4 % 2), (vv_, vb_, 4 + t4 % 2)):
                        for k4 in range(4):
                            tc = t4 * 4 + k4
                            for dc in range(DC):
                                mm(bank(pb, k4 * 128, (k4 + 1) * 128), HT[:, dc, tc * 128:(tc + 1) * 128], wv_[:, dc, hs], dc == 0, dc == DC - 1,
                                   [wb_, HT_b], [PS_b[pb]])
                    kb3 = bank(2 + t4 % 2).rearrange("p (c d) -> p c d", c=4)
                    act(KDF.ap[:, t4 * 4:(t4 + 1) * 4, :], kb3, AF.Identity, [PS_b[2 + t4 % 2], RET_b], [KDF.b], scale=KDEC[:, h, 0:1])
                    ts(KDB.ap[:, t4 * 4:(t4 + 1) * 4, :], kb3, KDEC[:, h, 1:2], ALU.mult, [PS_b[2 + t4 % 2], RET_b, KDF.b], [KDB.b])
                    cpa(VR.ap[:, t4 * 4:(t4 + 1) * 4, :], bank(4 + t4 % 2).rearrange("p (c d) -> p c d", c=4), [PS_b[4 + t4 % 2]], [VR.b])
                SB_ = [arb.alloc([NCH, 128], "SFB"), arb.alloc([NCH, 128], "SBB")]
                for r in range(2):
                    S = arf.alloc([128], "S%d" % r)
                    ld(S.ap, sret_in[l, r, h], (), [S.b], "S%d" % r)
                    KD = KDF if r == 0 else KDB
                    order = list(range(NCH)) if r == 0 else list(range(NCH - 1, -1, -1))
                    sos = [arf.alloc([128], "so%d_%d" % (r, k)) for k in range(2)]
                    for ci, cc in enumerate(order):
                        cpa(SB_[r].ap[:, cc, :], S.ap, [S.b], [SB_[r].b])
                        pb = 2 + (ci % 2)
                        mm(bank(pb, 0, 128), KD.ap[:, cc, :], VR.ap[:, cc, :], True, True, [KD.b, VR.b], [PS_b[pb]])
                        stt(S.ap, S.ap, CDEC[:, h, r:r + 1], bank(pb, 0, 128), ALU.mult, ALU.add, [S.b, PS_b[pb], RET_b], [S.b])
                        boundary = (cc % 2 == 1) if r == 0 else (cc % 2 == 0)
                        if boundary:
                            seg = cc // 2
                            so = sos[seg % 2]
                            cpv(so.ap, S.ap, [S.b], [so.b])
                            ld(nret_out[l, seg, r, h], so.ap, [so.b], (), "sos%d_%d" % (r, seg % 2))
                            if ci < NCH - 1:
                                ts(S.ap, S.ap, FLAG, ALU.mult, [S.b, CONST_b], [S.b])
                PT = [arb.alloc([128], "rpt%d" % k) for k in range(2)]
                for cc in range(NCH):
                    ts_ = slice(cc * 128, (cc + 1) * 128)
                    pb = 2 + (cc % 2)
                    mm(bank(pb, 0, 128), KR.ap[:, ts_], QR.ap[:, ts_], True, True, [KR.b, QR.b], [PS_b[pb]])
                    pt = PT[cc % 2]
                    tt(pt.ap, bank(pb, 0, 128), mcomb.ap, ALU.mult, [PS_b[pb], mcomb.b], [pt.b])
                    ob = 4 + cc // 4
                    osl = bank(ob, (cc % 4) * 128, (cc % 4 + 1) * 128)
                    mm(osl, VR.ap[:, cc, :], pt.ap, True, False, [VR.b, pt.b], [PS_b[ob]])
                    mm(osl, SB_[0].ap[:, cc, :], QDF.ap[:, ts_], False, False, [SB_[0].b, QDF.b], [PS_b[ob]])
                    mm(osl, SB_[1].ap[:, cc, :], QDB.ap[:, ts_], False, True, [SB_[1].b, QDB.b], [PS_b[ob]])
                rt = arb.alloc([NT], "rt")
                for hh in range(NH):
                    sl = slice(hh * 512, (hh + 1) * 512)
                    sq = arf.alloc([512], "rsq%d" % hh)
                    act(sq.ap, bank(4 + hh), AF.Square, [PS_b[4 + hh]], [sq.b])
                    mm(bank(6 + hh % 2), ONES[:], sq.ap, True, True, [sq.b, CONST_b], [PS_b[6 + hh % 2]])
                    rs = arf.alloc([512], "rrs%d" % hh)
                    act(rs.ap, bank(6 + hh % 2), AF.Sqrt, [PS_b[6 + hh % 2], CONST_b], [rs.b], bias=EPSC, scale=1.0 / 128)
                    recip(rs.ap, rs.ap, [rs.b], [rs.b])
                    tt(rs.ap, rs.ap, bank(4 + hh), ALU.mult, [rs.b, PS_b[4 + hh]], [rs.b])
                    stt(rt.ap[:, sl], rs.ap, LV[:, oRG_ + h:oRG_ + h + 1], SGR.ap[:, sl], ALU.mult, ALU.mult, [rs.b, LV_b, SGR.b], [rt.b])
                ci_ = (c.AW + c.LW) // 128 + h
                ld(mt_s[ci_], rt.ap, [rt.b], [mt_b[ci_]], "rts")
        phase()
        for q4 in range(4):
            c0, c1 = q4 * DC // 4, (q4 + 1) * DC // 4
            ld(HT[:, c0:c1, :], mt_s[c0:c1].rearrange("c p t -> p c t"), mt_b[c0:c1], [HT_b], "mtq%d" % q4)
        xts = [arf.alloc([NT], "xt%d" % k) for k in range(2)]
        gate_ap = GATE[:, l, DC:2 * DC]
        for p in range(D // 256):
            wv_, wb_ = load_panel(w_out[l][:, p * 256:(p + 1) * 256])
            for cl in range(2):
                cc = p * 2 + cl
                bs = [(cc % 2) * 2 + h for h in range(NH)]
                gemm_fm(wv_, wb_, cl * 128, bs)
                xt = xts[cc % 2]
                ld(xt.ap, xT_s[cc], [xT_b[cc]], [xt.b], xt.b.name)
                for h in range(NH):
                    sl = slice(h * 512, (h + 1) * 512)
                    stt(xt.ap[:, sl], bank(bs[h]), gate_ap[:, cc:cc + 1], xt.ap[:, sl], ALU.mult, ALU.add, [PS_b[bs[h]], xt.b, MOD_b], [xt.b])
                ld(xT_s[cc], xt.ap, [xt.b], [xT_b[cc]], xt.b.name + "s")
            mod_pump(1)

    def main_seq():
        for l in range(DEPTH):
            if l > 0 and not c.CC:
                mod_need(l, 2)
                mod_finalize(l, [0, 1, 2], True)
            ffn(l, 0)
            if l == 0 and not c.CC:
                mod_need(0, 1)
                mod_finalize(0, [1], False)
            mixer(l)
            if l == 0 and not c.CC:
                mod_need(0, 2)
                mod_finalize(0, [2], False)
            ffn(l, 2)
        final_norm()

    def final_norm():
      phase()
      for tc in range(NCH):
        arf.off = 0
        xh = arf.alloc([DC, 128], "fxh")
        ld(xh.ap, xT_s[:, :, tc * 128:(tc + 1) * 128].rearrange("c p t -> p c t"), xT_b, [xh.b], "fxh")
        rs = sumsq_rstd(lambda cc, xh=xh: (xh.ap[:, cc, :], xh.b), 128, D)
        ftmps = [arf.alloc([128], "ftmp0"), arf.alloc([128], "ftmp1")]
        HC = max(DC // 2, 4)
        for half in range(DC // HC):
            yo = arf.alloc([HC * 128], "yo%d" % half)
            for c4 in range(HC // 4):
                pb = c4 % 2
                for k in range(4):
                    cc = half * HC + c4 * 4 + k
                    tmp = ftmps[cc % 2]
                    stt(tmp.ap, xh.ap[:, cc, :], FGT[:, cc:cc + 1], rs.ap, ALU.mult, ALU.mult, [xh.b, rs.b, MOD_b], [tmp.b])
                    tr(bank(pb, k * 128, (k + 1) * 128), tmp.ap, IDN, [tmp.b, CST_b], [PS_b[pb]])
                cpa(yo.ap[:, c4 * 512:(c4 + 1) * 512], bank(pb), [PS_b[pb]], [yo.b])
            ld(y_out[tc * 128:(tc + 1) * 128, half * HC * 128:(half + 1) * HC * 128], yo.ap, [yo.b], (), "yos%d" % half)
            arf.off -= (HC * 128 + 15) // 16 * 16
    main_seq()
    sch.final_wait()
    print("build: ninst", sch.ninst, "nsem", len(sch.semmap), "ckpts", ckpt[0], {e: sch.cnt[e] for e in sch.cnt})

    with nc.Block() as block:
        @block.tensor
        def _(e):
            for f in sch.prog["pe"]:
                f(e)

        @block.scalar
        def _(e):
            for f in sch.prog["act"]:
                f(e)

        @block.vector
        def _(e):
            for f in sch.prog["dve"]:
                f(e)

        @block.gpsimd
        def _(e):
            for f in sch.prog["pool"]:
                f(e)

        @block.sync
        def _(e):
            for f in sch.prog["sp"]:
                f(e)
    es.close()
    return nc


def _consts(cfg):
    j = np.arange(128, dtype=np.float32)[:, None]
    i = np.arange(128, dtype=np.float32)[None, :]
    cst = np.zeros((128, 7 * 128 + 4), np.float32)
    cst[:, 0:128] = np.eye(128, dtype=np.float32)
    cst[:, 128:256] = np.maximum(i - j, 0)
    cst[:, 256:384] = np.maximum(j - i, 0)
    cst[:, 384:512] = (i >= j)
    cst[:, 512:640] = (j >= i)
    cst[:, 640:768] = i + 1
    cst[:, 768:896] = 128 - i
    cst[:, 896] = 127 - j[:, 0]
    cst[:, 897] = j[:, 0]
    cst[:, 898] = 128
    return cst


def _rope_tables(cfg):
    NT, GW = cfg.NT, cfg.GRID_W
    t = np.arange(NT)
    row = (t // GW).astype(np.float32)
    col = (t % GW).astype(np.float32)
    inv = (10000.0 ** (-np.arange(32, dtype=np.float32) / 32)).astype(np.float32)
    ar = row[:, None] * inv[None, :]
    ac = col[:, None] * inv[None, :]
    C = np.concatenate([np.cos(ar), np.cos(ar), np.cos(ac), np.cos(ac)], axis=1).astype(np.float32)
    S = np.concatenate([-np.sin(ar), np.sin(ar), -np.sin(ac), np.sin(ac)], axis=1).astype(np.float32)
    return C, S


_NC_CACHE = {}


def kernel(cfg=None, **inp):
    if cfg is None:
        cfg = Cfg()
    c = cfg
    key = (c.D, c.F, c.H, c.KVH, c.NB, c.RH, c.NT, c.PAST, c.DEPTH, c.NPC, c.NSC, c.FG, c.CC, c.stop)
    if key not in _NC_CACHE:
        _NC_CACHE[key] = build_nc(c)
    nc = _NC_CACHE[key]
    f32 = lambda a: np.ascontiguousarray(np.asarray(a, dtype=np.float32))
    NT, D, DEPTH = c.NT, c.D, c.DEPTH
    SPC = NT // 256
    shared = {
        "cst": _consts(c),
        "norm_g": f32(inp["norm_g"]).reshape(DEPTH, 3 * c.DC, 128),
        "b_mod": f32(inp["b_mod"]).reshape(DEPTH, 9 * c.DC, 128),
        "ffn_wg": f32(inp["ffn_wg"]), "ffn_wu": f32(inp["ffn_wu"]), "ffn_wd": f32(inp["ffn_wd"]),
        "w_in": f32(inp["w_in"]),
        "q_gain": f32(inp["q_gain"]), "k_gain": f32(inp["k_gain"]),
        "conv_w": f32(inp["lru_conv_w"]).reshape(DEPTH, 4 * c.NB, 128),
        "conv_b": f32(inp["lru_conv_b"]).reshape(DEPTH, c.NB, 128),
        "lru_wa": f32(inp["lru_wa"]), "lru_wi": f32(inp["lru_wi"]),
        "lru_ba": f32(inp["lru_ba"]).reshape(DEPTH, 2 * c.NB, 128),
        "lru_bi": f32(inp["lru_bi"]).reshape(DEPTH, 2 * c.NB, 128),
        "lru_lam": f32(inp["lru_lambda"]).reshape(DEPTH, 2 * c.NB, 128),
        "ret_logit": f32(inp["ret_logit"]).reshape(DEPTH, 2 * c.RH),
        "ret_g": f32(inp["ret_g"]).reshape(DEPTH, c.RH, 128),
        "w_out": f32(inp["w_out"]),
        "final_g": f32(inp["final_g"]).reshape(c.DC, 128),
    }
    xp, xs = f32(inp["x_prompt"]), f32(inp["x_sample"])
    ck, cv = f32(inp["cache_k"]), f32(inp["cache_v"])
    slru, sret = f32(inp["state_lru"]), f32(inp["state_ret"])
    cc_, cctx = f32(inp["c"]), f32(inp["c_ctx"])
    ropC, ropS = _rope_tables(c)
    NKEY = NT + c.PAST
    KCH_ = (NT + c.PAST) // 128
    mb_p = np.full((KCH_, c.NSEG), NEG, np.float32)
    for kc in range(NT // 128):
        mb_p[kc, kc // 2] = 0.0
    mb_p = np.ascontiguousarray(np.broadcast_to(mb_p.reshape(1, -1), (128, KCH_ * c.NSEG)))
    mb_s = np.zeros((128, KCH_ * c.NSEG), np.float32)
    zeros_ck = np.zeros((DEPTH, c.PAST, c.KVW), np.float32)
    wm = f32(inp["w_mod"])
    condall = np.ascontiguousarray(np.concatenate([cctx.reshape(1, D), cc_.reshape(-1, D)], axis=0))
    in_maps = []
    for core in range(c.NPC + c.NSC):
        m = dict(shared)
        if c.CC:
            m["w_mod"] = np.ascontiguousarray(wm[:, :, core * c.CS:(core + 1) * c.CS])
            m["condall"] = condall
            rstar = 0 if core < c.NPC else 1 + (core - c.NPC)
            sj = np.zeros((c.NCORES * c.NCOND, c.NCORES), np.float32)
            for j in range(c.NCORES):
                sj[j * c.NCOND + rstar, j] = 1.0
            m["selj"] = sj
        else:
            m["w_mod"] = wm
        if core < c.NPC:
            m["x"] = np.ascontiguousarray(xp[core * SPC:(core + 1) * SPC].reshape(NT, D))
            if not c.CC:
                m["cond"] = cctx.reshape(1, D)
            m["ck"] = zeros_ck
            m["cv"] = zeros_ck
            m["slru"] = np.zeros((DEPTH, 2 * c.NB, 128), np.float32)
            m["sret"] = np.zeros((DEPTH, 2, c.RH, 128, 128), np.float32)
            m["flag"] = np.zeros((128, 2), np.float32)
            m["mb"] = mb_p
            m["ropc"] = np.ones((NT, 128), np.float32)
            m["rops"] = np.zeros((NT, 128), np.float32)
        else:
            b = core - c.NPC
            m["x"] = np.ascontiguousarray(xs[b].reshape(NT, D))
            if not c.CC:
                m["cond"] = np.ascontiguousarray(cc_[b].reshape(1, D))
            m["ck"] = np.ascontiguousarray(ck[b].reshape(DEPTH, c.PAST, c.KVW))
            m["cv"] = np.ascontiguousarray(cv[b].reshape(DEPTH, c.PAST, c.KVW))
            m["slru"] = np.ascontiguousarray(slru[b].reshape(DEPTH, 2 * c.NB, 128))
            m["sret"] = np.ascontiguousarray(sret[b])
            m["flag"] = np.ones((128, 2), np.float32)
            m["mb"] = mb_s
            m["ropc"] = ropC
            m["rops"] = ropS
        in_maps.append(m)
    res = run_bass_kernel_spmd(nc, in_maps, core_ids=list(range(c.NPC + c.NSC)))
    R = res.results
    B = c.NPC * SPC
    y_p = np.zeros((B, 256, D), np.float32)
    y_s = np.zeros((c.NSC, NT, D), np.float32)
    nk = np.zeros((B, DEPTH, 256, c.KVH, 128), np.float32)
    nv = np.zeros((B, DEPTH, 256, c.KVH, 128), np.float32)
    nl = np.zeros((B, DEPTH, 2, c.LW), np.float32)
    nr = np.zeros((B, DEPTH, 2, c.RH, 128, 128), np.float32)
    for core in range(c.NPC):
        r = R[core]
        for s in range(SPC):
            b = core * SPC + s
            y_p[b] = r["y"][s * 256:(s + 1) * 256]
            for l in range(DEPTH):
                nk[b, l] = r["nk"][l, s * 256:(s + 1) * 256].reshape(256, c.KVH, 128)
                nv[b, l] = r["nv"][l, s * 256:(s + 1) * 256].reshape(256, c.KVH, 128)
                nl[b, l] = r["nlru"][l].reshape(c.NSEG, 2, c.LW)[s]
                nr[b, l] = r["nret"][l, s]
    for b in range(c.NSC):
        y_s[b] = R[c.NPC + b]["y"]
    return (y_p, y_s, nk, nv, nl, nr)
```

```python
import numpy as np
from contextlib import ExitStack
import concourse.bass as bass
import concourse.mybir as mybir
from concourse.bass_utils import run_bass_kernel_spmd

F32 = mybir.dt.float32
BF16 = mybir.dt.bfloat16
AF = mybir.ActivationFunctionType
ALU = mybir.AluOpType
AX = mybir.AxisListType

EPS = 1e-6
NEG = -30000.0


class Cfg:
    def __init__(self, D=4096, F=11008, H=16, KVH=4, NB=8, RH=8, NT=1024, PAST=512, DEPTH=2,
                 GRID_W=64, NPC=4, NSC=4, FG=4, stop=0, CC=False):
        self.stop = stop
        self.CC = CC
        self.NCORES = NPC + NSC
        self.NCOND = 1 + NSC
        self.CS = 9 * D // (NPC + NSC)
        self.D, self.F, self.H, self.KVH, self.NB, self.RH = D, F, H, KVH, NB, RH
        self.NT, self.PAST, self.DEPTH, self.GRID_W = NT, PAST, DEPTH, GRID_W
        self.NPC, self.NSC, self.FG = NPC, NSC, FG
        self.DC = D // 128
        self.AW, self.KVW, self.LW, self.RW = H * 128, KVH * 128, NB * 128, RH * 128
        self.INW = self.AW + 2 * self.KVW + 2 * self.LW + 4 * self.RW
        self.FC = F // 128
        self.NSEG = NT // 256
        self.NCH = NT // 128
        self.NH = NT // 512
        self.PCH = PAST // 128
        self.G = H // KVH
        self.oQ = 0
        self.oK = self.AW
        self.oV = self.oK + self.KVW
        self.oXB = self.oV + self.KVW
        self.oYB = self.oXB + self.LW
        self.oRQ = self.oYB + self.LW
        self.oRK = self.oRQ + self.RW
        self.oRV = self.oRK + self.RW
        self.oRG = self.oRV + self.RW
        assert self.AW + self.LW + self.RW == D


class StopBuild(Exception):
    pass


class Buf:
    __slots__ = ("w", "r", "name", "persistent")

    def __init__(self, name, persistent=False):
        self.w = None
        self.r = {}
        self.name = name
        self.persistent = persistent


class Sched:
    ENG = ("pe", "act", "dve", "pool", "sp")

    def __init__(self, nc, es):
        self.nc, self.es = nc, es
        self.prog = {e: [] for e in self.ENG}
        self.S = {e: es.enter_context(nc.semaphore("S_" + e)) for e in ("pe", "act", "dve", "pool")}
        self.cnt = {e: 0 for e in self.S}
        self.waited = {e: {} for e in self.ENG}
        self.semmap = {}
        self.dcnt = {}
        self.dsem = {}
        self.bar = []
        self.ninst = 0
        self.dead = False

    def _sem(self, key):
        if key not in self.semmap:
            s = self.es.enter_context(self.nc.semaphore("D%d" % len(self.semmap)))
            self.semmap[key] = s
            self.dcnt[id(s)] = 0
            self.dsem[id(s)] = s
        return self.semmap[key]

    def _deps(self, rd, wr):
        toks = []
        for b in rd:
            if b.w is not None:
                toks.append(b.w)
        for b in wr:
            if b.w is not None:
                toks.append(b.w)
            toks.extend(b.r.values())
        return toks

    def _wait(self, e, toks):
        p, wd = self.prog[e], self.waited[e]
        for (sem, val) in toks:
            if e == "pe" and sem is self.S["pe"]:
                continue
            k = id(sem)
            if wd.get(k, 0) < val:
                wd[k] = val
                p.append(lambda eng, sem=sem, val=val: eng.wait_ge(sem, val))
                self.ninst += 1

    def _mark(self, tok, rd, wr):
        k = id(tok[0])
        for b in rd:
            if k not in b.r or b.r[k][1] < tok[1]:
                b.r[k] = tok
        for b in wr:
            b.w = tok
            b.r = {}

    def op(self, e, fn, rd=(), wr=(), sig=True):
        if self.dead:
            return
        self._wait(e, self._deps(rd, wr))
        self.ninst += 1
        if sig:
            self.cnt[e] += 1
            sem = self.S[e]
            self.prog[e].append(lambda eng, fn=fn, sem=sem: fn(eng).then_inc(sem, 1))
            tok = (sem, self.cnt[e])
        else:
            self.prog[e].append(lambda eng, fn=fn: fn(eng))
            tok = (self.S[e], self.cnt[e] + 1)
        self._mark(tok, rd, wr)

    def dma(self, q, out, in_, rd=(), wr=(), key=None, fn=None):
        if self.dead:
            return
        sem = self._sem(key)
        k = id(sem)
        toks = self._deps(rd, wr)
        if self.dcnt[k] > 0:
            toks.append((sem, self.dcnt[k]))
        if q == "pool" and not all(b.persistent for b in list(rd) + list(wr)):
            toks = toks + self.bar
        self._wait(q, toks)
        self.dcnt[k] += 16
        if fn is not None:
            self.prog[q].append(lambda eng, fn=fn, sem=sem: fn(eng).then_inc(sem, 16))
        else:
            self.prog[q].append(lambda eng, out=out, in_=in_, sem=sem: eng.dma_start(out=out, in_=in_).then_inc(sem, 16))
        self.ninst += 1
        self._mark((sem, self.dcnt[k]), rd, wr)

    def barrier(self, engines=("pe", "act", "dve", "sp")):
        if self.dead:
            return
        toks = [(self.S[e], self.cnt[e]) for e in self.S if self.cnt[e] > 0]
        toks += [(self.dsem[k], v) for k, v in self.dcnt.items() if v > 0]
        self.bar = toks
        for e in engines:
            self._wait(e, toks)

    def final_wait(self):
        toks = [(self.dsem[k], v) for k, v in self.dcnt.items() if v > 0]
        toks += [(self.S[e], self.cnt[e]) for e in self.S if self.cnt[e] > 0]
        self._wait("sp", toks)


class TB:
    __slots__ = ("ap", "b")

    def __init__(self, ap, b):
        self.ap, self.b = ap, b


class Arena:
    def __init__(self, tensor, n, tag):
        self.t, self.n, self.tag, self.off = tensor, n, tag, 0
        self.live = []

    def reset(self):
        self.off = 0
        self.live = []

    def alloc(self, shape, name):
        size = int(np.prod(shape))
        size_al = (size + 15) // 16 * 16
        assert self.off + size_al <= self.n, "arena %s overflow: need %d have %d (%s)" % (self.tag, self.off + size_al, self.n, name)
        st, en = self.off, self.off + size_al
        self.off = en
        tb = None
        for (s0, e0, tb0, shp0) in self.live:
            if s0 == st and e0 == en and tb0.b.name == name and shp0 == tuple(shape):
                tb = tb0
        if tb is None:
            ap = self.t[:, st:st + size]
            if len(shape) == 2:
                ap = ap.rearrange("p (a b) -> p a b", a=shape[0])
            elif len(shape) == 3:
                ap = ap.rearrange("p (a b c) -> p a b c", a=shape[0], b=shape[1])
            tb = TB(ap, Buf(name))
            self.live.append((st, en, tb, tuple(shape)))
        nb = tb.b
        for (s0, e0, tb0, shp0) in self.live:
            if tb0 is not tb and s0 < en and st < e0:
                ob = tb0.b
                toks = list(ob.r.values()) + ([ob.w] if ob.w is not None else [])
                for tok in toks:
                    k = id(tok[0])
                    if k not in nb.r or nb.r[k][1] < tok[1]:
                        nb.r[k] = tok
        return tb


def build_nc(cfg):
    c = cfg
    D, DC, NT, NCH, NH, NSEG, PAST, PCH = c.D, c.DC, c.NT, c.NCH, c.NH, c.NSEG, c.PAST, c.PCH
    DEPTH, F, FC, INW, KVW, LW, RW, NB, RH, G = c.DEPTH, c.F, c.FC, c.INW, c.KVW, c.LW, c.RW, c.NB, c.RH, c.G
    NKEY = NT + PAST
    KCH = NCH + PCH
    nc = bass.Bass("TRN2", target_bir_lowering=False, num_devices=c.NPC + c.NSC)

    def din(name, shape):
        return nc.dram_tensor(name, list(shape), F32, kind="ExternalInput").ap()

    def dout(name, shape):
        return nc.dram_tensor(name, list(shape), F32, kind="ExternalOutput").ap()

    x_in = din("x", [NT, D])
    cond_in = din("cond", [1, D]) if not c.CC else None
    ck_in = din("ck", [DEPTH, PAST, KVW])
    cv_in = din("cv", [DEPTH, PAST, KVW])
    slru_in = din("slru", [DEPTH, 2 * NB, 128])
    sret_in = din("sret", [DEPTH, 2, RH, 128, 128])
    flag_in = din("flag", [128, 2])
    mb_in = din("mb", [128, (NCH + PCH) * NSEG])
    ropc_in = din("ropc", [NT, 128])
    rops_in = din("rops", [NT, 128])
    cst_in = din("cst", [128, 7 * 128 + 4])
    norm_g = din("norm_g", [DEPTH, 3 * DC, 128])
    NCORES, NCOND, CS = c.NCORES, c.NCOND, c.CS
    NRG = NCORES * NCOND
    if c.CC:
        assert NRG <= 128
        w_mod = din("w_mod", [DEPTH, D, CS])
        condall_in = din("condall", [NCOND, D])
        selj_in = din("selj", [NRG, NCORES])
        ag_in = nc.dram_tensor("ag_in", [NCOND, DEPTH * CS], F32, kind="Internal").ap()
        ag_out = nc.dram_tensor("ag_out", [NRG, DEPTH * CS], F32, kind="Internal").ap()
        agin_b, agout_b = Buf("agin", True), Buf("agout", True)
    else:
        w_mod = din("w_mod", [DEPTH, D, 9 * D])
    b_mod = din("b_mod", [DEPTH, 9 * DC, 128])
    ffn_wg = din("ffn_wg", [DEPTH, 2, D, F])
    ffn_wu = din("ffn_wu", [DEPTH, 2, D, F])
    ffn_wd = din("ffn_wd", [DEPTH, 2, F, D])
    w_in = din("w_in", [DEPTH, D, INW])
    q_gain = din("q_gain", [DEPTH, 128])
    k_gain = din("k_gain", [DEPTH, 128])
    conv_w = din("conv_w", [DEPTH, 4 * NB, 128])
    conv_b = din("conv_b", [DEPTH, NB, 128])
    lru_wa = din("lru_wa", [DEPTH, 2, NB, 128, 128])
    lru_ba = din("lru_ba", [DEPTH, 2 * NB, 128])
    lru_wi = din("lru_wi", [DEPTH, 2, NB, 128, 128])
    lru_bi = din("lru_bi", [DEPTH, 2 * NB, 128])
    lru_lam = din("lru_lam", [DEPTH, 2 * NB, 128])
    ret_logit = din("ret_logit", [DEPTH, 2 * RH])
    ret_g = din("ret_g", [DEPTH, RH, 128])
    w_out = din("w_out", [DEPTH, D, D])
    final_g = din("final_g", [DC, 128])

    y_out = dout("y", [NT, D])
    nk_out = dout("nk", [DEPTH, NT, KVW])
    nv_out = dout("nv", [DEPTH, NT, KVW])
    nlru_out = dout("nlru", [DEPTH, NSEG * 2 * NB, 128])
    nret_out = dout("nret", [DEPTH, NSEG, 2, RH, 128, 128])

    xT_s = nc.dram_tensor("xT_s", [DC, 128, NT], F32, kind="Internal").ap()
    a_s = nc.dram_tensor("a_s", [FC, 128, NT], BF16, kind="Internal").ap()
    mt_s = nc.dram_tensor("mt_s", [DC, 128, NT], BF16, kind="Internal").ap()
    modr_s = nc.dram_tensor("modr_s", [DEPTH, 9 * DC, 128], F32, kind="Internal").ap()
    xT_b = [Buf("xTs%d" % i, True) for i in range(DC)]
    a_b = [Buf("as%d" % i, True) for i in range(FC)]
    mt_b = [Buf("mts%d" % i, True) for i in range(DC)]
    modr_b = Buf("modrs", True)

    es = ExitStack()
    sch = Sched(nc, es)

    def sb(name, shape, dt):
        return es.enter_context(nc.sbuf_tensor(name, list(shape), dt))

    R1 = sb("R1", [128, DC * NT // 2], F32)
    HT = R1[:].bitcast(BF16).rearrange("p (c t) -> p c t", c=DC)
    ACC = R1[:].rearrange("p (c t) -> p c t", c=DC // 2)
    HT_b = Buf("HT", True)
    ACC_b = [Buf("ACC%d" % i, True) for i in range(DC // 2)]
    NSLOT = 4
    WPE = DC * 256
    R2 = sb("R2", [128, NSLOT * WPE], BF16)
    WS_b = [Buf("WS%d" % i, True) for i in range(NSLOT)]

    def wslot(i):
        return R2[:, i * WPE:(i + 1) * WPE]

    ARF_N = 8192
    ARB_N = 14336
    arf = Arena(sb("ARF", [128, ARF_N], F32), ARF_N, "f")
    arb = Arena(sb("ARB", [128, ARB_N], BF16), ARB_N, "b")
    PS = es.enter_context(nc.psum_tensor("PS", [128, 8, 512], F32))
    PS_b = [Buf("PS%d" % i, True) for i in range(8)]

    CST = sb("CST", [128, 7 * 128 + 4], F32)
    CST_b = Buf("CST", True)
    IDN = CST[:, 0:128]
    DPOS, DNEG, LOWM, UPM = (CST[:, 128 * i:128 * (i + 1)] for i in range(1, 5))
    IROW1, IROW2 = CST[:, 640:768], CST[:, 768:896]
    IVEC = CST[:, 896:900]
    ONES = sb("ONES", [128, 128], F32)
    ONESB = sb("ONESB", [128, 128], BF16)
    SMALL = sb("SMALL", [128, 16], F32)
    EPSC, ONEC = SMALL[:, 0:1], SMALL[:, 1:2]
    FLAG = SMALL[:, 2:3]
    CONST_b = Buf("CONST", True)
    MODT = sb("MODT", [128, DEPTH, 9 * DC], F32)
    NGT = sb("NGT", [128, DEPTH, 3 * DC], F32)
    GS = sb("GS", [128, DEPTH, 3 * DC], F32)
    GATE = sb("GATE", [128, DEPTH, 3 * DC], F32)
    FGT = sb("FGT", [128, DC], F32)
    ZSH = sb("ZSH", [128, DC], F32)
    MOD_b = Buf("MOD", True)
    ST = sb("ST", [128, DC, c.NCOND if c.CC else 1], BF16)
    SELJ = sb("SELJ", [128, c.NCORES], F32)
    MB = sb("MB", [128, NCH + PCH, NSEG], F32)
    TAB_b = Buf("TAB", True)
    NLV = 4 * NB + NB + 2 * NB * 4 + RH
    LV = sb("LV", [128, NLV], F32)
    LV_b = Buf("LV", True)
    oCW, oCB = 0, 4 * NB
    oBA, oBI, oLAM, oSL, oRG_ = 5 * NB, 7 * NB, 9 * NB, 11 * NB, 13 * NB
    SC8 = sb("SC8", [128, 2 * NB], F32)
    GAINQ = sb("GAINQ", [128, 128], F32)
    GAINK = sb("GAINK", [128, 128], F32)
    LG = sb("LG", [128, 2 * RH], F32)
    KDEC = sb("KDEC", [128, RH, 2], F32)
    CDEC = sb("CDEC", [128, RH, 2], F32)
    LST = sb("LST", [128, NSEG * 2 * NB], F32)
    LST_b = Buf("LST", True)
    RET_b = Buf("RETTAB", True)

    def mm(out, lhsT, rhs, start, stop, rd, wr):
        sch.op("pe", lambda e: e.matmul(out, lhsT, rhs, start=start, stop=stop), rd, wr, sig=stop)

    def tr(out, in_, idn, rd, wr):
        sch.op("pe", lambda e: e.transpose(out, in_, idn), rd, wr)

    def act(out, in_, func, rd, wr, bias=None, scale=None):
        kw = {}
        if bias is not None:
            kw["bias"] = bias
        if scale is not None:
            kw["scale"] = scale
        sch.op("act", lambda e: e.activation(out=out, in_=in_, func=func, **kw), rd, wr)

    def tt(out, in0, in1, op, rd, wr, eng="dve"):
        sch.op(eng, lambda e: e.tensor_tensor(out=out, in0=in0, in1=in1, op=op), rd, wr)

    def ts(out, in0, s1, op0, rd, wr, s2=None, op1=None, eng="dve"):
        if op1 is None:
            sch.op(eng, lambda e: e.tensor_scalar(out=out, in0=in0, scalar1=s1, scalar2=None, op0=op0), rd, wr)
        else:
            sch.op(eng, lambda e: e.tensor_scalar(out=out, in0=in0, scalar1=s1, scalar2=s2, op0=op0, op1=op1), rd, wr)

    def stt(out, in0, scalar, in1, op0, op1, rd, wr, eng="dve"):
        sch.op(eng, lambda e: e.scalar_tensor_tensor(out=out, in0=in0, scalar=scalar, in1=in1, op0=op0, op1=op1), rd, wr)

    def cpv(out, in_, rd, wr):
        sch.op("dve", lambda e: e.tensor_copy(out=out, in_=in_), rd, wr)

    def cpa(out, in_, rd, wr):
        act(out, in_, AF.Identity, rd, wr)

    def recip(out, in_, rd, wr):
        sch.op("dve", lambda e: e.reciprocal(out=out, in_=in_), rd, wr)

    def mset(ap, val, wr, eng="dve"):
        sch.op(eng, lambda e: e.memset(ap, val), (), wr)

    def ld(out, in_, rd, wr, key):
        sch.dma("sp", out, in_, rd, wr, key)

    def ldc(out, in_, rd, wr, key):
        sch.dma("pool", out, in_, rd, wr, key)

    ckpt = [0]

    def checkpoint(name=""):
        ckpt[0] += 1
        if c.stop:
            print("ckpt", ckpt[0], name)
        if c.stop and ckpt[0] >= c.stop:
            sch.dead = True

    def phase():
        checkpoint()
        sch.barrier()
        arf.reset()
        arb.reset()

    def bank(i, lo=0, hi=512):
        return PS[:, i, lo:hi]

    ld(CST[:], cst_in, (), [CST_b], "cst")
    mset(ONES[:], 1.0, [CONST_b])
    mset(ONESB[:], 1.0, [CONST_b])
    mset(SMALL[:, 0:1], EPS, [CONST_b])
    mset(SMALL[:, 1:2], 1.0, [CONST_b])
    mset(ZSH[:], 0.0, [CONST_b])
    ld(SMALL[:, 2:4], flag_in, (), [CONST_b], "flag")
    ld(MB[:].rearrange("p k s -> p (k s)"), mb_in, (), [TAB_b], "mb")

    def rows_to_cols(dst_ap, rows_dram, R, rd_b, wr_b, add_dram=None, add_b=None, psb=7):
        t = arf.alloc([128], "rtc_a")
        ld(t.ap[0:R, :], rows_dram, rd_b, [t.b], "rtc_a")
        if add_dram is not None:
            t2 = arf.alloc([128], "rtc_b")
            ld(t2.ap[0:R, :], add_dram, add_b, [t2.b], "rtc_b")
            tt(t.ap[0:R, :], t.ap[0:R, :], t2.ap[0:R, :], ALU.add, [t.b, t2.b], [t.b])
        tr(bank(psb, 0, R), t.ap[0:R, :], IDN[0:R, 0:R], [t.b, CST_b], [PS_b[psb]])
        cpv(dst_ap, bank(psb, 0, R), [PS_b[psb]], wr_b)

    if c.CC:
        phase()
        for r in range(NCOND):
            cr = arf.alloc([128], "cr%d" % r)
            ld(cr.ap[0:DC, :], condall_in[r:r + 1, :].rearrange("o (c p) -> (o c) p", p=128), (), [cr.b], "cr%d" % (r % 2))
            act(cr.ap[0:DC, :], cr.ap[0:DC, :], AF.Silu, [cr.b], [cr.b])
            tr(bank(6 + r % 2, 0, DC), cr.ap[0:DC, :], IDN[0:DC, 0:DC], [cr.b, CST_b], [PS_b[6 + r % 2]])
            cpv(ST[:, :, r], bank(6 + r % 2, 0, DC), [PS_b[6 + r % 2]], [MOD_b])
        ld(SELJ[0:NRG, :], selj_in, (), [MOD_b], "selj")
        stg_l = [arf.alloc([512], "modstg%d" % k) for k in range(4)]
        PW = 512 if CS % 512 == 0 else 384
        assert CS % PW == 0
        npp = CS // PW
        k = 0
        for l in range(DEPTH):
            for p in range(npp):
                s0 = (k % 2) * 2
                pb = k % 2
                stg = stg_l[k % 4]
                k += 1
                wv = R2[:, s0 * WPE:s0 * WPE + DC * PW].rearrange("p (c f) -> p c f", c=DC)
                ldc(wv, w_mod[l, :, p * PW:(p + 1) * PW].rearrange("(c p) f -> p c f", p=128), (), [WS_b[s0], WS_b[s0 + 1]], "ws%d" % s0)
                for dc in range(DC):
                    mm(PS[0:NCOND, pb, 0:PW], ST[:, dc, :], wv[:, dc, :], dc == 0, dc == DC - 1,
                       [MOD_b, WS_b[s0], WS_b[s0 + 1]], [PS_b[pb]])
                cpa(stg.ap[0:NCOND, 0:PW], PS[0:NCOND, pb, 0:PW], [PS_b[pb]], [stg.b])
                ld(ag_in[:, l * CS + p * PW:l * CS + (p + 1) * PW], stg.ap[0:NCOND, 0:PW], [stg.b], [agin_b], stg.b.name)
        sch.dma("pool", None, None, [agin_b], [agout_b], "agcc",
                fn=lambda eng: eng.collective_compute("AllGather", op=ALU.bypass, replica_groups=[list(range(NCORES))],
                                                      ins=[ag_in], outs=[ag_out]))
        RPR = CS // 128
        gts = [arf.alloc([512], "agt%d" % k) for k in range(2)]
        k = 0
        for t in range(DEPTH * npp):
            l, p = t // npp, t % npp
            gt = gts[t % 2]
            ld(gt.ap[0:NRG, 0:PW], ag_out[:, t * PW:(t + 1) * PW], [agout_b], [gt.b], gt.b.name)
            for j in range(NCORES):
                pb = 2 + k % 4
                stg = stg_l[k % 4]
                k += 1
                mm(PS[0:1, pb, 0:PW], SELJ[0:NRG, j:j + 1], gt.ap[0:NRG, 0:PW], True, True, [MOD_b, gt.b], [PS_b[pb]])
                cpa(stg.ap[0:1, 0:PW], PS[0:1, pb, 0:PW], [PS_b[pb]], [stg.b])
                ld(modr_s[l, j * RPR + p * (PW // 128):j * RPR + (p + 1) * (PW // 128), :].rearrange("(o r) f -> o (r f)", o=1), stg.ap[0:1, 0:PW],
                   [stg.b], [modr_b], stg.b.name)
    else:
        phase()
        cr = arf.alloc([128], "cr")
        ld(cr.ap[0:DC, :], cond_in.rearrange("o (c p) -> (o c) p", p=128), (), [cr.b], "cr")
        act(cr.ap[0:DC, :], cr.ap[0:DC, :], AF.Silu, [cr.b], [cr.b])
        tr(bank(7, 0, DC), cr.ap[0:DC, :], IDN[0:DC, 0:DC], [cr.b, CST_b], [PS_b[7]])
        cpv(ST[:, :, 0], bank(7, 0, DC), [PS_b[7]], [MOD_b])

        pass
    nmp = 9 * D // 512
    MSTG = sb("MSTG", [128, 2, 512], F32)
    MSTG_b = [Buf("MSTG0", True), Buf("MSTG1", True)]
    mctr = [0]

    def mod_panel(l, p, pair=None, pb=7):
        k = mctr[0]
        mctr[0] += 1
        s0 = (k % 2) * 2 if pair is None else pair
        wv = R2[:, s0 * WPE:(s0 + 2) * WPE].rearrange("p (c f) -> p c f", c=DC)
        ldc(wv, w_mod[l, :, p * 512:(p + 1) * 512].rearrange("(c p) f -> p c f", p=128), (), [WS_b[s0], WS_b[s0 + 1]], "ws%d" % s0)
        for dc in range(DC):
            mm(PS[0:1, pb, :], ST[:, dc, :], wv[:, dc, :], dc == 0, dc == DC - 1,
               [MOD_b, WS_b[s0], WS_b[s0 + 1]], [PS_b[pb]])
        cpa(MSTG[0:1, k % 2, :], PS[0:1, pb, :], [PS_b[pb]], [MSTG_b[k % 2]])
        ld(modr_s[l, p * 4:(p + 1) * 4, :].rearrange("(o r) f -> o (r f)", o=1), MSTG[0:1, k % 2, :], [MSTG_b[k % 2]], [modr_b], "mstg%d" % (k % 2))

    def mod_finalize(l, groups, with_ng):
        phase()
        for j3 in groups:
            R = 3 * DC
            rows_to_cols(MODT[:, l, j3 * R:(j3 + 1) * R], modr_s[l, j3 * R:(j3 + 1) * R, :], R, [modr_b], [MOD_b],
                         add_dram=b_mod[l, j3 * R:(j3 + 1) * R, :], add_b=())
        if with_ng:
            rows_to_cols(NGT[:, l, :], norm_g[l], 3 * DC, (), [MOD_b])
        for i in groups:
            stt(GS[:, l, i * DC:(i + 1) * DC], MODT[:, l, (3 * i + 1) * DC:(3 * i + 2) * DC], 1.0, NGT[:, l, i * DC:(i + 1) * DC],
                ALU.add, ALU.mult, [MOD_b], [MOD_b])
            gsc = 1.0 if i == 1 else 0.5
            ts(GATE[:, l, i * DC:(i + 1) * DC], MODT[:, l, (3 * i + 2) * DC:(3 * i + 3) * DC], gsc, ALU.mult, [MOD_b], [MOD_b])

    npg = nmp // 3
    if not c.CC:
        for p in range(npg):
            mod_panel(0, p)
        mod_queue = [(0, p) for p in range(npg, nmp)] + [(l, p) for l in range(1, DEPTH) for p in range(nmp)]
        mod_finalize(0, [0], True)
    else:
        mod_queue = []
        for l in range(DEPTH):
            mod_finalize(l, [0, 1, 2], True)
    rows_to_cols(FGT[:], final_g, DC, (), [MOD_b])

    def mod_pump(n, pair=None, pb=7):
        for _ in range(n):
            if mod_queue:
                l, p = mod_queue.pop(0)
                mod_panel(l, p, pair, pb)

    def mod_need(l, g):
        while mod_queue and (mod_queue[0][0], mod_queue[0][1] // npg) <= (l, g):
            l_, p_ = mod_queue.pop(0)
            mod_panel(l_, p_)

    phase()
    for tc in range(NCH):
        xin = arf.alloc([D], "xin")
        ld(xin.ap, x_in[tc * 128:(tc + 1) * 128, :], (), [xin.b], "xin")
        for c4 in range(DC // 4):
            pb = c4 % 2
            for k in range(4):
                cc = c4 * 4 + k
                tr(bank(pb, k * 128, (k + 1) * 128), xin.ap[:, cc * 128:(cc + 1) * 128], IDN, [xin.b, CST_b], [PS_b[pb]])
            stg = arf.alloc([4, 128], "xstg%d" % pb)
            cpv(stg.ap, bank(pb).rearrange("p (k t) -> p k t", k=4), [PS_b[pb]], [stg.b])
            ld(xT_s[c4 * 4:(c4 + 1) * 4, :, tc * 128:(tc + 1) * 128].rearrange("c p t -> p c t"), stg.ap,
               [stg.b], xT_b[c4 * 4:(c4 + 1) * 4], "xstg%d" % pb)
        arf.off = 0

    HTw_b = [Buf("HTw%d" % i, True) for i in range(DC)]

    def sumsq_rstd(getx, ntok, dview):
        accs = [arf.alloc([ntok], "ssacc%d" % k) for k in range(4)]
        sqs = [arf.alloc([ntok], "sq%d" % k) for k in range(4)]
        for cc in range(DC):
            xap, xb = getx(cc)
            k = cc % 4
            if cc < 4:
                act(accs[k].ap, xap, AF.Square, [xb], [accs[k].b])
            else:
                act(sqs[k].ap, xap, AF.Square, [xb], [sqs[k].b])
                tt(accs[k].ap, accs[k].ap, sqs[k].ap, ALU.add, [accs[k].b, sqs[k].b], [accs[k].b])
        tt(accs[0].ap, accs[0].ap, accs[1].ap, ALU.add, [accs[0].b, accs[1].b], [accs[0].b])
        tt(accs[2].ap, accs[2].ap, accs[3].ap, ALU.add, [accs[2].b, accs[3].b], [accs[2].b])
        tt(accs[0].ap, accs[0].ap, accs[2].ap, ALU.add, [accs[0].b, accs[2].b], [accs[0].b])
        mm(bank(6, 0, ntok), ONES[:], accs[0].ap, True, True, [accs[0].b, CONST_b], [PS_b[6]])
        rs = arf.alloc([ntok], "rstd")
        act(rs.ap, bank(6, 0, ntok), AF.Sqrt, [PS_b[6], CONST_b], [rs.b], bias=EPSC, scale=1.0 / dview)
        recip(rs.ap, rs.ap, [rs.b], [rs.b])
        return rs

    XHV = R2[:].bitcast(F32).rearrange("p (c t) -> p c t", c=DC)
    QC = DC // 4

    def norm_phase(gs_ap, sh_ap):
        phase()
        nth = NT // 512
        for th in range(nth):
            arf.off = 0
            for q4 in range(4):
                ld(XHV[:, q4 * QC:(q4 + 1) * QC, :], xT_s[q4 * QC:(q4 + 1) * QC, :, th * 512:(th + 1) * 512].rearrange("c p t -> p c t"),
                   xT_b[q4 * QC:(q4 + 1) * QC], [WS_b[q4]], "xhq%d" % q4)
            getx = lambda cc: (XHV[:, cc, :], WS_b[cc // QC])
            rs = sumsq_rstd(getx, 512, D)
            tmps = [arf.alloc([512], "ntmp%d" % k) for k in range(4)]
            for cc in range(DC):
                tmp = tmps[cc % 4]
                xap, xb = getx(cc)
                stt(tmp.ap, xap, gs_ap[:, cc:cc + 1], rs.ap, ALU.mult, ALU.mult, [xb, rs.b, MOD_b], [tmp.b])
                last = (th == nth - 1 and cc == DC - 1)
                act(HT[:, cc, th * 512:(th + 1) * 512], tmp.ap, AF.Identity, [tmp.b, MOD_b] + (HTw_b if last else []),
                    [HT_b] if last else [HTw_b[cc]], bias=sh_ap[:, cc:cc + 1])

    wctr = [0]

    def load_panel(w_dram_rows_cols):
        s = wctr[0] % NSLOT
        wctr[0] += 1
        v = wslot(s).rearrange("p (c f) -> p c f", c=DC)
        ldc(v, w_dram_rows_cols.rearrange("(c p) f -> p c f", p=128), (), [WS_b[s]], "ws%d" % s)
        return v, WS_b[s]

    def gemm_fm(wv, wb, col0, banks):
        for h in range(NH):
            for dc in range(DC):
                mm(bank(banks[h]), wv[:, dc, col0:col0 + 128], HT[:, dc, h * 512:(h + 1) * 512], dc == 0, dc == DC - 1,
                   [wb, HT_b], [PS_b[banks[h]]])

    def gemm_tm(wv, wb, tc, pb):
        for dc in range(DC):
            mm(bank(pb, 0, 256), HT[:, dc, tc * 128:(tc + 1) * 128], wv[:, dc, :], dc == 0, dc == DC - 1,
               [wb, HT_b], [PS_b[pb]])

    def ffn(l, i):
        gate_ap = GATE[:, l, i * DC:(i + 1) * DC]
        norm_phase(GS[:, l, i * DC:(i + 1) * DC], MODT[:, l, 3 * i * DC:(3 * i + 1) * DC])
        phase()
        wg, wu, wd = ffn_wg[l, i // 2], ffn_wu[l, i // 2], ffn_wd[l, i // 2]
        sgs = [arf.alloc([512], "sg%d" % k) for k in range(2)]
        asts = [arb.alloc([NT], "ast%d" % k) for k in range(2)]
        k = 0
        for p in range(F // 256):
            gv, gb = load_panel(wg[:, p * 256:(p + 1) * 256])
            uv, ub = load_panel(wu[:, p * 256:(p + 1) * 256])
            for fl in range(2):
                fc = p * 2 + fl
                base = (fc % 2) * 4
                gemm_fm(gv, gb, fl * 128, [base + h for h in range(NH)])
                gemm_fm(uv, ub, fl * 128, [base + 2 + h for h in range(NH)])
                ast = asts[fc % 2]
                for h in range(NH):
                    sg = sgs[k % 2]
                    k += 1
                    act(sg.ap, bank(base + h), AF.Silu, [PS_b[base + h]], [sg.b])
                    tt(ast.ap[:, h * 512:(h + 1) * 512], sg.ap, bank(base + 2 + h), ALU.mult, [sg.b, PS_b[base + 2 + h]], [ast.b])
                ld(a_s[fc], ast.ap, [ast.b], [a_b[fc]], "ast%d" % (fc % 2))
        phase()
        FG = c.FG
        ngr = (FC + FG - 1) // FG
        xts = [arf.alloc([NT], "xt%d" % k) for k in range(2)]
        ags = [arb.alloc([FG, NT], "ag%d" % k) for k in range(2)]
        HD = DC // 2
        HW = HD * 128
        assert FG * HW <= WPE
        gi = 0
        pumping = bool(mod_queue)
        for dh in range(2):
            for g in range(ngr):
                f0 = g * FG
                nf = min(FG, FC - f0)
                ag = ags[gi % 2]
                s0 = gi % (2 if pumping else NSLOT)
                gi += 1
                ld(ag.ap[:, 0:nf, :], a_s[f0:f0 + nf].rearrange("f p t -> p f t"), a_b[f0:f0 + nf], [ag.b], ag.b.name)
                wv = R2[:, s0 * WPE:s0 * WPE + FG * HW].rearrange("p (f d) -> p f d", f=FG)
                ldc(wv[:, 0:nf, :], wd[f0 * 128:(f0 + nf) * 128, dh * HW:(dh + 1) * HW].rearrange("(f p) d -> p f d", p=128),
                    (), [WS_b[s0]], "ws%d" % s0)
                k = 0
                for dc in range(HD):
                    for h in range(NH):
                        pb = k % (6 if pumping else 8)
                        k += 1
                        for fl in range(nf):
                            mm(bank(pb), wv[:, fl, dc * 128:(dc + 1) * 128], ag.ap[:, fl, h * 512:(h + 1) * 512], fl == 0, fl == nf - 1,
                               [WS_b[s0], ag.b], [PS_b[pb]])
                        dst = ACC[:, dc, h * 512:(h + 1) * 512]
                        if g == 0:
                            cpa(dst, bank(pb), [PS_b[pb]], [ACC_b[dc]])
                        else:
                            tt(dst, dst, bank(pb), ALU.add, [PS_b[pb], ACC_b[dc]], [ACC_b[dc]])
                if pumping:
                    mod_pump(1, pair=2, pb=6 + gi % 2)
            for dc in range(HD):
                cc = dh * HD + dc
                xt = xts[cc % 2]
                ld(xt.ap, xT_s[cc], [xT_b[cc]], [xt.b], xt.b.name)
                stt(xt.ap, ACC[:, dc, :], gate_ap[:, cc:cc + 1], xt.ap, ALU.mult, ALU.add, [ACC_b[dc], xt.b, MOD_b], [xt.b])
                ld(xT_s[cc], xt.ap, [xt.b], [xT_b[cc]], xt.b.name + "s")

    def layer_tables(l):
        phase()
        o = 0
        for (src, R) in ((conv_w[l], 4 * NB), (conv_b[l], NB), (lru_ba[l], 2 * NB), (lru_bi[l], 2 * NB),
                         (lru_lam[l], 2 * NB), (slru_in[l], 2 * NB), (ret_g[l], RH)):
            rows_to_cols(LV[:, o:o + R], src, R, (), [LV_b])
            o += R
        t = arf.alloc([2 * NB], "sc8t")
        act(t.ap, LV[:, oLAM:oLAM + 2 * NB], AF.Exp, [LV_b], [t.b], scale=-1.0)
        act(t.ap, t.ap, AF.Ln, [t.b, CONST_b], [t.b], bias=ONEC)
        ts(SC8[:], t.ap, -8.0, ALU.mult, [t.b], [LV_b])
        ld(GAINQ[:], q_gain[l:l + 1, :].partition_broadcast(128), (), [LV_b], "gq")
        ld(GAINK[:], k_gain[l:l + 1, :].partition_broadcast(128), (), [LV_b], "gk")
        ld(LG[:], ret_logit[l:l + 1, :].partition_broadcast(128), (), [RET_b], "lg")
        act(LG[:], LG[:], AF.Exp, [RET_b], [RET_b], scale=-1.0)
        act(LG[:], LG[:], AF.Ln, [RET_b, CONST_b], [RET_b], bias=ONEC)
        ts(LG[:], LG[:], -1.0, ALU.mult, [RET_b], [RET_b])
        for h in range(RH):
            lgf, lgb = LG[:, h:h + 1], LG[:, RH + h:RH + h + 1]
            act(KDEC[:, h, 0:1], IVEC[:, 0:1], AF.Exp, [CST_b, RET_b], [RET_b], scale=lgf)
            act(KDEC[:, h, 1:2], IVEC[:, 1:2], AF.Exp, [CST_b, RET_b], [RET_b], scale=lgb)
            act(CDEC[:, h, 0:1], IVEC[:, 2:3], AF.Exp, [CST_b, RET_b], [RET_b], scale=lgf)
            act(CDEC[:, h, 1:2], IVEC[:, 2:3], AF.Exp, [CST_b, RET_b], [RET_b], scale=lgb)
        ts(KDEC[:], KDEC[:], 128.0 ** -0.5, ALU.mult, [RET_b], [RET_b])

    def qk_evac(ps_ap, psb, gain_ap, ropc, rops, tc, store_dram=None, tag="q"):
        sq = arf.alloc([256], tag + "sq")
        act(sq.ap, ps_ap, AF.Square, [psb], [sq.b])
        ss = arf.alloc([2], tag + "ss")
        sch.op("dve", lambda e: e.tensor_reduce(out=ss.ap, in_=sq.ap.rearrange("p (h d) -> p h d", h=2), axis=AX.X, op=ALU.add),
               [sq.b], [ss.b])
        act(ss.ap, ss.ap, AF.Sqrt, [ss.b, CONST_b], [ss.b], bias=EPSC, scale=1.0 / 128)
        recip(ss.ap, ss.ap, [ss.b], [ss.b])
        kn = arf.alloc([256], tag + "kn")
        for h in range(2):
            stt(kn.ap[:, h * 128:(h + 1) * 128], ps_ap[:, h * 128:(h + 1) * 128], ss.ap[:, h:h + 1], gain_ap, ALU.mult, ALU.mult,
                [psb, ss.b, LV_b], [kn.b])
        if store_dram is not None:
            ld(store_dram, kn.ap, [kn.b], (), tag + "kns")
        kr = arf.alloc([256], tag + "kr")
        t2 = arf.alloc([256], tag + "t2")
        for h in range(2):
            xv = kn.ap[:, h * 128:(h + 1) * 128]
            tt(kr.ap[:, h * 128:(h + 1) * 128], xv, ropc.ap[:, tc, :], ALU.mult, [kn.b, ropc.b], [kr.b])
            x4 = xv.rearrange("p (a b d) -> p a b d", a=2, b=2)
            s4 = rops.ap[:, tc, :].rearrange("p (a b d) -> p a b d", a=2, b=2)
            o4 = t2.ap[:, h * 128:(h + 1) * 128].rearrange("p (a b d) -> p a b d", a=2, b=2)
            tt(o4[:, :, 0, :], x4[:, :, 1, :], s4[:, :, 0, :], ALU.mult, [kn.b, rops.b], [t2.b])
            tt(o4[:, :, 1, :], x4[:, :, 0, :], s4[:, :, 1, :], ALU.mult, [kn.b, rops.b], [t2.b])
        tt(kr.ap, kr.ap, t2.ap, ALU.add, [kr.b, t2.b], [kr.b])
        return kr

    def mixer(l):
        norm_phase(GS[:, l, DC:2 * DC], MODT[:, l, 3 * DC:4 * DC])
        layer_tables(l)
        win = w_in[l]
        scale = 128.0 ** -0.5
        for kp in range(KVW // 256):
            phase()
            ropc = arf.alloc([NCH, 128], "ropc")
            rops = arf.alloc([NCH, 128], "rops")
            ld(ropc.ap, ropc_in.rearrange("(c p) f -> p c f", p=128), (), [ropc.b], "ropc")
            ld(rops.ap, rops_in.rearrange("(c p) f -> p c f", p=128), (), [rops.b], "rops")
            KT = arb.alloc([2, NKEY], "KT")
            V = arb.alloc([KCH, 256], "V")
            QT = arb.alloc([G, NT], "QT")
            atts = [arb.alloc([NT], "att%d" % k) for k in range(2)]
            pts = [arb.alloc([512], "pt%d" % k) for k in range(4)]
            ldc(V.ap[:, NCH:KCH, :], cv_in[l, :, kp * 256:(kp + 1) * 256].rearrange("(c p) f -> p c f", p=128), (), [V.b], "Vc")
            for pc in range(PCH):
                ckt = arf.alloc([256], "ckt%d" % pc)
                ld(ckt.ap, ck_in[l, pc * 128:(pc + 1) * 128, kp * 256:(kp + 1) * 256], (), [ckt.b], "ckt%d" % (pc % 2))
                for h in range(2):
                    tr(bank(2 + h, 0, 128), ckt.ap[:, h * 128:(h + 1) * 128], IDN, [ckt.b, CST_b], [PS_b[2 + h]])
                    cpa(KT.ap[:, h, NT + pc * 128:NT + (pc + 1) * 128], bank(2 + h, 0, 128), [PS_b[2 + h]], [KT.b])
            mark = arf.off
            checkpoint("att: after cacheK")
            kv_, kb_ = load_panel(win[:, c.oK + kp * 256:c.oK + (kp + 1) * 256])
            for tc in range(NCH):
                pb = tc % 2
                arf.off = mark
                gemm_tm(kv_, kb_, tc, pb)
                kr = qk_evac(bank(pb, 0, 256), PS_b[pb], GAINK[:], ropc, rops, tc,
                             store_dram=nk_out[l, tc * 128:(tc + 1) * 128, kp * 256:(kp + 1) * 256], tag="k")
                for h in range(2):
                    tr(bank(2 + h, 0, 128), kr.ap[:, h * 128:(h + 1) * 128], IDN, [kr.b, CST_b], [PS_b[2 + h]])
                    cpa(KT.ap[:, h, tc * 128:(tc + 1) * 128], bank(2 + h, 0, 128), [PS_b[2 + h]], [KT.b])
            checkpoint("att: after K panel")
            mod_pump(1)
            vv_, vb_ = load_panel(win[:, c.oV + kp * 256:c.oV + (kp + 1) * 256])
            for tc in range(NCH):
                pb = tc % 2
                arf.off = mark
                gemm_tm(vv_, vb_, tc, pb)
                vf = arf.alloc([256], "vf")
                cpa(vf.ap, bank(pb, 0, 256), [PS_b[pb]], [vf.b])
                ld(nv_out[l, tc * 128:(tc + 1) * 128, kp * 256:(kp + 1) * 256], vf.ap, [vf.b], (), "vfs")
                cpv(V.ap[:, tc, :], vf.ap, [vf.b], [V.b])
            checkpoint("att: after V panel")
            mod_pump(1)
            for gl in range(2):
                g = kp * 2 + gl
                for qp in range(G // 2):
                    qv_, qb_ = load_panel(win[:, c.oQ + (g * G + qp * 2) * 128:c.oQ + (g * G + qp * 2 + 2) * 128])
                    for tc in range(NCH):
                        pb = tc % 2
                        arf.off = mark
                        gemm_tm(qv_, qb_, tc, pb)
                        qr = qk_evac(bank(pb, 0, 256), PS_b[pb], GAINQ[:], ropc, rops, tc, tag="q")
                        for h in range(2):
                            tr(bank(2 + h, 0, 128), qr.ap[:, h * 128:(h + 1) * 128], IDN, [qr.b, CST_b], [PS_b[2 + h]])
                            cpa(QT.ap[:, qp * 2 + h, tc * 128:(tc + 1) * 128], bank(2 + h, 0, 128), [PS_b[2 + h]], [QT.b])
                    mod_pump(1)
                arf.off = mark
                checkpoint("att: after Q panels")
                rcs = [arf.alloc([512], "rc0"), arf.alloc([512], "rc1")]
                for hq0 in range(0, G, 2):
                    for hh in range(NH):
                        def score(ci, kc, hh=hh, hq0=hq0):
                            cb = 4 + 2 * ci + (kc % 2)
                            mm(bank(cb), KT.ap[:, gl, kc * 128:(kc + 1) * 128], QT.ap[:, hq0 + ci, hh * 512:(hh + 1) * 512], True, True,
                               [KT.b, QT.b], [PS_b[cb]])
                        for ci in range(2):
                            score(ci, 0)
                        for kc in range(KCH):
                            if kc + 1 < KCH:
                                for ci in range(2):
                                    score(ci, kc + 1)
                            for ci in range(2):
                                cb = 4 + 2 * ci + (kc % 2)
                                pt = pts[ci * 2 + kc % 2]
                                for q2 in range(2):
                                    act(pt.ap[:, q2 * 256:(q2 + 1) * 256], bank(cb, q2 * 256, (q2 + 1) * 256), AF.Exp, [PS_b[cb], TAB_b], [pt.b],
                                        scale=scale, bias=MB[:, kc, hh * 2 + q2:hh * 2 + q2 + 1])
                            for ci in range(2):
                                pt = pts[ci * 2 + kc % 2]
                                mm(bank(2 * ci), V.ap[:, kc, gl * 128:(gl + 1) * 128], pt.ap, kc == 0, kc == KCH - 1, [V.b, pt.b], [PS_b[2 * ci]])
                                mm(bank(2 * ci + 1), ONESB[:], pt.ap, kc == 0, kc == KCH - 1, [CONST_b, pt.b], [PS_b[2 * ci + 1]])
                        for ci in range(2):
                            recip(rcs[ci].ap, bank(2 * ci + 1), [PS_b[2 * ci + 1]], [rcs[ci].b])
                            tt(atts[ci].ap[:, hh * 512:(hh + 1) * 512], bank(2 * ci), rcs[ci].ap, ALU.mult, [PS_b[2 * ci], rcs[ci].b], [atts[ci].b])
                    for ci in range(2):
                        ld(mt_s[g * G + hq0 + ci], atts[ci].ap, [atts[ci].b], [mt_b[g * G + hq0 + ci]], "atts%d" % ci)
        for p in range(LW // 256):
            phase()
            WAI = arb.alloc([8, 128], "WAI")
            for r in range(2):
                ldc(WAI.ap[:, r * 2:(r + 1) * 2, :], lru_wa[l, r, p * 2:(p + 1) * 2].rearrange("n k j -> k n j"), (), [WAI.b], "wa%d" % r)
                ldc(WAI.ap[:, 4 + r * 2:4 + (r + 1) * 2, :], lru_wi[l, r, p * 2:(p + 1) * 2].rearrange("n k j -> k n j"), (), [WAI.b], "wi%d" % r)
            xv_, xb_ = load_panel(win[:, c.oXB + p * 256:c.oXB + (p + 1) * 256])
            yv_, yb_ = load_panel(win[:, c.oYB + p * 256:c.oYB + (p + 1) * 256])
            for nl in range(2):
                n = p * 2 + nl
                arf.off = 0
                arb.off = 8 * 128
                gemm_fm(xv_, xb_, nl * 128, [0, 1][:NH])
                HS = arf.alloc([NSEG, 256], "HS")
                XBP = arf.alloc([NSEG, 259], "XBP")
                for h in range(NH):
                    cpa(XBP.ap[:, 2 * h:2 * h + 2, 2:258], bank(h).rearrange("p (s t) -> p s t", s=2), [PS_b[h]], [XBP.b])
                mset(XBP.ap[:, 0, 0:2], 0.0, [XBP.b])
                mset(XBP.ap[:, NSEG - 1, 258:259], 0.0, [XBP.b])
                ts(XBP.ap[:, 1:NSEG, 0:2], XBP.ap[:, 0:NSEG - 1, 256:258], FLAG, ALU.mult, [XBP.b, CONST_b], [XBP.b])
                ts(XBP.ap[:, 0:NSEG - 1, 258:259], XBP.ap[:, 1:NSEG, 2:3], FLAG, ALU.mult, [XBP.b, CONST_b], [XBP.b])
                XC = arf.alloc([NSEG, 256], "XC")
                cw = lambda j: LV[:, oCW + j * NB + n:oCW + j * NB + n + 1]
                act(XC.ap, XBP.ap[:, :, 2:258], AF.Identity, [XBP.b, LV_b], [XC.b], bias=LV[:, oCB + n:oCB + n + 1], scale=cw(2))
                for (j, o0) in ((0, 0), (1, 1), (3, 3)):
                    stt(XC.ap, XBP.ap[:, :, o0:o0 + 256], cw(j), XC.ap, ALU.mult, ALU.add, [XBP.b, XC.b, LV_b], [XC.b])
                XCB = arb.alloc([NT], "XCB")
                xcf = XC.ap.rearrange("p s t -> p (s t)")
                cpa(XCB.ap, xcf, [XC.b], [XCB.b])
                A = arf.alloc([NSEG, 256], "A")
                U = arf.alloc([NSEG, 256], "U")
                Hb = arf.alloc([NSEG, 256], "Hb")
                MU = arf.alloc([NT], "MU")
                cr_ = arf.alloc([1], "carry")
                Af = A.ap.rearrange("p s t -> p (s t)")
                Uf = U.ap.rearrange("p s t -> p (s t)")
                for r in range(2):
                    ia, ii = r * 2 + nl, 4 + r * 2 + nl
                    for h in range(NH):
                        mm(bank(2 + h), WAI.ap[:, ia, :], XCB.ap[:, h * 512:(h + 1) * 512], True, True, [WAI.b, XCB.b], [PS_b[2 + h]])
                        mm(bank(4 + h), WAI.ap[:, ii, :], XCB.ap[:, h * 512:(h + 1) * 512], True, True, [WAI.b, XCB.b], [PS_b[4 + h]])
                    Hd = HS if r == 0 else Hb
                    for h in range(NH):
                        sl = slice(h * 512, (h + 1) * 512)
                        act(Af[:, sl], bank(2 + h), AF.Sigmoid, [PS_b[2 + h], LV_b], [A.b], bias=LV[:, oBA + r * NB + n:oBA + r * NB + n + 1])
                        act(Uf[:, sl], bank(4 + h), AF.Sigmoid, [PS_b[4 + h], LV_b], [U.b], bias=LV[:, oBI + r * NB + n:oBI + r * NB + n + 1])
                    act(Af, Af, AF.Exp, [A.b, LV_b], [A.b], scale=SC8[:, r * NB + n:r * NB + n + 1])
                    tt(Uf, Uf, xcf, ALU.mult, [U.b, XC.b], [U.b])
                    tt(MU.ap, Af, Af, ALU.mult, [A.b], [MU.b])
                    act(MU.ap, MU.ap, AF.Sqrt, [MU.b, CONST_b], [MU.b], bias=ONEC, scale=-1.0)
                    tt(Uf, Uf, MU.ap, ALU.mult, [U.b, MU.b], [U.b])
                    order = range(NSEG) if r == 0 else range(NSEG - 1, -1, -1)
                    for si, s in enumerate(order):
                        if si == 0:
                            init = LV[:, oSL + r * NB + n:oSL + r * NB + n + 1]
                            ird = [LV_b]
                        else:
                            init = cr_.ap
                            ird = [cr_.b]

                        def view(tb, s=s, r=r):
                            a = tb.ap[:, s, :]
                            if r == 0:
                                return a
                            return bass.AP(a.tensor, a.offset + 255, [[a.ap[0][0], 128], [-1, 256]])
                        o_, d0, d1 = view(Hd), view(A), view(U)
                        sch.op("dve", lambda e, o_=o_, d0=d0, d1=d1, init=init: e.tensor_tensor_scan(
                            out=o_, data0=d0, data1=d1, initial=init, op0=ALU.mult, op1=ALU.add), [A.b, U.b] + ird, [Hd.b])
                        last = Hd.ap[:, s, 255:256] if r == 0 else Hd.ap[:, s, 0:1]
                        col = (s * 2 + r) * NB + n
                        cpa(LST[:, col:col + 1], last, [Hd.b], [LST_b])
                        if si < NSEG - 1:
                            ts(cr_.ap, last, FLAG, ALU.mult, [Hd.b, CONST_b], [cr_.b])
                tt(HS.ap, HS.ap, Hb.ap, ALU.add, [HS.b, Hb.b], [HS.b])
                gemm_fm(yv_, yb_, nl * 128, [0, 1][:NH])
                lr = arb.alloc([NT], "lr")
                hsf = HS.ap.rearrange("p s t -> p (s t)")
                t = arf.alloc([512], "gl_t")
                for h in range(NH):
                    sl = slice(h * 512, (h + 1) * 512)
                    act(t.ap, bank(h), AF.Square, [PS_b[h]], [t.b])
                    ts(t.ap, t.ap, 0.044715, ALU.mult, [t.b], [t.b], s2=1.0, op1=ALU.add)
                    tt(t.ap, t.ap, bank(h), ALU.mult, [t.b, PS_b[h]], [t.b])
                    act(t.ap, t.ap, AF.Sigmoid, [t.b], [t.b], scale=1.5957691216057308)
                    tt(t.ap, t.ap, bank(h), ALU.mult, [t.b, PS_b[h]], [t.b])
                    tt(lr.ap[:, sl], t.ap, hsf[:, sl], ALU.mult, [t.b, HS.b], [lr.b])
                ld(mt_s[c.AW // 128 + n], lr.ap, [lr.b], [mt_b[c.AW // 128 + n]], "lrs")
        phase()
        NR = NSEG * 2 * NB
        tr(PS[0:NR, 7, 0:128], LST[:], IDN, [LST_b, CST_b], [PS_b[7]])
        lso = arf.alloc([128], "lso")
        cpv(lso.ap[0:NR, :], PS[0:NR, 7, 0:128], [PS_b[7]], [lso.b])
        ld(nlru_out[l], lso.ap[0:NR, :], [lso.b], (), "lsos")
        kscale = 128.0 ** -0.5
        for p in range(RW // 256):
            phase()
            qv_, qb_ = load_panel(win[:, c.oRQ + p * 256:c.oRQ + (p + 1) * 256])
            kv_, kb_ = load_panel(win[:, c.oRK + p * 256:c.oRK + (p + 1) * 256])
            vv_, vb_ = load_panel(win[:, c.oRV + p * 256:c.oRV + (p + 1) * 256])
            gv_, gb_ = load_panel(win[:, c.oRG + p * 256:c.oRG + (p + 1) * 256])
            for hl in range(2):
                h = p * 2 + hl
                arf.off = 0
                arb.off = 0
                hs = slice(hl * 128, (hl + 1) * 128)
                QR = arb.alloc([NT], "QR")
                QDF = arb.alloc([NT], "QDF")
                QDB = arb.alloc([NT], "QDB")
                KR = arb.alloc([NT], "KR")
                KDF = arb.alloc([NCH, 128], "KDF")
                KDB = arb.alloc([NCH, 128], "KDB")
                VR = arb.alloc([NCH, 128], "VR")
                SGR = arf.alloc([NT], "SGR")
                mcomb = arf.alloc([128], "mcomb")
                qdec = arf.alloc([2, 128], "qdec")
                t2 = arf.alloc([128], "mc2")
                lgf, lgb = LG[:, h:h + 1], LG[:, RH + h:RH + h + 1]
                act(mcomb.ap, DPOS, AF.Exp, [CST_b, RET_b], [mcomb.b], scale=lgf)
                tt(mcomb.ap, mcomb.ap, LOWM, ALU.mult, [mcomb.b, CST_b], [mcomb.b])
                act(t2.ap, DNEG, AF.Exp, [CST_b, RET_b], [t2.b], scale=lgb)
                tt(t2.ap, t2.ap, UPM, ALU.mult, [t2.b, CST_b], [t2.b])
                tt(mcomb.ap, mcomb.ap, t2.ap, ALU.add, [mcomb.b, t2.b], [mcomb.b])
                ts(mcomb.ap, mcomb.ap, kscale, ALU.mult, [mcomb.b], [mcomb.b])
                act(qdec.ap[:, 0, :], IROW1, AF.Exp, [CST_b, RET_b], [qdec.b], scale=lgf)
                act(qdec.ap[:, 1, :], IROW2, AF.Exp, [CST_b, RET_b], [qdec.b], scale=lgb)
                gemm_fm(qv_, qb_, hl * 128, [0, 1][:NH])
                for hh in range(NH):
                    sl = slice(hh * 512, (hh + 1) * 512)
                    cpa(QR.ap[:, sl], bank(hh), [PS_b[hh]], [QR.b])
                    b3 = bank(hh).rearrange("p (c i) -> p c i", c=4)
                    tt(QDF.ap[:, sl].rearrange("p (c i) -> p c i", c=4), b3, qdec.ap[:, 0:1, :].broadcast_to([128, 4, 128]), ALU.mult,
                       [PS_b[hh], qdec.b, QR.b], [QDF.b])
                    tt(QDB.ap[:, sl].rearrange("p (c i) -> p c i", c=4), b3, qdec.ap[:, 1:2, :].broadcast_to([128, 4, 128]), ALU.mult,
                       [PS_b[hh], qdec.b, QR.b], [QDB.b])
                gemm_fm(kv_, kb_, hl * 128, [2, 3][:NH])
                for hh in range(NH):
                    cpa(KR.ap[:, hh * 512:(hh + 1) * 512], bank(2 + hh), [PS_b[2 + hh]], [KR.b])
                gemm_fm(gv_, gb_, hl * 128, [0, 1][:NH])
                for hh in range(NH):
                    act(SGR.ap[:, hh * 512:(hh + 1) * 512], bank(hh), AF.Silu, [PS_b[hh]], [SGR.b])
                for t4 in range(NCH // 4):
                    for (wv_, wb_, pb) in ((kv_, kb_, 2 + t4 % 2), (vv_, vb_, 4 + t4 % 2)):
                        for k4 in range(4):
                            tc = t4 * 4 + k4
                            for dc in range(DC):
                                mm(bank(pb, k4 * 128, (k4 + 1) * 128), HT[:, dc, tc * 128:(tc + 1) * 128], wv_[:, dc, hs], dc == 0, dc == DC - 1,
                                   [wb_, HT_b], [PS_b[pb]])
                    kb3 = bank(2 + t4 % 2).rearrange("p (c d) -> p c d", c=4)
                    act(KDF.ap[:, t4 * 4:(t4 + 1) * 4, :], kb3, AF.Identity, [PS_b[2 + t4 % 2], RET_b], [KDF.b], scale=KDEC[:, h, 0:1])
                    ts(KDB.ap[:, t4 * 4:(t4 + 1) * 4, :], kb3, KDEC[:, h, 1:2], ALU.mult, [PS_b[2 + t4 % 2], RET_b, KDF.b], [KDB.b])
                    cpa(VR.ap[:, t4 * 4:(t4 + 1) * 4, :], bank(4 + t4 % 2).rearrange("p (c d) -> p c d", c=4), [PS_b[4 + t4 % 2]], [VR.b])
                SB_ = [arb.alloc([NCH, 128], "SFB"), arb.alloc([NCH, 128], "SBB")]
                for r in range(2):
                    S = arf.alloc([128], "S%d" % r)
                    ld(S.ap, sret_in[l, r, h], (), [S.b], "S%d" % r)
                    KD = KDF if r == 0 else KDB
                    order = list(range(NCH)) if r == 0 else list(range(NCH - 1, -1, -1))
                    sos = [arf.alloc([128], "so%d_%d" % (r, k)) for k in range(2)]
                    for ci, cc in enumerate(order):
                        cpa(SB_[r].ap[:, cc, :], S.ap, [S.b], [SB_[r].b])
                        pb = 2 + (ci % 2)
                        mm(bank(pb, 0, 128), KD.ap[:, cc, :], VR.ap[:, cc, :], True, True, [KD.b, VR.b], [PS_b[pb]])
                        stt(S.ap, S.ap, CDEC[:, h, r:r + 1], bank(pb, 0, 128), ALU.mult, ALU.add, [S.b, PS_b[pb], RET_b], [S.b])
                        boundary = (cc % 2 == 1) if r == 0 else (cc % 2 == 0)
                        if boundary:
                            seg = cc // 2
                            so = sos[seg % 2]
                            cpv(so.ap, S.ap, [S.b], [so.b])
                            ld(nret_out[l, seg, r, h], so.ap, [so.b], (), "sos%d_%d" % (r, seg % 2))
                            if ci < NCH - 1:
                                ts(S.ap, S.ap, FLAG, ALU.mult, [S.b, CONST_b], [S.b])
                PT = [arb.alloc([128], "rpt%d" % k) for k in range(2)]
                for cc in range(NCH):
                    ts_ = slice(cc * 128, (cc + 1) * 128)
                    pb = 2 + (cc % 2)
                    mm(bank(pb, 0, 128), KR.ap[:, ts_], QR.ap[:, ts_], True, True, [KR.b, QR.b], [PS_b[pb]])
                    pt = PT[cc % 2]
                    tt(pt.ap, bank(pb, 0, 128), mcomb.ap, ALU.mult, [PS_b[pb], mcomb.b], [pt.b])
                    ob = 4 + cc // 4
                    osl = bank(ob, (cc % 4) * 128, (cc % 4 + 1) * 128)
                    mm(osl, VR.ap[:, cc, :], pt.ap, True, False, [VR.b, pt.b], [PS_b[ob]])
                    mm(osl, SB_[0].ap[:, cc, :], QDF.ap[:, ts_], False, False, [SB_[0].b, QDF.b], [PS_b[ob]])
                    mm(osl, SB_[1].ap[:, cc, :], QDB.ap[:, ts_], False, True, [SB_[1].b, QDB.b], [PS_b[ob]])
                rt = arb.alloc([NT], "rt")
                for hh in range(NH):
                    sl = slice(hh * 512, (hh + 1) * 512)
                    sq = arf.alloc([512], "rsq%d" % hh)
                    act(sq.ap, bank(4 + hh), AF.Square, [PS_b[4 + hh]], [sq.b])
                    mm(bank(6 + hh % 2), ONES[:], sq.ap, True, True, [sq.b, CONST_b], [PS_b[6 + hh % 2]])
                    rs = arf.alloc([512], "rrs%d" % hh)
                    act(rs.ap, bank(6 + hh % 2), AF.Sqrt, [PS_b[6 + hh % 2], CONST_b], [rs.b], bias=EPSC, scale=1.0 / 128)
                    recip(rs.ap, rs.ap, [rs.b], [rs.b])
                    tt(rs.ap, rs.ap, bank(4 + hh), ALU.mult, [rs.b, PS_b[4 + hh]], [rs.b])
                    stt(rt.ap[:, sl], rs.ap, LV[:, oRG_ + h:oRG_ + h + 1], SGR.ap[:, sl], ALU.mult, ALU.mult, [rs.b, LV_b, SGR.b], [rt.b])
                ci_ = (c.AW + c.LW) // 128 + h
                ld(mt_s[ci_], rt.ap, [rt.b], [mt_b[ci_]], "rts")
        phase()
        for q4 in range(4):
            c0, c1 = q4 * DC // 4, (q4 + 1) * DC // 4
            ld(HT[:, c0:c1, :], mt_s[c0:c1].rearrange("c p t -> p c t"), mt_b[c0:c1], [HT_b], "mtq%d" % q4)
        xts = [arf.alloc([NT], "xt%d" % k) for k in range(2)]
        gate_ap = GATE[:, l, DC:2 * DC]
        for p in range(D // 256):
            wv_, wb_ = load_panel(w_out[l][:, p * 256:(p + 1) * 256])
            for cl in range(2):
                cc = p * 2 + cl
                bs = [(cc % 2) * 2 + h for h in range(NH)]
                gemm_fm(wv_, wb_, cl * 128, bs)
                xt = xts[cc % 2]
                ld(xt.ap, xT_s[cc], [xT_b[cc]], [xt.b], xt.b.name)
                for h in range(NH):
                    sl = slice(h * 512, (h + 1) * 512)
                    stt(xt.ap[:, sl], bank(bs[h]), gate_ap[:, cc:cc + 1], xt.ap[:, sl], ALU.mult, ALU.add, [PS_b[bs[h]], xt.b, MOD_b], [xt.b])
                ld(xT_s[cc], xt.ap, [xt.b], [xT_b[cc]], xt.b.name + "s")
            mod_pump(1)

    def main_seq():
        for l in range(DEPTH):
            if l > 0 and not c.CC:
                mod_need(l, 2)
                mod_finalize(l, [0, 1, 2], True)
            ffn(l, 0)
            if l == 0 and not c.CC:
                mod_need(0, 1)
                mod_finalize(0, [1], False)
            mixer(l)
            if l == 0 and not c.CC:
                mod_need(0, 2)
                mod_finalize(0, [2], False)
            ffn(l, 2)
        final_norm()

    def final_norm():
      phase()
      for tc in range(NCH):
        arf.off = 0
        xh = arf.alloc([DC, 128], "fxh")
        ld(xh.ap, xT_s[:, :, tc * 128:(tc + 1) * 128].rearrange("c p t -> p c t"), xT_b, [xh.b], "fxh")
        rs = sumsq_rstd(lambda cc, xh=xh: (xh.ap[:, cc, :], xh.b), 128, D)
        ftmps = [arf.alloc([128], "ftmp0"), arf.alloc([128], "ftmp1")]
        HC = max(DC // 2, 4)
        for half in range(DC // HC):
            yo = arf.alloc([HC * 128], "yo%d" % half)
            for c4 in range(HC // 4):
                pb = c4 % 2
                for k in range(4):
                    cc = half * HC + c4 * 4 + k
                    tmp = ftmps[cc % 2]
                    stt(tmp.ap, xh.ap[:, cc, :], FGT[:, cc:cc + 1], rs.ap, ALU.mult, ALU.mult, [xh.b, rs.b, MOD_b], [tmp.b])
                    tr(bank(pb, k * 128, (k + 1) * 128), tmp.ap, IDN, [tmp.b, CST_b], [PS_b[pb]])
                cpa(yo.ap[:, c4 * 512:(c4 + 1) * 512], bank(pb), [PS_b[pb]], [yo.b])
            ld(y_out[tc * 128:(tc + 1) * 128, half * HC * 128:(half + 1) * HC * 128], yo.ap, [yo.b], (), "yos%d" % half)
            arf.off -= (HC * 128 + 15) // 16 * 16
    main_seq()
    sch.final_wait()
    print("build: ninst", sch.ninst, "nsem", len(sch.semmap), "ckpts", ckpt[0], {e: sch.cnt[e] for e in sch.cnt})

    with nc.Block() as block:
        @block.tensor
        def _(e):
            for f in sch.prog["pe"]:
                f(e)

        @block.scalar
        def _(e):
            for f in sch.prog["act"]:
                f(e)

        @block.vector
        def _(e):
            for f in sch.prog["dve"]:
                f(e)

        @block.gpsimd
        def _(e):
            for f in sch.prog["pool"]:
                f(e)

        @block.sync
        def _(e):
            for f in sch.prog["sp"]:
                f(e)
    es.close()
    return nc


def _consts(cfg):
    j = np.arange(128, dtype=np.float32)[:, None]
    i = np.arange(128, dtype=np.float32)[None, :]
    cst = np.zeros((128, 7 * 128 + 4), np.float32)
    cst[:, 0:128] = np.eye(128, dtype=np.float32)
    cst[:, 128:256] = np.maximum(i - j, 0)
    cst[:, 256:384] = np.maximum(j - i, 0)
    cst[:, 384:512] = (i >= j)
    cst[:, 512:640] = (j >= i)
    cst[:, 640:768] = i + 1
    cst[:, 768:896] = 128 - i
    cst[:, 896] = 127 - j[:, 0]
    cst[:, 897] = j[:, 0]
    cst[:, 898] = 128
    return cst


def _rope_tables(cfg):
    NT, GW = cfg.NT, cfg.GRID_W
    t = np.arange(NT)
    row = (t // GW).astype(np.float32)
    col = (t % GW).astype(np.float32)
    inv = (10000.0 ** (-np.arange(32, dtype=np.float32) / 32)).astype(np.float32)
    ar = row[:, None] * inv[None, :]
    ac = col[:, None] * inv[None, :]
    C = np.concatenate([np.cos(ar), np.cos(ar), np.cos(ac), np.cos(ac)], axis=1).astype(np.float32)
    S = np.concatenate([-np.sin(ar), np.sin(ar), -np.sin(ac), np.sin(ac)], axis=1).astype(np.float32)
    return C, S


_NC_CACHE = {}


def kernel(cfg=None, **inp):
    if cfg is None:
        cfg = Cfg()
    c = cfg
    key = (c.D, c.F, c.H, c.KVH, c.NB, c.RH, c.NT, c.PAST, c.DEPTH, c.NPC, c.NSC, c.FG, c.CC, c.stop)
    if key not in _NC_CACHE:
        _NC_CACHE[key] = build_nc(c)
    nc = _NC_CACHE[key]
    f32 = lambda a: np.ascontiguousarray(np.asarray(a, dtype=np.float32))
    NT, D, DEPTH = c.NT, c.D, c.DEPTH
    SPC = NT // 256
    shared = {
        "cst": _consts(c),
        "norm_g": f32(inp["norm_g"]).reshape(DEPTH, 3 * c.DC, 128),
        "b_mod": f32(inp["b_mod"]).reshape(DEPTH, 9 * c.DC, 128),
        "ffn_wg": f32(inp["ffn_wg"]), "ffn_wu": f32(inp["ffn_wu"]), "ffn_wd": f32(inp["ffn_wd"]),
        "w_in": f32(inp["w_in"]),
        "q_gain": f32(inp["q_gain"]), "k_gain": f32(inp["k_gain"]),
        "conv_w": f32(inp["lru_conv_w"]).reshape(DEPTH, 4 * c.NB, 128),
        "conv_b": f32(inp["lru_conv_b"]).reshape(DEPTH, c.NB, 128),
        "lru_wa": f32(inp["lru_wa"]), "lru_wi": f32(inp["lru_wi"]),
        "lru_ba": f32(inp["lru_ba"]).reshape(DEPTH, 2 * c.NB, 128),
        "lru_bi": f32(inp["lru_bi"]).reshape(DEPTH, 2 * c.NB, 128),
        "lru_lam": f32(inp["lru_lambda"]).reshape(DEPTH, 2 * c.NB, 128),
        "ret_logit": f32(inp["ret_logit"]).reshape(DEPTH, 2 * c.RH),
        "ret_g": f32(inp["ret_g"]).reshape(DEPTH, c.RH, 128),
        "w_out": f32(inp["w_out"]),
        "final_g": f32(inp["final_g"]).reshape(c.DC, 128),
    }
    xp, xs = f32(inp["x_prompt"]), f32(inp["x_sample"])
    ck, cv = f32(inp["cache_k"]), f32(inp["cache_v"])
    slru, sret = f32(inp["state_lru"]), f32(inp["state_ret"])
    cc_, cctx = f32(inp["c"]), f32(inp["c_ctx"])
    ropC, ropS = _rope_tables(c)
    NKEY = NT + c.PAST
    KCH_ = (NT + c.PAST) // 128
    mb_p = np.full((KCH_, c.NSEG), NEG, np.float32)
    for kc in range(NT // 128):
        mb_p[kc, kc // 2] = 0.0
    mb_p = np.ascontiguousarray(np.broadcast_to(mb_p.reshape(1, -1), (128, KCH_ * c.NSEG)))
    mb_s = np.zeros((128, KCH_ * c.NSEG), np.float32)
    zeros_ck = np.zeros((DEPTH, c.PAST, c.KVW), np.float32)
    wm = f32(inp["w_mod"])
    condall = np.ascontiguousarray(np.concatenate([cctx.reshape(1, D), cc_.reshape(-1, D)], axis=0))
    in_maps = []
    for core in range(c.NPC + c.NSC):
        m = dict(shared)
        if c.CC:
            m["w_mod"] = np.ascontiguousarray(wm[:, :, core * c.CS:(core + 1) * c.CS])
            m["condall"] = condall
            rstar = 0 if core < c.NPC else 1 + (core - c.NPC)
            sj = np.zeros((c.NCORES * c.NCOND, c.NCORES), np.float32)
            for j in range(c.NCORES):
                sj[j * c.NCOND + rstar, j] = 1.0
            m["selj"] = sj
        else:
            m["w_mod"] = wm
        if core < c.NPC:
            m["x"] = np.ascontiguousarray(xp[core * SPC:(core + 1) * SPC].reshape(NT, D))
            if not c.CC:
                m["cond"] = cctx.reshape(1, D)
            m["ck"] = zeros_ck
            m["cv"] = zeros_ck
            m["slru"] = np.zeros((DEPTH, 2 * c.NB, 128), np.float32)
            m["sret"] = np.zeros((DEPTH, 2, c.RH, 128, 128), np.float32)
            m["flag"] = np.zeros((128, 2), np.float32)
            m["mb"] = mb_p
            m["ropc"] = np.ones((NT, 128), np.float32)
            m["rops"] = np.zeros((NT, 128), np.float32)
        else:
            b = core - c.NPC
            m["x"] = np.ascontiguousarray(xs[b].reshape(NT, D))
            if not c.CC:
                m["cond"] = np.ascontiguousarray(cc_[b].reshape(1, D))
            m["ck"] = np.ascontiguousarray(ck[b].reshape(DEPTH, c.PAST, c.KVW))
            m["cv"] = np.ascontiguousarray(cv[b].reshape(DEPTH, c.PAST, c.KVW))
            m["slru"] = np.ascontiguousarray(slru[b].reshape(DEPTH, 2 * c.NB, 128))
            m["sret"] = np.ascontiguousarray(sret[b])
            m["flag"] = np.ones((128, 2), np.float32)
            m["mb"] = mb_s
            m["ropc"] = ropC
            m["rops"] = ropS
        in_maps.append(m)
    res = run_bass_kernel_spmd(nc, in_maps, core_ids=list(range(c.NPC + c.NSC)))
    R = res.results
    B = c.NPC * SPC
    y_p = np.zeros((B, 256, D), np.float32)
    y_s = np.zeros((c.NSC, NT, D), np.float32)
    nk = np.zeros((B, DEPTH, 256, c.KVH, 128), np.float32)
    nv = np.zeros((B, DEPTH, 256, c.KVH, 128), np.float32)
    nl = np.zeros((B, DEPTH, 2, c.LW), np.float32)
    nr = np.zeros((B, DEPTH, 2, c.RH, 128, 128), np.float32)
    for core in range(c.NPC):
        r = R[core]
        for s in range(SPC):
            b = core * SPC + s
            y_p[b] = r["y"][s * 256:(s + 1) * 256]
            for l in range(DEPTH):
                nk[b, l] = r["nk"][l, s * 256:(s + 1) * 256].reshape(256, c.KVH, 128)
                nv[b, l] = r["nv"][l, s * 256:(s + 1) * 256].reshape(256, c.KVH, 128)
                nl[b, l] = r["nlru"][l].reshape(c.NSEG, 2, c.LW)[s]
                nr[b, l] = r["nret"][l, s]
    for b in range(c.NSC):
        y_s[b] = R[c.NPC + b]["y"]
    return (y_p, y_s, nk, nv, nl, nr)
```

```python
import numpy as np
from contextlib import ExitStack
import concourse.bass as bass
import concourse.mybir as mybir
from concourse.bass_utils import run_bass_kernel_spmd

F32 = mybir.dt.float32
BF16 = mybir.dt.bfloat16
AF = mybir.ActivationFunctionType
ALU = mybir.AluOpType
AX = mybir.AxisListType

EPS = 1e-6
NEG = -30000.0


class Cfg:
    def __init__(self, D=4096, F=11008, H=16, KVH=4, NB=8, RH=8, NT=1024, PAST=512, DEPTH=2,
                 GRID_W=64, NPC=4, NSC=4, FG=4, stop=0, CC=False):
        self.stop = stop
        self.CC = CC
        self.NCORES = NPC + NSC
        self.NCOND = 1 + NSC
        self.CS = 9 * D // (NPC + NSC)
        self.D, self.F, self.H, self.KVH, self.NB, self.RH = D, F, H, KVH, NB, RH
        self.NT, self.PAST, self.DEPTH, self.GRID_W = NT, PAST, DEPTH, GRID_W
        self.NPC, self.NSC, self.FG = NPC, NSC, FG
        self.DC = D // 128
        self.AW, self.KVW, self.LW, self.RW = H * 128, KVH * 128, NB * 128, RH * 128
        self.INW = self.AW + 2 * self.KVW + 2 * self.LW + 4 * self.RW
        self.FC = F // 128
        self.NSEG = NT // 256
        self.NCH = NT // 128
        self.NH = NT // 512
        self.PCH = PAST // 128
        self.G = H // KVH
        self.oQ = 0
        self.oK = self.AW
        self.oV = self.oK + self.KVW
        self.oXB = self.oV + self.KVW
        self.oYB = self.oXB + self.LW
        self.oRQ = self.oYB + self.LW
        self.oRK = self.oRQ + self.RW
        self.oRV = self.oRK + self.RW
        self.oRG = self.oRV + self.RW
        assert self.AW + self.LW + self.RW == D


class StopBuild(Exception):
    pass


class Buf:
    __slots__ = ("w", "r", "name", "persistent")

    def __init__(self, name, persistent=False):
        self.w = None
        self.r = {}
        self.name = name
        self.persistent = persistent


class Sched:
    ENG = ("pe", "act", "dve", "pool", "sp")

    def __init__(self, nc, es):
        self.nc, self.es = nc, es
        self.prog = {e: [] for e in self.ENG}
        self.S = {e: es.enter_context(nc.semaphore("S_" + e)) for e in ("pe", "act", "dve", "pool")}
        self.cnt = {e: 0 for e in self.S}
        self.waited = {e: {} for e in self.ENG}
        self.semmap = {}
        self.dcnt = {}
        self.dsem = {}
        self.bar = []
        self.ninst = 0
        self.dead = False

    def _sem(self, key):
        if key not in self.semmap:
            s = self.es.enter_context(self.nc.semaphore("D%d" % len(self.semmap)))
            self.semmap[key] = s
            self.dcnt[id(s)] = 0
            self.dsem[id(s)] = s
        return self.semmap[key]

    def _deps(self, rd, wr):
        toks = []
        for b in rd:
            if b.w is not None:
                toks.append(b.w)
        for b in wr:
            if b.w is not None:
                toks.append(b.w)
            toks.extend(b.r.values())
        return toks

    def _wait(self, e, toks):
        p, wd = self.prog[e], self.waited[e]
        for (sem, val) in toks:
            if e == "pe" and sem is self.S["pe"]:
                continue
            k = id(sem)
            if wd.get(k, 0) < val:
                wd[k] = val
                p.append(lambda eng, sem=sem, val=val: eng.wait_ge(sem, val))
                self.ninst += 1

    def _mark(self, tok, rd, wr):
        k = id(tok[0])
        for b in rd:
            if k not in b.r or b.r[k][1] < tok[1]:
                b.r[k] = tok
        for b in wr:
            b.w = tok
            b.r = {}

    def op(self, e, fn, rd=(), wr=(), sig=True):
        if self.dead:
            return
        self._wait(e, self._deps(rd, wr))
        self.ninst += 1
        if sig:
            self.cnt[e] += 1
            sem = self.S[e]
            self.prog[e].append(lambda eng, fn=fn, sem=sem: fn(eng).then_inc(sem, 1))
            tok = (sem, self.cnt[e])
        else:
            self.prog[e].append(lambda eng, fn=fn: fn(eng))
            tok = (self.S[e], self.cnt[e] + 1)
        self._mark(tok, rd, wr)

    def dma(self, q, out, in_, rd=(), wr=(), key=None, fn=None):
        if self.dead:
            return
        sem = self._sem(key)
        k = id(sem)
        toks = self._deps(rd, wr)
        if self.dcnt[k] > 0:
            toks.append((sem, self.dcnt[k]))
        if q == "pool" and not all(b.persistent for b in list(rd) + list(wr)):
            toks = toks + self.bar
        self._wait(q, toks)
        self.dcnt[k] += 16
        if fn is not None:
            self.prog[q].append(lambda eng, fn=fn, sem=sem: fn(eng).then_inc(sem, 16))
        else:
            self.prog[q].append(lambda eng, out=out, in_=in_, sem=sem: eng.dma_start(out=out, in_=in_).then_inc(sem, 16))
        self.ninst += 1
        self._mark((sem, self.dcnt[k]), rd, wr)

    def barrier(self, engines=("pe", "act", "dve", "sp")):
        if self.dead:
            return
        toks = [(self.S[e], self.cnt[e]) for e in self.S if self.cnt[e] > 0]
        toks += [(self.dsem[k], v) for k, v in self.dcnt.items() if v > 0]
        self.bar = toks
        for e in engines:
            self._wait(e, toks)

    def final_wait(self):
        toks = [(self.dsem[k], v) for k, v in self.dcnt.items() if v > 0]
        toks += [(self.S[e], self.cnt[e]) for e in self.S if self.cnt[e] > 0]
        self._wait("sp", toks)


class TB:
    __slots__ = ("ap", "b")

    def __init__(self, ap, b):
        self.ap, self.b = ap, b


class Arena:
    def __init__(self, tensor, n, tag):
        self.t, self.n, self.tag, self.off = tensor, n, tag, 0
        self.live = []

    def reset(self, clear=True):
        self.off = 0
        if clear:
            self.live = []

    def alloc(self, shape, name):
        size = int(np.prod(shape))
        size_al = (size + 15) // 16 * 16
        assert self.off + size_al <= self.n, "arena %s overflow: need %d have %d (%s)" % (self.tag, self.off + size_al, self.n, name)
        st, en = self.off, self.off + size_al
        self.off = en
        tb = None
        for (s0, e0, tb0, shp0) in self.live:
            if s0 == st and e0 == en and tb0.b.name == name and shp0 == tuple(shape):
                tb = tb0
        if tb is None:
            ap = self.t[:, st:st + size]
            if len(shape) == 2:
                ap = ap.rearrange("p (a b) -> p a b", a=shape[0])
            elif len(shape) == 3:
                ap = ap.rearrange("p (a b c) -> p a b c", a=shape[0], b=shape[1])
            tb = TB(ap, Buf(name))
            self.live.append((st, en, tb, tuple(shape)))
        nb = tb.b
        for (s0, e0, tb0, shp0) in self.live:
            if tb0 is not tb and s0 < en and st < e0:
                ob = tb0.b
                toks = list(ob.r.values()) + ([ob.w] if ob.w is not None else [])
                for tok in toks:
                    k = id(tok[0])
                    if k not in nb.r or nb.r[k][1] < tok[1]:
                        nb.r[k] = tok
        return tb


def build_nc(cfg):
    c = cfg
    D, DC, NT, NCH, NH, NSEG, PAST, PCH = c.D, c.DC, c.NT, c.NCH, c.NH, c.NSEG, c.PAST, c.PCH
    DEPTH, F, FC, INW, KVW, LW, RW, NB, RH, G = c.DEPTH, c.F, c.FC, c.INW, c.KVW, c.LW, c.RW, c.NB, c.RH, c.G
    NKEY = NT + PAST
    KCH = NCH + PCH
    nc = bass.Bass("TRN2", target_bir_lowering=False, num_devices=c.NPC + c.NSC)

    def din(name, shape):
        return nc.dram_tensor(name, list(shape), F32, kind="ExternalInput").ap()

    def dout(name, shape):
        return nc.dram_tensor(name, list(shape), F32, kind="ExternalOutput").ap()

    x_in = din("x", [NT, D])
    cond_in = din("cond", [1, D]) if not c.CC else None
    ck_in = din("ck", [DEPTH, PAST, KVW])
    cv_in = din("cv", [DEPTH, PAST, KVW])
    slru_in = din("slru", [DEPTH, 2 * NB, 128])
    sret_in = din("sret", [DEPTH, 2, RH, 128, 128])
    flag_in = din("flag", [128, 2])
    mb_in = din("mb", [128, (NCH + PCH) * NSEG])
    ropc_in = din("ropc", [NT, 128])
    rops_in = din("rops", [NT, 128])
    cst_in = din("cst", [128, 7 * 128 + 4])
    norm_g = din("norm_g", [DEPTH, 3 * DC, 128])
    NCORES, NCOND, CS = c.NCORES, c.NCOND, c.CS
    NRG = NCORES * NCOND
    if c.CC:
        assert NRG <= 128
        w_mod = din("w_mod", [DEPTH, D, CS])
        condall_in = din("condall", [NCOND, D])
        selj_in = din("selj", [NRG, NCORES])
        ag_in = nc.dram_tensor("ag_in", [NCOND, DEPTH * CS], F32, kind="Internal").ap()
        ag_out = nc.dram_tensor("ag_out", [NRG, DEPTH * CS], F32, kind="Internal").ap()
        agin_b, agout_b = Buf("agin", True), Buf("agout", True)
    else:
        w_mod = din("w_mod", [DEPTH, D, 9 * D])
    b_mod = din("b_mod", [DEPTH, 9 * DC, 128])
    ffn_wg = din("ffn_wg", [DEPTH, 2, D, F])
    ffn_wu = din("ffn_wu", [DEPTH, 2, D, F])
    ffn_wd = din("ffn_wd", [DEPTH, 2, F, D])
    w_in = din("w_in", [DEPTH, D, INW])
    q_gain = din("q_gain", [DEPTH, 128])
    k_gain = din("k_gain", [DEPTH, 128])
    conv_w = din("conv_w", [DEPTH, 4 * NB, 128])
    conv_b = din("conv_b", [DEPTH, NB, 128])
    lru_wa = din("lru_wa", [DEPTH, 2, NB, 128, 128])
    lru_ba = din("lru_ba", [DEPTH, 2 * NB, 128])
    lru_wi = din("lru_wi", [DEPTH, 2, NB, 128, 128])
    lru_bi = din("lru_bi", [DEPTH, 2 * NB, 128])
    lru_lam = din("lru_lam", [DEPTH, 2 * NB, 128])
    ret_logit = din("ret_logit", [DEPTH, 2 * RH])
    ret_g = din("ret_g", [DEPTH, RH, 128])
    w_out = din("w_out", [DEPTH, D, D])
    final_g = din("final_g", [DC, 128])

    y_out = dout("y", [NT, D])
    nk_out = dout("nk", [DEPTH, NT, KVW])
    nv_out = dout("nv", [DEPTH, NT, KVW])
    nlru_out = dout("nlru", [DEPTH, NSEG * 2 * NB, 128])
    nret_out = dout("nret", [DEPTH, NSEG, 2, RH, 128, 128])

    xT_s = nc.dram_tensor("xT_s", [DC, 128, NT], F32, kind="Internal").ap()
    a_s = nc.dram_tensor("a_s", [FC, 128, NT], BF16, kind="Internal").ap()
    mt_s = nc.dram_tensor("mt_s", [DC, 128, NT], BF16, kind="Internal").ap()
    modr_s = nc.dram_tensor("modr_s", [DEPTH, 9 * DC, 128], F32, kind="Internal").ap()
    xT_b = [Buf("xTs%d" % i, True) for i in range(DC)]
    a_b = [Buf("as%d" % i, True) for i in range(FC)]
    mt_b = [Buf("mts%d" % i, True) for i in range(DC)]
    modr_b = Buf("modrs", True)

    es = ExitStack()
    sch = Sched(nc, es)

    def sb(name, shape, dt):
        return es.enter_context(nc.sbuf_tensor(name, list(shape), dt))

    R1 = sb("R1", [128, DC * NT // 2], F32)
    HT = R1[:].bitcast(BF16).rearrange("p (c t) -> p c t", c=DC)
    ACC = R1[:].rearrange("p (c t) -> p c t", c=DC // 2)
    HT_b = Buf("HT", True)
    ACC_b = [Buf("ACC%d" % i, True) for i in range(DC // 2)]
    NSLOT = 4
    WPE = DC * 256
    R2 = sb("R2", [128, NSLOT * WPE], BF16)
    WS_b = [Buf("WS%d" % i, True) for i in range(NSLOT)]

    def wslot(i):
        return R2[:, i * WPE:(i + 1) * WPE]

    ARF_N = 8192
    ARB_N = 14336
    arf = Arena(sb("ARF", [128, ARF_N], F32), ARF_N, "f")
    arb = Arena(sb("ARB", [128, ARB_N], BF16), ARB_N, "b")
    PS = es.enter_context(nc.psum_tensor("PS", [128, 8, 512], F32))
    PS_b = [Buf("PS%d" % i, True) for i in range(8)]

    CST = sb("CST", [128, 7 * 128 + 4], F32)
    CST_b = Buf("CST", True)
    IDN = CST[:, 0:128]
    DPOS, DNEG, LOWM, UPM = (CST[:, 128 * i:128 * (i + 1)] for i in range(1, 5))
    IROW1, IROW2 = CST[:, 640:768], CST[:, 768:896]
    IVEC = CST[:, 896:900]
    ONES = sb("ONES", [128, 128], F32)
    ONESB = sb("ONESB", [128, 128], BF16)
    SMALL = sb("SMALL", [128, 16], F32)
    EPSC, ONEC = SMALL[:, 0:1], SMALL[:, 1:2]
    FLAG = SMALL[:, 2:3]
    CONST_b = Buf("CONST", True)
    MODT = sb("MODT", [128, DEPTH, 9 * DC], F32)
    NGT = sb("NGT", [128, DEPTH, 3 * DC], F32)
    GS = sb("GS", [128, DEPTH, 3 * DC], F32)
    GATE = sb("GATE", [128, DEPTH, 3 * DC], F32)
    FGT = sb("FGT", [128, DC], F32)
    ZSH = sb("ZSH", [128, DC], F32)
    MOD_b = Buf("MOD", True)
    ST = sb("ST", [128, DC, c.NCOND if c.CC else 1], BF16)
    SELJ = sb("SELJ", [128, c.NCORES], F32)
    MB = sb("MB", [128, NCH + PCH, NSEG], F32)
    TAB_b = Buf("TAB", True)
    NLV = 4 * NB + NB + 2 * NB * 4 + RH
    LV = sb("LV", [128, NLV], F32)
    LV_b = Buf("LV", True)
    oCW, oCB = 0, 4 * NB
    oBA, oBI, oLAM, oSL, oRG_ = 5 * NB, 7 * NB, 9 * NB, 11 * NB, 13 * NB
    SC8 = sb("SC8", [128, 2 * NB], F32)
    GAINQ = sb("GAINQ", [128, 128], F32)
    GAINK = sb("GAINK", [128, 128], F32)
    LG = sb("LG", [128, 2 * RH], F32)
    KDEC = sb("KDEC", [128, RH, 2], F32)
    CDEC = sb("CDEC", [128, RH, 2], F32)
    LST = sb("LST", [128, NSEG * 2 * NB], F32)
    LST_b = Buf("LST", True)
    RET_b = Buf("RETTAB", True)

    def mm(out, lhsT, rhs, start, stop, rd, wr):
        sch.op("pe", lambda e: e.matmul(out, lhsT, rhs, start=start, stop=stop), rd, wr, sig=stop)

    def tr(out, in_, idn, rd, wr):
        sch.op("pe", lambda e: e.transpose(out, in_, idn), rd, wr)

    def act(out, in_, func, rd, wr, bias=None, scale=None):
        kw = {}
        if bias is not None:
            kw["bias"] = bias
        if scale is not None:
            kw["scale"] = scale
        sch.op("act", lambda e: e.activation(out=out, in_=in_, func=func, **kw), rd, wr)

    def tt(out, in0, in1, op, rd, wr, eng="dve"):
        sch.op(eng, lambda e: e.tensor_tensor(out=out, in0=in0, in1=in1, op=op), rd, wr)

    def ts(out, in0, s1, op0, rd, wr, s2=None, op1=None, eng="dve"):
        if op1 is None:
            sch.op(eng, lambda e: e.tensor_scalar(out=out, in0=in0, scalar1=s1, scalar2=None, op0=op0), rd, wr)
        else:
            sch.op(eng, lambda e: e.tensor_scalar(out=out, in0=in0, scalar1=s1, scalar2=s2, op0=op0, op1=op1), rd, wr)

    def stt(out, in0, scalar, in1, op0, op1, rd, wr, eng="dve"):
        sch.op(eng, lambda e: e.scalar_tensor_tensor(out=out, in0=in0, scalar=scalar, in1=in1, op0=op0, op1=op1), rd, wr)

    def cpv(out, in_, rd, wr):
        sch.op("dve", lambda e: e.tensor_copy(out=out, in_=in_), rd, wr)

    def cpa(out, in_, rd, wr):
        act(out, in_, AF.Identity, rd, wr)

    def recip(out, in_, rd, wr):
        sch.op("dve", lambda e: e.reciprocal(out=out, in_=in_), rd, wr)

    def mset(ap, val, wr, eng="dve"):
        sch.op(eng, lambda e: e.memset(ap, val), (), wr)

    def ld(out, in_, rd, wr, key):
        sch.dma("sp", out, in_, rd, wr, key)

    def ldc(out, in_, rd, wr, key):
        sch.dma("pool", out, in_, rd, wr, key)

    ckpt = [0]

    def checkpoint(name=""):
        ckpt[0] += 1
        if c.stop:
            print("ckpt", ckpt[0], name)
        if c.stop and ckpt[0] >= c.stop:
            sch.dead = True

    def phase(bar=True):
        checkpoint()
        if bar:
            sch.barrier()
        arf.reset(bar)
        arb.reset(bar)

    def bank(i, lo=0, hi=512):
        return PS[:, i, lo:hi]

    ld(CST[:], cst_in, (), [CST_b], "cst")
    mset(ONES[:], 1.0, [CONST_b])
    mset(ONESB[:], 1.0, [CONST_b])
    mset(SMALL[:, 0:1], EPS, [CONST_b])
    mset(SMALL[:, 1:2], 1.0, [CONST_b])
    mset(ZSH[:], 0.0, [CONST_b])
    ld(SMALL[:, 2:4], flag_in, (), [CONST_b], "flag")
    ld(MB[:].rearrange("p k s -> p (k s)"), mb_in, (), [TAB_b], "mb")

    def rows_to_cols(dst_ap, rows_dram, R, rd_b, wr_b, add_dram=None, add_b=None, psb=7):
        t = arf.alloc([128], "rtc_a")
        ld(t.ap[0:R, :], rows_dram, rd_b, [t.b], "rtc_a")
        if add_dram is not None:
            t2 = arf.alloc([128], "rtc_b")
            ld(t2.ap[0:R, :], add_dram, add_b, [t2.b], "rtc_b")
            tt(t.ap[0:R, :], t.ap[0:R, :], t2.ap[0:R, :], ALU.add, [t.b, t2.b], [t.b])
        tr(bank(psb, 0, R), t.ap[0:R, :], IDN[0:R, 0:R], [t.b, CST_b], [PS_b[psb]])
        cpv(dst_ap, bank(psb, 0, R), [PS_b[psb]], wr_b)

    if c.CC:
        phase()
        for r in range(NCOND):
            cr = arf.alloc([128], "cr%d" % r)
            ld(cr.ap[0:DC, :], condall_in[r:r + 1, :].rearrange("o (c p) -> (o c) p", p=128), (), [cr.b], "cr%d" % (r % 2))
            act(cr.ap[0:DC, :], cr.ap[0:DC, :], AF.Silu, [cr.b], [cr.b])
            tr(bank(6 + r % 2, 0, DC), cr.ap[0:DC, :], IDN[0:DC, 0:DC], [cr.b, CST_b], [PS_b[6 + r % 2]])
            cpv(ST[:, :, r], bank(6 + r % 2, 0, DC), [PS_b[6 + r % 2]], [MOD_b])
        ld(SELJ[0:NRG, :], selj_in, (), [MOD_b], "selj")
        stg_l = [arf.alloc([512], "modstg%d" % k) for k in range(4)]
        PW = 512 if CS % 512 == 0 else 384
        assert CS % PW == 0
        npp = CS // PW
        k = 0
        for l in range(DEPTH):
            for p in range(npp):
                s0 = (k % 2) * 2
                pb = k % 2
                stg = stg_l[k % 4]
                k += 1
                wv = R2[:, s0 * WPE:s0 * WPE + DC * PW].rearrange("p (c f) -> p c f", c=DC)
                ldc(wv, w_mod[l, :, p * PW:(p + 1) * PW].rearrange("(c p) f -> p c f", p=128), (), [WS_b[s0], WS_b[s0 + 1]], "ws%d" % s0)
                for dc in range(DC):
                    mm(PS[0:NCOND, pb, 0:PW], ST[:, dc, :], wv[:, dc, :], dc == 0, dc == DC - 1,
                       [MOD_b, WS_b[s0], WS_b[s0 + 1]], [PS_b[pb]])
                cpa(stg.ap[0:NCOND, 0:PW], PS[0:NCOND, pb, 0:PW], [PS_b[pb]], [stg.b])
                ld(ag_in[:, l * CS + p * PW:l * CS + (p + 1) * PW], stg.ap[0:NCOND, 0:PW], [stg.b], [agin_b], stg.b.name)
        sch.dma("pool", None, None, [agin_b], [agout_b], "agcc",
                fn=lambda eng: eng.collective_compute("AllGather", op=ALU.bypass, replica_groups=[list(range(NCORES))],
                                                      ins=[ag_in], outs=[ag_out]))
        RPR = CS // 128
        gts = [arf.alloc([512], "agt%d" % k) for k in range(2)]
        k = 0
        for t in range(DEPTH * npp):
            l, p = t // npp, t % npp
            gt = gts[t % 2]
            ld(gt.ap[0:NRG, 0:PW], ag_out[:, t * PW:(t + 1) * PW], [agout_b], [gt.b], gt.b.name)
            for j in range(NCORES):
                pb = 2 + k % 4
                stg = stg_l[k % 4]
                k += 1
                mm(PS[0:1, pb, 0:PW], SELJ[0:NRG, j:j + 1], gt.ap[0:NRG, 0:PW], True, True, [MOD_b, gt.b], [PS_b[pb]])
                cpa(stg.ap[0:1, 0:PW], PS[0:1, pb, 0:PW], [PS_b[pb]], [stg.b])
                ld(modr_s[l, j * RPR + p * (PW // 128):j * RPR + (p + 1) * (PW // 128), :].rearrange("(o r) f -> o (r f)", o=1), stg.ap[0:1, 0:PW],
                   [stg.b], [modr_b], stg.b.name)
    else:
        phase()
        cr = arf.alloc([128], "cr")
        ld(cr.ap[0:DC, :], cond_in.rearrange("o (c p) -> (o c) p", p=128), (), [cr.b], "cr")
        act(cr.ap[0:DC, :], cr.ap[0:DC, :], AF.Silu, [cr.b], [cr.b])
        tr(bank(7, 0, DC), cr.ap[0:DC, :], IDN[0:DC, 0:DC], [cr.b, CST_b], [PS_b[7]])
        cpv(ST[:, :, 0], bank(7, 0, DC), [PS_b[7]], [MOD_b])

        pass
    nmp = 9 * D // 512
    MSTG = sb("MSTG", [128, 2, 512], F32)
    MSTG_b = [Buf("MSTG0", True), Buf("MSTG1", True)]
    mctr = [0]

    def mod_panel(l, p, pair=None, pb=7):
        k = mctr[0]
        mctr[0] += 1
        s0 = (k % 2) * 2 if pair is None else pair
        wv = R2[:, s0 * WPE:(s0 + 2) * WPE].rearrange("p (c f) -> p c f", c=DC)
        ldc(wv, w_mod[l, :, p * 512:(p + 1) * 512].rearrange("(c p) f -> p c f", p=128), (), [WS_b[s0], WS_b[s0 + 1]], "ws%d" % s0)
        for dc in range(DC):
            mm(PS[0:1, pb, :], ST[:, dc, :], wv[:, dc, :], dc == 0, dc == DC - 1,
               [MOD_b, WS_b[s0], WS_b[s0 + 1]], [PS_b[pb]])
        cpa(MSTG[0:1, k % 2, :], PS[0:1, pb, :], [PS_b[pb]], [MSTG_b[k % 2]])
        ld(modr_s[l, p * 4:(p + 1) * 4, :].rearrange("(o r) f -> o (r f)", o=1), MSTG[0:1, k % 2, :], [MSTG_b[k % 2]], [modr_b], "mstg%d" % (k % 2))

    def mod_finalize(l, groups, with_ng):
        phase()
        for j3 in groups:
            R = 3 * DC
            rows_to_cols(MODT[:, l, j3 * R:(j3 + 1) * R], modr_s[l, j3 * R:(j3 + 1) * R, :], R, [modr_b], [MOD_b],
                         add_dram=b_mod[l, j3 * R:(j3 + 1) * R, :], add_b=())
        if with_ng:
            rows_to_cols(NGT[:, l, :], norm_g[l], 3 * DC, (), [MOD_b])
        for i in groups:
            stt(GS[:, l, i * DC:(i + 1) * DC], MODT[:, l, (3 * i + 1) * DC:(3 * i + 2) * DC], 1.0, NGT[:, l, i * DC:(i + 1) * DC],
                ALU.add, ALU.mult, [MOD_b], [MOD_b])
            gsc = 1.0 if i == 1 else 0.5
            ts(GATE[:, l, i * DC:(i + 1) * DC], MODT[:, l, (3 * i + 2) * DC:(3 * i + 3) * DC], gsc, ALU.mult, [MOD_b], [MOD_b])

    npg = nmp // 3
    if not c.CC:
        for p in range(npg):
            mod_panel(0, p)
        mod_queue = [(0, p) for p in range(npg, nmp)] + [(l, p) for l in range(1, DEPTH) for p in range(nmp)]
        mod_finalize(0, [0], True)
    else:
        mod_queue = []
        for l in range(DEPTH):
            mod_finalize(l, [0, 1, 2], True)
    rows_to_cols(FGT[:], final_g, DC, (), [MOD_b])

    def mod_pump(n, pair=None, pb=7):
        for _ in range(n):
            if mod_queue:
                l, p = mod_queue.pop(0)
                mod_panel(l, p, pair, pb)

    def mod_need(l, g):
        while mod_queue and (mod_queue[0][0], mod_queue[0][1] // npg) <= (l, g):
            l_, p_ = mod_queue.pop(0)
            mod_panel(l_, p_)

    phase()
    for tc in range(NCH):
        xin = arf.alloc([D], "xin")
        ld(xin.ap, x_in[tc * 128:(tc + 1) * 128, :], (), [xin.b], "xin")
        for c4 in range(DC // 4):
            pb = c4 % 2
            for k in range(4):
                cc = c4 * 4 + k
                tr(bank(pb, k * 128, (k + 1) * 128), xin.ap[:, cc * 128:(cc + 1) * 128], IDN, [xin.b, CST_b], [PS_b[pb]])
            stg = arf.alloc([4, 128], "xstg%d" % pb)
            cpv(stg.ap, bank(pb).rearrange("p (k t) -> p k t", k=4), [PS_b[pb]], [stg.b])
            ld(xT_s[c4 * 4:(c4 + 1) * 4, :, tc * 128:(tc + 1) * 128].rearrange("c p t -> p c t"), stg.ap,
               [stg.b], xT_b[c4 * 4:(c4 + 1) * 4], "xstg%d" % pb)
        arf.off = 0

    HTw_b = [Buf("HTw%d" % i, True) for i in range(DC)]

    def sumsq_rstd(getx, ntok, dview):
        accs = [arf.alloc([ntok], "ssacc%d" % k) for k in range(4)]
        sqs = [arf.alloc([ntok], "sq%d" % k) for k in range(4)]
        for cc in range(DC):
            xap, xb = getx(cc)
            k = cc % 4
            if cc < 4:
                act(accs[k].ap, xap, AF.Square, [xb], [accs[k].b])
            else:
                act(sqs[k].ap, xap, AF.Square, [xb], [sqs[k].b])
                tt(accs[k].ap, accs[k].ap, sqs[k].ap, ALU.add, [accs[k].b, sqs[k].b], [accs[k].b])
        tt(accs[0].ap, accs[0].ap, accs[1].ap, ALU.add, [accs[0].b, accs[1].b], [accs[0].b])
        tt(accs[2].ap, accs[2].ap, accs[3].ap, ALU.add, [accs[2].b, accs[3].b], [accs[2].b])
        tt(accs[0].ap, accs[0].ap, accs[2].ap, ALU.add, [accs[0].b, accs[2].b], [accs[0].b])
        mm(bank(6, 0, ntok), ONES[:], accs[0].ap, True, True, [accs[0].b, CONST_b], [PS_b[6]])
        rs = arf.alloc([ntok], "rstd")
        act(rs.ap, bank(6, 0, ntok), AF.Sqrt, [PS_b[6], CONST_b], [rs.b], bias=EPSC, scale=1.0 / dview)
        recip(rs.ap, rs.ap, [rs.b], [rs.b])
        return rs

    XHV = R2[:].bitcast(F32).rearrange("p (c t) -> p c t", c=DC)
    QC = DC // 4

    def norm_phase(gs_ap, sh_ap):
        phase()
        nth = NT // 512
        for th in range(nth):
            arf.off = 0
            for q4 in range(4):
                ld(XHV[:, q4 * QC:(q4 + 1) * QC, :], xT_s[q4 * QC:(q4 + 1) * QC, :, th * 512:(th + 1) * 512].rearrange("c p t -> p c t"),
                   xT_b[q4 * QC:(q4 + 1) * QC], [WS_b[q4]], "xhq%d" % q4)
            getx = lambda cc: (XHV[:, cc, :], WS_b[cc // QC])
            rs = sumsq_rstd(getx, 512, D)
            tmps = [arf.alloc([512], "ntmp%d" % k) for k in range(4)]
            for cc in range(DC):
                tmp = tmps[cc % 4]
                xap, xb = getx(cc)
                stt(tmp.ap, xap, gs_ap[:, cc:cc + 1], rs.ap, ALU.mult, ALU.mult, [xb, rs.b, MOD_b], [tmp.b])
                last = (th == nth - 1 and cc == DC - 1)
                act(HT[:, cc, th * 512:(th + 1) * 512], tmp.ap, AF.Identity, [tmp.b, MOD_b] + (HTw_b if last else []),
                    [HT_b] if last else [HTw_b[cc]], bias=sh_ap[:, cc:cc + 1])

    wctr = [0]

    def load_panel(w_dram_rows_cols):
        s = wctr[0] % NSLOT
        wctr[0] += 1
        v = wslot(s).rearrange("p (c f) -> p c f", c=DC)
        ldc(v, w_dram_rows_cols.rearrange("(c p) f -> p c f", p=128), (), [WS_b[s]], "ws%d" % s)
        return v, WS_b[s]

    def gemm_fm(wv, wb, col0, banks):
        for h in range(NH):
            for dc in range(DC):
                mm(bank(banks[h]), wv[:, dc, col0:col0 + 128], HT[:, dc, h * 512:(h + 1) * 512], dc == 0, dc == DC - 1,
                   [wb, HT_b], [PS_b[banks[h]]])

    def gemm_tm(wv, wb, tc, pb):
        for dc in range(DC):
            mm(bank(pb, 0, 256), HT[:, dc, tc * 128:(tc + 1) * 128], wv[:, dc, :], dc == 0, dc == DC - 1,
               [wb, HT_b], [PS_b[pb]])

    def ffn(l, i):
        gate_ap = GATE[:, l, i * DC:(i + 1) * DC]
        norm_phase(GS[:, l, i * DC:(i + 1) * DC], MODT[:, l, 3 * i * DC:(3 * i + 1) * DC])
        phase()
        wg, wu, wd = ffn_wg[l, i // 2], ffn_wu[l, i // 2], ffn_wd[l, i // 2]
        sgs = [arf.alloc([512], "sg%d" % k) for k in range(2)]
        asts = [arb.alloc([NT], "ast%d" % k) for k in range(2)]
        k = 0
        for p in range(F // 256):
            gv, gb = load_panel(wg[:, p * 256:(p + 1) * 256])
            uv, ub = load_panel(wu[:, p * 256:(p + 1) * 256])
            for fl in range(2):
                fc = p * 2 + fl
                base = (fc % 2) * 4
                gemm_fm(gv, gb, fl * 128, [base + h for h in range(NH)])
                gemm_fm(uv, ub, fl * 128, [base + 2 + h for h in range(NH)])
                ast = asts[fc % 2]
                for h in range(NH):
                    sg = sgs[k % 2]
                    k += 1
                    act(sg.ap, bank(base + h), AF.Silu, [PS_b[base + h]], [sg.b])
                    tt(ast.ap[:, h * 512:(h + 1) * 512], sg.ap, bank(base + 2 + h), ALU.mult, [sg.b, PS_b[base + 2 + h]], [ast.b])
                ld(a_s[fc], ast.ap, [ast.b], [a_b[fc]], "ast%d" % (fc % 2))
        phase()
        FG = c.FG
        ngr = (FC + FG - 1) // FG
        xts = [arf.alloc([NT], "xt%d" % k) for k in range(2)]
        ags = [arb.alloc([FG, NT], "ag%d" % k) for k in range(2)]
        HD = DC // 2
        HW = HD * 128
        assert FG * HW <= WPE
        gi = 0
        pumping = bool(mod_queue)
        for dh in range(2):
            for g in range(ngr):
                f0 = g * FG
                nf = min(FG, FC - f0)
                ag = ags[gi % 2]
                s0 = gi % (2 if pumping else NSLOT)
                gi += 1
                ld(ag.ap[:, 0:nf, :], a_s[f0:f0 + nf].rearrange("f p t -> p f t"), a_b[f0:f0 + nf], [ag.b], ag.b.name)
                wv = R2[:, s0 * WPE:s0 * WPE + FG * HW].rearrange("p (f d) -> p f d", f=FG)
                ldc(wv[:, 0:nf, :], wd[f0 * 128:(f0 + nf) * 128, dh * HW:(dh + 1) * HW].rearrange("(f p) d -> p f d", p=128),
                    (), [WS_b[s0]], "ws%d" % s0)
                k = 0
                for dc in range(HD):
                    for h in range(NH):
                        pb = k % (6 if pumping else 8)
                        k += 1
                        for fl in range(nf):
                            mm(bank(pb), wv[:, fl, dc * 128:(dc + 1) * 128], ag.ap[:, fl, h * 512:(h + 1) * 512], fl == 0, fl == nf - 1,
                               [WS_b[s0], ag.b], [PS_b[pb]])
                        dst = ACC[:, dc, h * 512:(h + 1) * 512]
                        if g == 0:
                            cpa(dst, bank(pb), [PS_b[pb]], [ACC_b[dc]])
                        else:
                            tt(dst, dst, bank(pb), ALU.add, [PS_b[pb], ACC_b[dc]], [ACC_b[dc]])
                if pumping:
                    mod_pump(1, pair=2, pb=6 + gi % 2)
            for dc in range(HD):
                cc = dh * HD + dc
                xt = xts[cc % 2]
                ld(xt.ap, xT_s[cc], [xT_b[cc]], [xt.b], xt.b.name)
                stt(xt.ap, ACC[:, dc, :], gate_ap[:, cc:cc + 1], xt.ap, ALU.mult, ALU.add, [ACC_b[dc], xt.b, MOD_b], [xt.b])
                ld(xT_s[cc], xt.ap, [xt.b], [xT_b[cc]], xt.b.name + "s")

    def layer_tables(l):
        phase()
        o = 0
        for (src, R) in ((conv_w[l], 4 * NB), (conv_b[l], NB), (lru_ba[l], 2 * NB), (lru_bi[l], 2 * NB),
                         (lru_lam[l], 2 * NB), (slru_in[l], 2 * NB), (ret_g[l], RH)):
            rows_to_cols(LV[:, o:o + R], src, R, (), [LV_b])
            o += R
        t = arf.alloc([2 * NB], "sc8t")
        act(t.ap, LV[:, oLAM:oLAM + 2 * NB], AF.Exp, [LV_b], [t.b], scale=-1.0)
        act(t.ap, t.ap, AF.Ln, [t.b, CONST_b], [t.b], bias=ONEC)
        ts(SC8[:], t.ap, -8.0, ALU.mult, [t.b], [LV_b])
        ld(GAINQ[:], q_gain[l:l + 1, :].partition_broadcast(128), (), [LV_b], "gq")
        ld(GAINK[:], k_gain[l:l + 1, :].partition_broadcast(128), (), [LV_b], "gk")
        ld(LG[:], ret_logit[l:l + 1, :].partition_broadcast(128), (), [RET_b], "lg")
        act(LG[:], LG[:], AF.Exp, [RET_b], [RET_b], scale=-1.0)
        act(LG[:], LG[:], AF.Ln, [RET_b, CONST_b], [RET_b], bias=ONEC)
        ts(LG[:], LG[:], -1.0, ALU.mult, [RET_b], [RET_b])
        for h in range(RH):
            lgf, lgb = LG[:, h:h + 1], LG[:, RH + h:RH + h + 1]
            act(KDEC[:, h, 0:1], IVEC[:, 0:1], AF.Exp, [CST_b, RET_b], [RET_b], scale=lgf)
            act(KDEC[:, h, 1:2], IVEC[:, 1:2], AF.Exp, [CST_b, RET_b], [RET_b], scale=lgb)
            act(CDEC[:, h, 0:1], IVEC[:, 2:3], AF.Exp, [CST_b, RET_b], [RET_b], scale=lgf)
            act(CDEC[:, h, 1:2], IVEC[:, 2:3], AF.Exp, [CST_b, RET_b], [RET_b], scale=lgb)
        ts(KDEC[:], KDEC[:], 128.0 ** -0.5, ALU.mult, [RET_b], [RET_b])

    def qk_evac(ps_ap, psb, gain_ap, ropc, rops, tc, store_dram=None, tag="q"):
        sq = arf.alloc([256], tag + "sq")
        act(sq.ap, ps_ap, AF.Square, [psb], [sq.b])
        ss = arf.alloc([2], tag + "ss")
        sch.op("dve", lambda e: e.tensor_reduce(out=ss.ap, in_=sq.ap.rearrange("p (h d) -> p h d", h=2), axis=AX.X, op=ALU.add),
               [sq.b], [ss.b])
        act(ss.ap, ss.ap, AF.Sqrt, [ss.b, CONST_b], [ss.b], bias=EPSC, scale=1.0 / 128)
        recip(ss.ap, ss.ap, [ss.b], [ss.b])
        kn = arf.alloc([256], tag + "kn")
        for h in range(2):
            stt(kn.ap[:, h * 128:(h + 1) * 128], ps_ap[:, h * 128:(h + 1) * 128], ss.ap[:, h:h + 1], gain_ap, ALU.mult, ALU.mult,
                [psb, ss.b, LV_b], [kn.b])
        if store_dram is not None:
            ld(store_dram, kn.ap, [kn.b], (), tag + "kns")
        kr = arf.alloc([256], tag + "kr")
        t2 = arf.alloc([256], tag + "t2")
        for h in range(2):
            xv = kn.ap[:, h * 128:(h + 1) * 128]
            tt(kr.ap[:, h * 128:(h + 1) * 128], xv, ropc.ap[:, tc, :], ALU.mult, [kn.b, ropc.b], [kr.b])
            x4 = xv.rearrange("p (a b d) -> p a b d", a=2, b=2)
            s4 = rops.ap[:, tc, :].rearrange("p (a b d) -> p a b d", a=2, b=2)
            o4 = t2.ap[:, h * 128:(h + 1) * 128].rearrange("p (a b d) -> p a b d", a=2, b=2)
            tt(o4[:, :, 0, :], x4[:, :, 1, :], s4[:, :, 0, :], ALU.mult, [kn.b, rops.b], [t2.b])
            tt(o4[:, :, 1, :], x4[:, :, 0, :], s4[:, :, 1, :], ALU.mult, [kn.b, rops.b], [t2.b])
        tt(kr.ap, kr.ap, t2.ap, ALU.add, [kr.b, t2.b], [kr.b])
        return kr

    def mixer(l):
        norm_phase(GS[:, l, DC:2 * DC], MODT[:, l, 3 * DC:4 * DC])
        layer_tables(l)
        win = w_in[l]
        scale = 128.0 ** -0.5
        for kp in range(KVW // 256):
            phase(False)
            ropc = arf.alloc([NCH, 128], "ropc")
            rops = arf.alloc([NCH, 128], "rops")
            ld(ropc.ap, ropc_in.rearrange("(c p) f -> p c f", p=128), (), [ropc.b], "ropc")
            ld(rops.ap, rops_in.rearrange("(c p) f -> p c f", p=128), (), [rops.b], "rops")
            KT = arb.alloc([2, NKEY], "KT")
            V = arb.alloc([KCH, 256], "V")
            QT = arb.alloc([G, NT], "QT")
            atts = [arb.alloc([NT], "att%d" % k) for k in range(2)]
            pts = [arb.alloc([512], "pt%d" % k) for k in range(4)]
            ldc(V.ap[:, NCH:KCH, :], cv_in[l, :, kp * 256:(kp + 1) * 256].rearrange("(c p) f -> p c f", p=128), (), [V.b], "Vc")
            for pc in range(PCH):
                ckt = arf.alloc([256], "ckt%d" % pc)
                ld(ckt.ap, ck_in[l, pc * 128:(pc + 1) * 128, kp * 256:(kp + 1) * 256], (), [ckt.b], "ckt%d" % (pc % 2))
                for h in range(2):
                    tr(bank(2 + h, 0, 128), ckt.ap[:, h * 128:(h + 1) * 128], IDN, [ckt.b, CST_b], [PS_b[2 + h]])
                    cpa(KT.ap[:, h, NT + pc * 128:NT + (pc + 1) * 128], bank(2 + h, 0, 128), [PS_b[2 + h]], [KT.b])
            mark = arf.off
            checkpoint("att: after cacheK")
            kv_, kb_ = load_panel(win[:, c.oK + kp * 256:c.oK + (kp + 1) * 256])
            for tc in range(NCH):
                pb = tc % 2
                arf.off = mark
                gemm_tm(kv_, kb_, tc, pb)
                kr = qk_evac(bank(pb, 0, 256), PS_b[pb], GAINK[:], ropc, rops, tc,
                             store_dram=nk_out[l, tc * 128:(tc + 1) * 128, kp * 256:(kp + 1) * 256], tag="k")
                for h in range(2):
                    tr(bank(2 + h, 0, 128), kr.ap[:, h * 128:(h + 1) * 128], IDN, [kr.b, CST_b], [PS_b[2 + h]])
                    cpa(KT.ap[:, h, tc * 128:(tc + 1) * 128], bank(2 + h, 0, 128), [PS_b[2 + h]], [KT.b])
            checkpoint("att: after K panel")
            mod_pump(1)
            vv_, vb_ = load_panel(win[:, c.oV + kp * 256:c.oV + (kp + 1) * 256])
            for tc in range(NCH):
                pb = tc % 2
                arf.off = mark
                gemm_tm(vv_, vb_, tc, pb)
                vf = arf.alloc([256], "vf")
                cpa(vf.ap, bank(pb, 0, 256), [PS_b[pb]], [vf.b])
                ld(nv_out[l, tc * 128:(tc + 1) * 128, kp * 256:(kp + 1) * 256], vf.ap, [vf.b], (), "vfs")
                cpv(V.ap[:, tc, :], vf.ap, [vf.b], [V.b])
            checkpoint("att: after V panel")
            mod_pump(1)
            for gl in range(2):
                g = kp * 2 + gl
                for qp in range(G // 2):
                    qv_, qb_ = load_panel(win[:, c.oQ + (g * G + qp * 2) * 128:c.oQ + (g * G + qp * 2 + 2) * 128])
                    for tc in range(NCH):
                        pb = tc % 2
                        arf.off = mark
                        gemm_tm(qv_, qb_, tc, pb)
                        qr = qk_evac(bank(pb, 0, 256), PS_b[pb], GAINQ[:], ropc, rops, tc, tag="q")
                        for h in range(2):
                            tr(bank(2 + h, 0, 128), qr.ap[:, h * 128:(h + 1) * 128], IDN, [qr.b, CST_b], [PS_b[2 + h]])
                            cpa(QT.ap[:, qp * 2 + h, tc * 128:(tc + 1) * 128], bank(2 + h, 0, 128), [PS_b[2 + h]], [QT.b])
                    mod_pump(1)
                arf.off = mark
                checkpoint("att: after Q panels")
                rcs = [arf.alloc([512], "rc0"), arf.alloc([512], "rc1")]
                for hq0 in range(0, G, 2):
                    for hh in range(NH):
                        def score(ci, kc, hh=hh, hq0=hq0):
                            cb = 4 + 2 * ci + (kc % 2)
                            mm(bank(cb), KT.ap[:, gl, kc * 128:(kc + 1) * 128], QT.ap[:, hq0 + ci, hh * 512:(hh + 1) * 512], True, True,
                               [KT.b, QT.b], [PS_b[cb]])
                        for ci in range(2):
                            score(ci, 0)
                        for kc in range(KCH):
                            if kc + 1 < KCH:
                                for ci in range(2):
                                    score(ci, kc + 1)
                            for ci in range(2):
                                cb = 4 + 2 * ci + (kc % 2)
                                pt = pts[ci * 2 + kc % 2]
                                for q2 in range(2):
                                    act(pt.ap[:, q2 * 256:(q2 + 1) * 256], bank(cb, q2 * 256, (q2 + 1) * 256), AF.Exp, [PS_b[cb], TAB_b], [pt.b],
                                        scale=scale, bias=MB[:, kc, hh * 2 + q2:hh * 2 + q2 + 1])
                            for ci in range(2):
                                pt = pts[ci * 2 + kc % 2]
                                mm(bank(2 * ci), V.ap[:, kc, gl * 128:(gl + 1) * 128], pt.ap, kc == 0, kc == KCH - 1, [V.b, pt.b], [PS_b[2 * ci]])
                                mm(bank(2 * ci + 1), ONESB[:], pt.ap, kc == 0, kc == KCH - 1, [CONST_b, pt.b], [PS_b[2 * ci + 1]])
                        for ci in range(2):
                            recip(rcs[ci].ap, bank(2 * ci + 1), [PS_b[2 * ci + 1]], [rcs[ci].b])
                            tt(atts[ci].ap[:, hh * 512:(hh + 1) * 512], bank(2 * ci), rcs[ci].ap, ALU.mult, [PS_b[2 * ci], rcs[ci].b], [atts[ci].b])
                    for ci in range(2):
                        ld(mt_s[g * G + hq0 + ci], atts[ci].ap, [atts[ci].b], [mt_b[g * G + hq0 + ci]], "atts%d" % ci)
        for p in range(LW // 256):
            phase(False)
            WAI = arb.alloc([8, 128], "WAI")
            for r in range(2):
                ldc(WAI.ap[:, r * 2:(r + 1) * 2, :], lru_wa[l, r, p * 2:(p + 1) * 2].rearrange("n k j -> k n j"), (), [WAI.b], "wa%d" % r)
                ldc(WAI.ap[:, 4 + r * 2:4 + (r + 1) * 2, :], lru_wi[l, r, p * 2:(p + 1) * 2].rearrange("n k j -> k n j"), (), [WAI.b], "wi%d" % r)
            xv_, xb_ = load_panel(win[:, c.oXB + p * 256:c.oXB + (p + 1) * 256])
            yv_, yb_ = load_panel(win[:, c.oYB + p * 256:c.oYB + (p + 1) * 256])
            for nl in range(2):
                n = p * 2 + nl
                arf.off = 0
                arb.off = 8 * 128
                gemm_fm(xv_, xb_, nl * 128, [0, 1][:NH])
                HS = arf.alloc([NSEG, 256], "HS")
                XBP = arf.alloc([NSEG, 259], "XBP")
                for h in range(NH):
                    cpa(XBP.ap[:, 2 * h:2 * h + 2, 2:258], bank(h).rearrange("p (s t) -> p s t", s=2), [PS_b[h]], [XBP.b])
                mset(XBP.ap[:, 0, 0:2], 0.0, [XBP.b])
                mset(XBP.ap[:, NSEG - 1, 258:259], 0.0, [XBP.b])
                ts(XBP.ap[:, 1:NSEG, 0:2], XBP.ap[:, 0:NSEG - 1, 256:258], FLAG, ALU.mult, [XBP.b, CONST_b], [XBP.b])
                ts(XBP.ap[:, 0:NSEG - 1, 258:259], XBP.ap[:, 1:NSEG, 2:3], FLAG, ALU.mult, [XBP.b, CONST_b], [XBP.b])
                XC = arf.alloc([NSEG, 256], "XC")
                cw = lambda j: LV[:, oCW + j * NB + n:oCW + j * NB + n + 1]
                act(XC.ap, XBP.ap[:, :, 2:258], AF.Identity, [XBP.b, LV_b], [XC.b], bias=LV[:, oCB + n:oCB + n + 1], scale=cw(2))
                for (j, o0) in ((0, 0), (1, 1), (3, 3)):
                    stt(XC.ap, XBP.ap[:, :, o0:o0 + 256], cw(j), XC.ap, ALU.mult, ALU.add, [XBP.b, XC.b, LV_b], [XC.b])
                XCB = arb.alloc([NT], "XCB")
                xcf = XC.ap.rearrange("p s t -> p (s t)")
                cpa(XCB.ap, xcf, [XC.b], [XCB.b])
                A = arf.alloc([NSEG, 256], "A")
                U = arf.alloc([NSEG, 256], "U")
                Hb = arf.alloc([NSEG, 256], "Hb")
                MU = arf.alloc([NT], "MU")
                cr_ = arf.alloc([1], "carry")
                Af = A.ap.rearrange("p s t -> p (s t)")
                Uf = U.ap.rearrange("p s t -> p (s t)")
                for r in range(2):
                    ia, ii = r * 2 + nl, 4 + r * 2 + nl
                    for h in range(NH):
                        mm(bank(2 + h), WAI.ap[:, ia, :], XCB.ap[:, h * 512:(h + 1) * 512], True, True, [WAI.b, XCB.b], [PS_b[2 + h]])
                        mm(bank(4 + h), WAI.ap[:, ii, :], XCB.ap[:, h * 512:(h + 1) * 512], True, True, [WAI.b, XCB.b], [PS_b[4 + h]])
                    Hd = HS if r == 0 else Hb
                    for h in range(NH):
                        sl = slice(h * 512, (h + 1) * 512)
                        act(Af[:, sl], bank(2 + h), AF.Sigmoid, [PS_b[2 + h], LV_b], [A.b], bias=LV[:, oBA + r * NB + n:oBA + r * NB + n + 1])
                        act(Uf[:, sl], bank(4 + h), AF.Sigmoid, [PS_b[4 + h], LV_b], [U.b], bias=LV[:, oBI + r * NB + n:oBI + r * NB + n + 1])
                    act(Af, Af, AF.Exp, [A.b, LV_b], [A.b], scale=SC8[:, r * NB + n:r * NB + n + 1])
                    tt(Uf, Uf, xcf, ALU.mult, [U.b, XC.b], [U.b])
                    tt(MU.ap, Af, Af, ALU.mult, [A.b], [MU.b])
                    act(MU.ap, MU.ap, AF.Sqrt, [MU.b, CONST_b], [MU.b], bias=ONEC, scale=-1.0)
                    tt(Uf, Uf, MU.ap, ALU.mult, [U.b, MU.b], [U.b])
                    order = range(NSEG) if r == 0 else range(NSEG - 1, -1, -1)
                    for si, s in enumerate(order):
                        if si == 0:
                            init = LV[:, oSL + r * NB + n:oSL + r * NB + n + 1]
                            ird = [LV_b]
                        else:
                            init = cr_.ap
                            ird = [cr_.b]

                        def view(tb, s=s, r=r):
                            a = tb.ap[:, s, :]
                            if r == 0:
                                return a
                            return bass.AP(a.tensor, a.offset + 255, [[a.ap[0][0], 128], [-1, 256]])
                        o_, d0, d1 = view(Hd), view(A), view(U)
                        sch.op("dve", lambda e, o_=o_, d0=d0, d1=d1, init=init: e.tensor_tensor_scan(
                            out=o_, data0=d0, data1=d1, initial=init, op0=ALU.mult, op1=ALU.add), [A.b, U.b] + ird, [Hd.b])
                        last = Hd.ap[:, s, 255:256] if r == 0 else Hd.ap[:, s, 0:1]
                        col = (s * 2 + r) * NB + n
                        cpa(LST[:, col:col + 1], last, [Hd.b], [LST_b])
                        if si < NSEG - 1:
                            ts(cr_.ap, last, FLAG, ALU.mult, [Hd.b, CONST_b], [cr_.b])
                tt(HS.ap, HS.ap, Hb.ap, ALU.add, [HS.b, Hb.b], [HS.b])
                gemm_fm(yv_, yb_, nl * 128, [0, 1][:NH])
                lr = arb.alloc([NT], "lr")
                hsf = HS.ap.rearrange("p s t -> p (s t)")
                t = arf.alloc([512], "gl_t")
                for h in range(NH):
                    sl = slice(h * 512, (h + 1) * 512)
                    act(t.ap, bank(h), AF.Square, [PS_b[h]], [t.b])
                    ts(t.ap, t.ap, 0.044715, ALU.mult, [t.b], [t.b], s2=1.0, op1=ALU.add)
                    tt(t.ap, t.ap, bank(h), ALU.mult, [t.b, PS_b[h]], [t.b])
                    act(t.ap, t.ap, AF.Sigmoid, [t.b], [t.b], scale=1.5957691216057308)
                    tt(t.ap, t.ap, bank(h), ALU.mult, [t.b, PS_b[h]], [t.b])
                    tt(lr.ap[:, sl], t.ap, hsf[:, sl], ALU.mult, [t.b, HS.b], [lr.b])
                ld(mt_s[c.AW // 128 + n], lr.ap, [lr.b], [mt_b[c.AW // 128 + n]], "lrs")
        phase(False)
        NR = NSEG * 2 * NB
        tr(PS[0:NR, 7, 0:128], LST[:], IDN, [LST_b, CST_b], [PS_b[7]])
        lso = arf.alloc([128], "lso")
        cpv(lso.ap[0:NR, :], PS[0:NR, 7, 0:128], [PS_b[7]], [lso.b])
        ld(nlru_out[l], lso.ap[0:NR, :], [lso.b], (), "lsos")
        kscale = 128.0 ** -0.5
        for p in range(RW // 256):
            phase(False)
            qv_, qb_ = load_panel(win[:, c.oRQ + p * 256:c.oRQ + (p + 1) * 256])
            kv_, kb_ = load_panel(win[:, c.oRK + p * 256:c.oRK + (p + 1) * 256])
            vv_, vb_ = load_panel(win[:, c.oRV + p * 256:c.oRV + (p + 1) * 256])
            gv_, gb_ = load_panel(win[:, c.oRG + p * 256:c.oRG + (p + 1) * 256])
            for hl in range(2):
                h = p * 2 + hl
                arf.off = 0
                arb.off = 0
                hs = slice(hl * 128, (hl + 1) * 128)
                QR = arb.alloc([NT], "QR")
                QDF = arb.alloc([NT], "QDF")
                QDB = arb.alloc([NT], "QDB")
                KR = arb.alloc([NT], "KR")
                KDF = arb.alloc([NCH, 128], "KDF")
                KDB = arb.alloc([NCH, 128], "KDB")
                VR = arb.alloc([NCH, 128], "VR")
                SGR = arf.alloc([NT], "SGR")
                mcomb = arf.alloc([128], "mcomb")
                qdec = arf.alloc([2, 128], "qdec")
                t2 = arf.alloc([128], "mc2")
                lgf, lgb = LG[:, h:h + 1], LG[:, RH + h:RH + h + 1]
                act(mcomb.ap, DPOS, AF.Exp, [CST_b, RET_b], [mcomb.b], scale=lgf)
                tt(mcomb.ap, mcomb.ap, LOWM, ALU.mult, [mcomb.b, CST_b], [mcomb.b])
                act(t2.ap, DNEG, AF.Exp, [CST_b, RET_b], [t2.b], scale=lgb)
                tt(t2.ap, t2.ap, UPM, ALU.mult, [t2.b, CST_b], [t2.b])
                tt(mcomb.ap, mcomb.ap, t2.ap, ALU.add, [mcomb.b, t2.b], [mcomb.b])
                ts(mcomb.ap, mcomb.ap, kscale, ALU.mult, [mcomb.b], [mcomb.b])
                act(qdec.ap[:, 0, :], IROW1, AF.Exp, [CST_b, RET_b], [qdec.b], scale=lgf)
                act(qdec.ap[:, 1, :], IROW2, AF.Exp, [CST_b, RET_b], [qdec.b], scale=lgb)
                gemm_fm(qv_, qb_, hl * 128, [0, 1][:NH])
                for hh in range(NH):
                    sl = slice(hh * 512, (hh + 1) * 512)
                    cpa(QR.ap[:, sl], bank(hh), [PS_b[hh]], [QR.b])
                    b3 = bank(hh).rearrange("p (c i) -> p c i", c=4)
                    tt(QDF.ap[:, sl].rearrange("p (c i) -> p c i", c=4), b3, qdec.ap[:, 0:1, :].broadcast_to([128, 4, 128]), ALU.mult,
                       [PS_b[hh], qdec.b, QR.b], [QDF.b])
                    tt(QDB.ap[:, sl].rearrange("p (c i) -> p c i", c=4), b3, qdec.ap[:, 1:2, :].broadcast_to([128, 4, 128]), ALU.mult,
                       [PS_b[hh], qdec.b, QR.b], [QDB.b])
                gemm_fm(kv_, kb_, hl * 128, [2, 3][:NH])
                for hh in range(NH):
                    cpa(KR.ap[:, hh * 512:(hh + 1) * 512], bank(2 + hh), [PS_b[2 + hh]], [KR.b])
                gemm_fm(gv_, gb_, hl * 128, [0, 1][:NH])
                for hh in range(NH):
                    act(SGR.ap[:, hh * 512:(hh + 1) * 512], bank(hh), AF.Silu, [PS_b[hh]], [SGR.b])
                for t4 in range(NCH // 4):
                    for (wv_, wb_, pb) in ((kv_, kb_, 2 + t4 % 2), (vv_, vb_, 4 + t4 % 2)):
                        for k4 in range(4):
                            tc = t4 * 4 + k4
                            for dc in range(DC):
                                mm(bank(pb, k4 * 128, (k4 + 1) * 128), HT[:, dc, tc * 128:(tc + 1) * 128], wv_[:, dc, hs], dc == 0, dc == DC - 1,
                                   [wb_, HT_b], [PS_b[pb]])
                    kb3 = bank(2 + t4 % 2).rearrange("p (c d) -> p c d", c=4)
                    act(KDF.ap[:, t4 * 4:(t4 + 1) * 4, :], kb3, AF.Identity, [PS_b[2 + t4 % 2], RET_b], [KDF.b], scale=KDEC[:, h, 0:1])
                    ts(KDB.ap[:, t4 * 4:(t4 + 1) * 4, :], kb3, KDEC[:, h, 1:2], ALU.mult, [PS_b[2 + t4 % 2], RET_b, KDF.b], [KDB.b])
                    cpa(VR.ap[:, t4 * 4:(t4 + 1) * 4, :], bank(4 + t4 % 2).rearrange("p (c d) -> p c d", c=4), [PS_b[4 + t4 % 2]], [VR.b])
                SB_ = [arb.alloc([NCH, 128], "SFB"), arb.alloc([NCH, 128], "SBB")]
                for r in range(2):
                    S = arf.alloc([128], "S%d" % r)
                    ld(S.ap, sret_in[l, r, h], (), [S.b], "S%d" % r)
                    KD = KDF if r == 0 else KDB
                    order = list(range(NCH)) if r == 0 else list(range(NCH - 1, -1, -1))
                    sos = [arf.alloc([128], "so%d_%d" % (r, k)) for k in range(2)]
                    for ci, cc in enumerate(order):
                        cpa(SB_[r].ap[:, cc, :], S.ap, [S.b], [SB_[r].b])
                        pb = 2 + (ci % 2)
                        mm(bank(pb, 0, 128), KD.ap[:, cc, :], VR.ap[:, cc, :], True, True, [KD.b, VR.b], [PS_b[pb]])
                        stt(S.ap, S.ap, CDEC[:, h, r:r + 1], bank(pb, 0, 128), ALU.mult, ALU.add, [S.b, PS_b[pb], RET_b], [S.b])
                        boundary = (cc % 2 == 1) if r == 0 else (cc % 2 == 0)
                        if boundary:
                            seg = cc // 2
                            so = sos[seg % 2]
                            cpv(so.ap, S.ap, [S.b], [so.b])
                            ld(nret_out[l, seg, r, h], so.ap, [so.b], (), "sos%d_%d" % (r, seg % 2))
                            if ci < NCH - 1:
                                ts(S.ap, S.ap, FLAG, ALU.mult, [S.b, CONST_b], [S.b])
                PT = [arb.alloc([128], "rpt%d" % k) for k in range(2)]
                for cc in range(NCH):
                    ts_ = slice(cc * 128, (cc + 1) * 128)
                    pb = 2 + (cc % 2)
                    mm(bank(pb, 0, 128), KR.ap[:, ts_], QR.ap[:, ts_], True, True, [KR.b, QR.b], [PS_b[pb]])
                    pt = PT[cc % 2]
                    tt(pt.ap, bank(pb, 0, 128), mcomb.ap, ALU.mult, [PS_b[pb], mcomb.b], [pt.b])
                    ob = 4 + cc // 4
                    osl = bank(ob, (cc % 4) * 128, (cc % 4 + 1) * 128)
                    mm(osl, VR.ap[:, cc, :], pt.ap, True, False, [VR.b, pt.b], [PS_b[ob]])
                    mm(osl, SB_[0].ap[:, cc, :], QDF.ap[:, ts_], False, False, [SB_[0].b, QDF.b], [PS_b[ob]])
                    mm(osl, SB_[1].ap[:, cc, :], QDB.ap[:, ts_], False, True, [SB_[1].b, QDB.b], [PS_b[ob]])
                rt = arb.alloc([NT], "rt")
                for hh in range(NH):
                    sl = slice(hh * 512, (hh + 1) * 512)
                    sq = arf.alloc([512], "rsq%d" % hh)
                    act(sq.ap, bank(4 + hh), AF.Square, [PS_b[4 + hh]], [sq.b])
                    mm(bank(6 + hh % 2), ONES[:], sq.ap, True, True, [sq.b, CONST_b], [PS_b[6 + hh % 2]])
                    rs = arf.alloc([512], "rrs%d" % hh)
                    act(rs.ap, bank(6 + hh % 2), AF.Sqrt, [PS_b[6 + hh % 2], CONST_b], [rs.b], bias=EPSC, scale=1.0 / 128)
                    recip(rs.ap, rs.ap, [rs.b], [rs.b])
                    tt(rs.ap, rs.ap, bank(4 + hh), ALU.mult, [rs.b, PS_b[4 + hh]], [rs.b])
                    stt(rt.ap[:, sl], rs.ap, LV[:, oRG_ + h:oRG_ + h + 1], SGR.ap[:, sl], ALU.mult, ALU.mult, [rs.b, LV_b, SGR.b], [rt.b])
                ci_ = (c.AW + c.LW) // 128 + h
                ld(mt_s[ci_], rt.ap, [rt.b], [mt_b[ci_]], "rts")
        phase()
        for q4 in range(4):
            c0, c1 = q4 * DC // 4, (q4 + 1) * DC // 4
            ld(HT[:, c0:c1, :], mt_s[c0:c1].rearrange("c p t -> p c t"), mt_b[c0:c1], [HT_b], "mtq%d" % q4)
        xts = [arf.alloc([NT], "xt%d" % k) for k in range(2)]
        gate_ap = GATE[:, l, DC:2 * DC]
        for p in range(D // 256):
            wv_, wb_ = load_panel(w_out[l][:, p * 256:(p + 1) * 256])
            for cl in range(2):
                cc = p * 2 + cl
                bs = [(cc % 2) * 2 + h for h in range(NH)]
                gemm_fm(wv_, wb_, cl * 128, bs)
                xt = xts[cc % 2]
                ld(xt.ap, xT_s[cc], [xT_b[cc]], [xt.b], xt.b.name)
                for h in range(NH):
                    sl = slice(h * 512, (h + 1) * 512)
                    stt(xt.ap[:, sl], bank(bs[h]), gate_ap[:, cc:cc + 1], xt.ap[:, sl], ALU.mult, ALU.add, [PS_b[bs[h]], xt.b, MOD_b], [xt.b])
                ld(xT_s[cc], xt.ap, [xt.b], [xT_b[cc]], xt.b.name + "s")
            mod_pump(1)

    def main_seq():
        for l in range(DEPTH):
            if l > 0 and not c.CC:
                mod_need(l, 2)
                mod_finalize(l, [0, 1, 2], True)
            ffn(l, 0)
            if l == 0 and not c.CC:
                mod_need(0, 1)
                mod_finalize(0, [1], False)
            mixer(l)
            if l == 0 and not c.CC:
                mod_need(0, 2)
                mod_finalize(0, [2], False)
            ffn(l, 2)
        final_norm()

    def final_norm():
      phase()
      for tc in range(NCH):
        arf.off = 0
        xh = arf.alloc([DC, 128], "fxh")
        ld(xh.ap, xT_s[:, :, tc * 128:(tc + 1) * 128].rearrange("c p t -> p c t"), xT_b, [xh.b], "fxh")
        rs = sumsq_rstd(lambda cc, xh=xh: (xh.ap[:, cc, :], xh.b), 128, D)
        ftmps = [arf.alloc([128], "ftmp0"), arf.alloc([128], "ftmp1")]
        HC = max(DC // 2, 4)
        for half in range(DC // HC):
            yo = arf.alloc([HC * 128], "yo%d" % half)
            for c4 in range(HC // 4):
                pb = c4 % 2
                for k in range(4):
                    cc = half * HC + c4 * 4 + k
                    tmp = ftmps[cc % 2]
                    stt(tmp.ap, xh.ap[:, cc, :], FGT[:, cc:cc + 1], rs.ap, ALU.mult, ALU.mult, [xh.b, rs.b, MOD_b], [tmp.b])
                    tr(bank(pb, k * 128, (k + 1) * 128), tmp.ap, IDN, [tmp.b, CST_b], [PS_b[pb]])
                cpa(yo.ap[:, c4 * 512:(c4 + 1) * 512], bank(pb), [PS_b[pb]], [yo.b])
            ld(y_out[tc * 128:(tc + 1) * 128, half * HC * 128:(half + 1) * HC * 128], yo.ap, [yo.b], (), "yos%d" % half)
            arf.off -= (HC * 128 + 15) // 16 * 16
    main_seq()
    sch.final_wait()
    print("build: ninst", sch.ninst, "nsem", len(sch.semmap), "ckpts", ckpt[0], {e: sch.cnt[e] for e in sch.cnt})

    with nc.Block() as block:
        @block.tensor
        def _(e):
            for f in sch.prog["pe"]:
                f(e)

        @block.scalar
        def _(e):
            for f in sch.prog["act"]:
                f(e)

        @block.vector
        def _(e):
            for f in sch.prog["dve"]:
                f(e)

        @block.gpsimd
        def _(e):
            for f in sch.prog["pool"]:
                f(e)

        @block.sync
        def _(e):
            for f in sch.prog["sp"]:
                f(e)
    es.close()
    return nc


def _consts(cfg):
    j = np.arange(128, dtype=np.float32)[:, None]
    i = np.arange(128, dtype=np.float32)[None, :]
    cst = np.zeros((128, 7 * 128 + 4), np.float32)
    cst[:, 0:128] = np.eye(128, dtype=np.float32)
    cst[:, 128:256] = np.maximum(i - j, 0)
    cst[:, 256:384] = np.maximum(j - i, 0)
    cst[:, 384:512] = (i >= j)
    cst[:, 512:640] = (j >= i)
    cst[:, 640:768] = i + 1
    cst[:, 768:896] = 128 - i
    cst[:, 896] = 127 - j[:, 0]
    cst[:, 897] = j[:, 0]
    cst[:, 898] = 128
    return cst


def _rope_tables(cfg):
    NT, GW = cfg.NT, cfg.GRID_W
    t = np.arange(NT)
    row = (t // GW).astype(np.float32)
    col = (t % GW).astype(np.float32)
    inv = (10000.0 ** (-np.arange(32, dtype=np.float32) / 32)).astype(np.float32)
    ar = row[:, None] * inv[None, :]
    ac = col[:, None] * inv[None, :]
    C = np.concatenate([np.cos(ar), np.cos(ar), np.cos(ac), np.cos(ac)], axis=1).astype(np.float32)
    S = np.concatenate([-np.sin(ar), np.sin(ar), -np.sin(ac), np.sin(ac)], axis=1).astype(np.float32)
    return C, S


_NC_CACHE = {}


def kernel(cfg=None, **inp):
    if cfg is None:
        cfg = Cfg()
    c = cfg
    key = (c.D, c.F, c.H, c.KVH, c.NB, c.RH, c.NT, c.PAST, c.DEPTH, c.NPC, c.NSC, c.FG, c.CC, c.stop)
    if key not in _NC_CACHE:
        _NC_CACHE[key] = build_nc(c)
    nc = _NC_CACHE[key]
    f32 = lambda a: np.ascontiguousarray(np.asarray(a, dtype=np.float32))
    NT, D, DEPTH = c.NT, c.D, c.DEPTH
    SPC = NT // 256
    shared = {
        "cst": _consts(c),
        "norm_g": f32(inp["norm_g"]).reshape(DEPTH, 3 * c.DC, 128),
        "b_mod": f32(inp["b_mod"]).reshape(DEPTH, 9 * c.DC, 128),
        "ffn_wg": f32(inp["ffn_wg"]), "ffn_wu": f32(inp["ffn_wu"]), "ffn_wd": f32(inp["ffn_wd"]),
        "w_in": f32(inp["w_in"]),
        "q_gain": f32(inp["q_gain"]), "k_gain": f32(inp["k_gain"]),
        "conv_w": f32(inp["lru_conv_w"]).reshape(DEPTH, 4 * c.NB, 128),
        "conv_b": f32(inp["lru_conv_b"]).reshape(DEPTH, c.NB, 128),
        "lru_wa": f32(inp["lru_wa"]), "lru_wi": f32(inp["lru_wi"]),
        "lru_ba": f32(inp["lru_ba"]).reshape(DEPTH, 2 * c.NB, 128),
        "lru_bi": f32(inp["lru_bi"]).reshape(DEPTH, 2 * c.NB, 128),
        "lru_lam": f32(inp["lru_lambda"]).reshape(DEPTH, 2 * c.NB, 128),
        "ret_logit": f32(inp["ret_logit"]).reshape(DEPTH, 2 * c.RH),
        "ret_g": f32(inp["ret_g"]).reshape(DEPTH, c.RH, 128),
        "w_out": f32(inp["w_out"]),
        "final_g": f32(inp["final_g"]).reshape(c.DC, 128),
    }
    xp, xs = f32(inp["x_prompt"]), f32(inp["x_sample"])
    ck, cv = f32(inp["cache_k"]), f32(inp["cache_v"])
    slru, sret = f32(inp["state_lru"]), f32(inp["state_ret"])
    cc_, cctx = f32(inp["c"]), f32(inp["c_ctx"])
    ropC, ropS = _rope_tables(c)
    NKEY = NT + c.PAST
    KCH_ = (NT + c.PAST) // 128
    mb_p = np.full((KCH_, c.NSEG), NEG, np.float32)
    for kc in range(NT // 128):
        mb_p[kc, kc // 2] = 0.0
    mb_p = np.ascontiguousarray(np.broadcast_to(mb_p.reshape(1, -1), (128, KCH_ * c.NSEG)))
    mb_s = np.zeros((128, KCH_ * c.NSEG), np.float32)
    zeros_ck = np.zeros((DEPTH, c.PAST, c.KVW), np.float32)
    wm = f32(inp["w_mod"])
    condall = np.ascontiguousarray(np.concatenate([cctx.reshape(1, D), cc_.reshape(-1, D)], axis=0))
    in_maps = []
    for core in range(c.NPC + c.NSC):
        m = dict(shared)
        if c.CC:
            m["w_mod"] = np.ascontiguousarray(wm[:, :, core * c.CS:(core + 1) * c.CS])
            m["condall"] = condall
            rstar = 0 if core < c.NPC else 1 + (core - c.NPC)
            sj = np.zeros((c.NCORES * c.NCOND, c.NCORES), np.float32)
            for j in range(c.NCORES):
                sj[j * c.NCOND + rstar, j] = 1.0
            m["selj"] = sj
        else:
            m["w_mod"] = wm
        if core < c.NPC:
            m["x"] = np.ascontiguousarray(xp[core * SPC:(core + 1) * SPC].reshape(NT, D))
            if not c.CC:
                m["cond"] = cctx.reshape(1, D)
            m["ck"] = zeros_ck
            m["cv"] = zeros_ck
            m["slru"] = np.zeros((DEPTH, 2 * c.NB, 128), np.float32)
            m["sret"] = np.zeros((DEPTH, 2, c.RH, 128, 128), np.float32)
            m["flag"] = np.zeros((128, 2), np.float32)
            m["mb"] = mb_p
            m["ropc"] = np.ones((NT, 128), np.float32)
            m["rops"] = np.zeros((NT, 128), np.float32)
        else:
            b = core - c.NPC
            m["x"] = np.ascontiguousarray(xs[b].reshape(NT, D))
            if not c.CC:
                m["cond"] = np.ascontiguousarray(cc_[b].reshape(1, D))
            m["ck"] = np.ascontiguousarray(ck[b].reshape(DEPTH, c.PAST, c.KVW))
            m["cv"] = np.ascontiguousarray(cv[b].reshape(DEPTH, c.PAST, c.KVW))
            m["slru"] = np.ascontiguousarray(slru[b].reshape(DEPTH, 2 * c.NB, 128))
            m["sret"] = np.ascontiguousarray(sret[b])
            m["flag"] = np.ones((128, 2), np.float32)
            m["mb"] = mb_s
            m["ropc"] = ropC
            m["rops"] = ropS
        in_maps.append(m)
    res = run_bass_kernel_spmd(nc, in_maps, core_ids=list(range(c.NPC + c.NSC)))
    R = res.results
    B = c.NPC * SPC
    y_p = np.zeros((B, 256, D), np.float32)
    y_s = np.zeros((c.NSC, NT, D), np.float32)
    nk = np.zeros((B, DEPTH, 256, c.KVH, 128), np.float32)
    nv = np.zeros((B, DEPTH, 256, c.KVH, 128), np.float32)
    nl = np.zeros((B, DEPTH, 2, c.LW), np.float32)
    nr = np.zeros((B, DEPTH, 2, c.RH, 128, 128), np.float32)
    for core in range(c.NPC):
        r = R[core]
        for s in range(SPC):
            b = core * SPC + s
            y_p[b] = r["y"][s * 256:(s + 1) * 256]
            for l in range(DEPTH):
                nk[b, l] = r["nk"][l, s * 256:(s + 1) * 256].reshape(256, c.KVH, 128)
                nv[b, l] = r["nv"][l, s * 256:(s + 1) * 256].reshape(256, c.KVH, 128)
                nl[b, l] = r["nlru"][l].reshape(c.NSEG, 2, c.LW)[s]
                nr[b, l] = r["nret"][l, s]
    for b in range(c.NSC):
        y_s[b] = R[c.NPC + b]["y"]
    return (y_p, y_s, nk, nv, nl, nr)
```
